# Optimizing a Trainium2 kernel written in Bass

```python
import jax, jax.numpy as jnp
from jax import lax
import numpy as np

D_MODEL = 2048
BATCH = 8
SEQ = 2048
DEPTH = 1

CHUNK = 64
N_META = 16
HEAD_DIM = 128
D_ATTN = D_MODEL // 2
ATTN_HEADS = D_ATTN // HEAD_DIM
POOL_WINDOWS = (2, 4, 8, 16)
N_POOL_GROUPS = len(POOL_WINDOWS)
D_POOL = D_MODEL // 2
POOL_GROUP_DIM = D_POOL // N_POOL_GROUPS
D_FF = 11 * D_MODEL // 4
CONV_WIDTH = 3
Q_BLOCK = 128
EPS = 1e-6

SPLIT_SIZES = (D_ATTN, D_ATTN, D_ATTN, ATTN_HEADS, D_POOL, D_MODEL, D_MODEL)
SPLIT_IDX = tuple(int(i) for i in np.cumsum(SPLIT_SIZES)[:-1])
D_IN = sum(SPLIT_SIZES)
F_OFF = 3 * D_ATTN

kernel_name = 'hybrid_fox_pool_convffn'


def rmsnorm(x, g):
    xf = x.astype(jnp.float32)
    y = xf * lax.rsqrt(jnp.mean(xf * xf, axis=-1, keepdims=True) + EPS)
    return (y * g.astype(jnp.float32)).astype(x.dtype)


def forgetting_attention(q, k, v, logf):
    B, L, H, Dh = q.shape
    n_blocks = -(-L // Q_BLOCK)
    Lp = n_blocks * Q_BLOCK
    pad = ((0, 0), (0, Lp - L), (0, 0), (0, 0))
    q = jnp.pad(q, pad)
    k = jnp.pad(k, pad)
    v = jnp.pad(v, pad)
    logf = jnp.pad(logf, ((0, 0), (0, Lp - L), (0, 0)))
    c = jnp.cumsum(logf, axis=1)
    c = jnp.transpose(c, (0, 2, 1))
    scale = HEAD_DIM ** -0.5
    pos = jnp.arange(Lp)
    outs = []
    for i in range(n_blocks):
        q0, q1 = i * Q_BLOCK, (i + 1) * Q_BLOCK
        qb = q[:, q0:q1]
        kb = k[:, :q1]
        vb = v[:, :q1]
        s = jnp.einsum('bqhd,bkhd->bhqk', qb, kb).astype(jnp.float32) * scale
        bias = c[:, :, q0:q1, None] - c[:, :, None, :q1]
        mask = pos[q0:q1, None] >= pos[None, :q1]
        s = jnp.where(mask[None, None], s + bias, -jnp.inf)
        p = jax.nn.softmax(s, axis=-1).astype(vb.dtype)
        outs.append(jnp.einsum('bhqk,bkhd->bqhd', p, vb))
    o = jnp.concatenate(outs, axis=1)[:, :L]
    return o.reshape(B, L, H * Dh)


def multiscale_pool(u, pool_w, pool_scale):
    B, L, _ = u.shape
    uf = u.astype(jnp.float32).reshape(B, L, N_POOL_GROUPS, POOL_GROUP_DIM)
    csum = jnp.cumsum(uf, axis=1)
    pos = jnp.arange(L)
    outs = []
    for g, w in enumerate(POOL_WINDOWS):
        cg = csum[:, :, g]
        prev = jnp.pad(cg[:, :L - w], ((0, 0), (w, 0), (0, 0)))
        cnt = jnp.minimum(pos + 1, w).astype(jnp.float32)[None, :, None]
        outs.append((cg - prev) / cnt - uf[:, :, g])
    d = jnp.stack(outs, axis=2).astype(u.dtype)
    y = jnp.einsum('blgc,gcd->blgd', d, pool_w).reshape(B, L, D_POOL)
    return y * pool_scale


def mixer_block(h, w_in, b_in, w_attn_o, pool_w, pool_scale, w_pool_o, w_out):
    B, L, _ = h.shape
    z = h @ w_in + b_in
    q, k, v, f_logit, u, g_attn, g_pool = jnp.split(z, SPLIT_IDX, axis=-1)
    q = q.reshape(B, L, ATTN_HEADS, HEAD_DIM)
    k = k.reshape(B, L, ATTN_HEADS, HEAD_DIM)
    v = v.reshape(B, L, ATTN_HEADS, HEAD_DIM)
    logf = jax.nn.log_sigmoid(f_logit.astype(jnp.float32))
    attn = forgetting_attention(q, k, v, logf)
    pool = multiscale_pool(u, pool_w, pool_scale)
    m = jax.nn.sigmoid(g_attn) * (attn @ w_attn_o) + jax.nn.sigmoid(g_pool) * (pool @ w_pool_o)
    return m @ w_out


def causal_dwconv(a, conv_w, conv_b):
    L = a.shape[1]
    ap = jnp.pad(a, ((0, 0), (CONV_WIDTH - 1, 0), (0, 0)))
    out = conv_b
    for j in range(CONV_WIDTH):
        out = out + ap[:, j:j + L] * conv_w[j]
    return out


def conv_ffn(h, w_up, conv_w, conv_b, w_down):
    a = causal_dwconv(h @ w_up, conv_w, conv_b)
    gate, val = jnp.split(a, 2, axis=-1)
    return (jax.nn.gelu(gate, approximate=True) * val) @ w_down


def setup_inputs(seed: int = 0) -> dict:
    key = jax.random.key(seed)
    ks = jax.random.split(key, 18)
    f32 = jnp.float32
    nrm = lambda k, shape, s: jax.random.normal(k, shape, f32) * s
    gain = lambda k, n: jnp.ones((DEPTH, n), f32) + nrm(k, (DEPTH, n), 0.02)
    b_in = nrm(ks[4], (DEPTH, D_IN), 0.02)
    f_bias = jax.random.uniform(ks[5], (DEPTH, ATTN_HEADS), f32, 1.0, 5.0)
    b_in = b_in.at[:, F_OFF:F_OFF + ATTN_HEADS].add(f_bias)
    return {
        'x': nrm(ks[0], (BATCH, SEQ, D_MODEL), 1.0),
        'meta_tokens': nrm(ks[1], (N_META, D_MODEL), 1.0),
        'mix_pre_g': gain(ks[2], D_MODEL),
        'w_in': nrm(ks[3], (DEPTH, D_MODEL, D_IN), D_MODEL ** -0.5),
        'b_in': b_in,
        'w_attn_o': nrm(ks[6], (DEPTH, D_ATTN, D_MODEL), D_ATTN ** -0.5),
        'pool_w': nrm(ks[7], (DEPTH, N_POOL_GROUPS, POOL_GROUP_DIM, POOL_GROUP_DIM), POOL_GROUP_DIM ** -0.5),
        'pool_scale': jnp.ones((DEPTH, D_POOL), f32) + nrm(ks[8], (DEPTH, D_POOL), 0.1),
        'w_pool_o': nrm(ks[9], (DEPTH, D_POOL, D_MODEL), D_POOL ** -0.5),
        'w_out': nrm(ks[10], (DEPTH, D_MODEL, D_MODEL), D_MODEL ** -0.5),
        'mix_post_g': gain(ks[11], D_MODEL),
        'ffn_pre_g': gain(ks[12], D_MODEL),
        'w_ffn_up': nrm(ks[13], (DEPTH, D_MODEL, 2 * D_FF), D_MODEL ** -0.5),
        'ffn_conv_w': nrm(ks[14], (DEPTH, CONV_WIDTH, 2 * D_FF), CONV_WIDTH ** -0.5),
        'ffn_conv_b': nrm(ks[15], (DEPTH, 2 * D_FF), 0.02),
        'w_ffn_down': nrm(ks[16], (DEPTH, D_FF, D_MODEL), D_FF ** -0.5),
        'ffn_post_g': gain(ks[17], D_MODEL),
    }


def reference(x, meta_tokens, mix_pre_g, w_in, b_in, w_attn_o, pool_w, pool_scale, w_pool_o,
              w_out, mix_post_g, ffn_pre_g, w_ffn_up, ffn_conv_w, ffn_conv_b, w_ffn_down,
              ffn_post_g):
    B = x.shape[0]
    meta = jnp.broadcast_to(meta_tokens.astype(x.dtype)[None], (B, N_META, D_MODEL))
    r = jnp.concatenate([meta, x], axis=1)
    for l in range(DEPTH):
        h = rmsnorm(r, mix_pre_g[l])
        mix = mixer_block(h, w_in[l], b_in[l], w_attn_o[l], pool_w[l], pool_scale[l],
                          w_pool_o[l], w_out[l])
        r = r + rmsnorm(mix, mix_post_g[l])
        h = rmsnorm(r, ffn_pre_g[l])
        ff = conv_ffn(h, w_ffn_up[l], ffn_conv_w[l], ffn_conv_b[l], w_ffn_down[l])
        r = r + rmsnorm(ff, ffn_post_g[l])
    return r[:, N_META:]
```

```python
import numpy as np
import concourse.bass as bass
import concourse.mybir as mybir
from concourse.bass_utils import run_bass_kernel_spmd

F32 = mybir.dt.float32
BF16 = mybir.dt.bfloat16
AF = mybir.ActivationFunctionType
ALU = mybir.AluOpType

D = 2048
SEQ = 2048
NMETA = 16
L = SEQ + NMETA
DIN = 8200
DFF = 5632
NJ = DFF // 128
EPS = 1e-6
QSCALE = 128 ** -0.5
POOL_WINDOWS = (2, 4, 8, 16)

BLK = [(0, 16)] + [(16 + 128 * i, 128) for i in range(16)]
TILES = [(0, 16)] + [(16 + 512 * i, 512) for i in range(4)]
GROUPS = [dict(start=0, n=528, blocks=list(range(0, 5)), tiles=[(0, 16), (16, 512)])]
for _g in range(1, 4):
    GROUPS.append(dict(start=16 + 512 * _g, n=512, blocks=list(range(1 + 4 * _g, 5 + 4 * _g)),
                       tiles=[(16 + 512 * _g, 512)]))

C_G1, C_BQ, C_BK, C_BV, C_BU, C_BGA, C_BGP, C_PSC, C_G2 = 0, 16, 24, 32, 40, 48, 64, 80, 88
C_CB, C_CW0, C_CW1, C_CW2, C_BF, NCV = 104, 192, 280, 368, 456, 457
K_ID, K_MASK, K_RCNT, K_ONES, K_EPS, NKC = 0, 128, 256, 320, 448, 449


class _Op:
    __slots__ = ("eng", "fn", "deps", "dsem", "ticket", "observed", "idx", "waits")


class Sched:
    def __init__(self, nc):
        self.nc = nc
        self.ops = []
        self.lastw = {}
        self.readers = {}
        self.dma_tot = {}
        self.esem = {}
        self._dsems = {}
        self.bar_op = None
        self.last_on = {}

    def dsem(self, name):
        if name not in self._dsems:
            self._dsems[name] = self.nc.alloc_semaphore("d_" + name)
            self.dma_tot[self._dsems[name]] = 0
        return self._dsems[name]

    def add(self, eng, fn, reads=(), writes=(), dsem=None):
        op = _Op()
        op.eng, op.fn, op.dsem = eng, fn, dsem
        op.idx = len(self.ops)
        op.observed = False
        deps = {}
        for k in reads:
            w = self.lastw.get(k)
            if w is not None:
                deps[w] = deps.get(w, 0) | 1
        for k in writes:
            w = self.lastw.get(k)
            if w is not None:
                deps[w] = deps.get(w, 0) | 1
            for r in self.readers.get(k, ()):
                deps[r] = deps.get(r, 0) | 2
        op.deps = []
        for d, kind in deps.items():
            dop = self.ops[d]
            if dop.dsem is not None:
                op.deps.append(("dma", dop.dsem, self.dma_tot[dop.dsem]))
            else:
                if dop.eng == eng and eng == "pe":
                    continue
                op.deps.append(("eng", d))
        if self.bar_op is not None and eng != "pool":
            op.deps.append(("eng", self.bar_op))
        if dsem is not None:
            self.dma_tot[dsem] += 16
        else:
            self.last_on[eng] = op.idx
        for k in reads:
            self.readers.setdefault(k, []).append(op.idx)
        for k in writes:
            self.lastw[k] = op.idx
            self.readers[k] = []
        self.ops.append(op)
        return op

    def barrier(self, scratch):
        op = _Op()
        op.eng, op.dsem = "pool", None
        op.fn = lambda e: e.memset(scratch, 0.0)
        op.idx = len(self.ops)
        op.observed = False
        op.deps = [("eng", i) for e, i in self.last_on.items() if e != "pool"]
        op.deps += [("dma", s, t) for s, t in self.dma_tot.items() if t > 0]
        self.ops.append(op)
        self.bar_op = op.idx
        self.last_on["pool"] = op.idx
        self.lastw = {}
        self.readers = {}

    def emit(self):
        nc = self.nc
        for o in self.ops:
            for d in o.deps:
                if d[0] == "eng":
                    self.ops[d[1]].observed = True
        cnt = {}
        for o in self.ops:
            if o.dsem is None and o.observed:
                cnt[o.eng] = cnt.get(o.eng, 0) + 1
                o.ticket = cnt[o.eng]
        for e in cnt:
            self.esem[e] = nc.alloc_semaphore("e_" + e)
        for o in self.ops:
            w = {}
            for d in o.deps:
                if d[0] == "dma":
                    sem, val = d[1], d[2]
                else:
                    dop = self.ops[d[1]]
                    sem, val = self.esem[dop.eng], dop.ticket
                if w.get(sem, 0) < val:
                    w[sem] = val
            o.waits = w
        self.stats = dict(cnt)
        with nc.Block() as block:
            for engname, deco in (("sp", block.sync), ("act", block.scalar), ("pe", block.tensor),
                                  ("dve", block.vector), ("pool", block.gpsimd)):
                ops = [o for o in self.ops if o.eng == engname]

                def body(e, ops=ops, engname=engname):
                    waited = {}
                    for o in ops:
                        for sem, val in o.waits.items():
                            if waited.get(sem, 0) < val:
                                e.wait_ge(sem, val)
                                waited[sem] = val
                        ins = o.fn(e)
                        if o.dsem is not None:
                            ins.then_inc(o.dsem, 16)
                        elif o.observed:
                            ins.then_inc(self.esem[o.eng], 1)
                    if engname == "sp":
                        for sem, tot in self.dma_tot.items():
                            if tot > 0:
                                e.wait_ge(sem, tot)

                deco(body)


class Arena:
    def __init__(self, nc):
        self.nc = nc
        self.base = (nc.sbuf_base + 63) // 64 * 64
        self.top = nc.sbuf_top
        self.cur = self.base
        self.n = 0
        self.peak = 0

    def alloc(self, name, shape, dtype):
        esz = 2 if dtype == BF16 else 4
        size = esz
        for s in shape[1:]:
            size *= s
        off = (self.cur + 63) // 64 * 64
        assert off + size <= self.top, f"SBUF overflow allocating {name}: need {off + size - self.top} more bytes"
        self.cur = off + size
        self.peak = max(self.peak, self.cur)
        self.n += 1
        self.last_off = off
        return self.nc.alloc_sbuf_tensor_at(f"{name}_{self.n}", list(shape), dtype, offset=off)

    def mark(self):
        return self.cur

    def reset(self, m):
        self.cur = m


def build_program(stop=None, debug=False):
    nc = bass.Bass("TRN2", target_bir_lowering=False)
    dt = lambda name, shape, kind, dtp=F32: nc.dram_tensor(name, list(shape), dtp, kind=kind).ap()
    x = dt("x", [SEQ, D], "ExternalInput")
    meta = dt("meta", [NMETA, D], "ExternalInput")
    w_in = dt("w_in", [64 * 128, 2048], "ExternalInput")
    w_f = dt("w_f", [128, 128], "ExternalInput")
    w_ap = dt("w_ap", [16 * 128, 2048], "ExternalInput")
    pool_w = dt("pool_w", [128, 2048], "ExternalInput")
    w_out = dt("w_out", [16 * 128, 2048], "ExternalInput")
    w_up = dt("w_up", [88 * 128, 2048], "ExternalInput")
    w_down = dt("w_down", [16 * 128, DFF], "ExternalInput")
    cvec_d = dt("cvec", [128, NCV], "ExternalInput")
    rowv_d = dt("rowv", [2, D], "ExternalInput")
    consts_d = dt("consts", [128, NKC], "ExternalInput")
    out = dt("out", [SEQ, D], "ExternalOutput")
    r1s = dt("r1s", [L, D], "Internal")
    wscB = dt("wscB", [64 * 128, 2048], "Internal", BF16)
    wscC = dt("wscC", [136 * 128, 2048], "Internal", BF16)
    dbg = {}
    if debug:
        dbg["attnT"] = dt("dbg_attnT", [128, 8, L], "ExternalOutput", BF16)
        dbg["poolT"] = dt("dbg_poolT", [128, 8, L], "ExternalOutput", BF16)
        dbg["hT"] = dt("dbg_hT", [128, 16, L], "ExternalOutput", BF16)
        dbg["c8"] = dt("dbg_c8", [8, L], "ExternalOutput")

    S = Sched(nc)
    A = Arena(nc)
    psA = nc.alloc_psum_tensor("psA", [128, 2048], F32)
    psB = nc.alloc_psum_tensor("psB", [128, 2048], F32)

    def bank(i):
        t = psA if i < 4 else psB
        return t[:, (i % 4) * 512:(i % 4 + 1) * 512]

    PS = lambda i: ("ps", i)

    cvec = A.alloc("cvec", [128, NCV], F32)
    consts = A.alloc("consts", [128, NKC], F32)
    ones_bf = A.alloc("ones_bf", [128, 128], BF16)
    stat = A.alloc("stat", [128, 64], F32)
    bar_scr = A.alloc("bar", [128, 8], F32)
    identf = consts[:, K_ID:K_ID + 128]
    maskf = consts[:, K_MASK:K_MASK + 128]
    onesf = consts[:, K_ONES:K_ONES + 128]
    eps_t = consts[:, K_EPS:K_EPS + 1]

    S.add("sp", lambda e: e.dma_start(out=cvec[:], in_=cvec_d[:, :]), writes=["cvec"], dsem=S.dsem("cvec"))
    S.add("sp", lambda e: e.dma_start(out=consts[:], in_=consts_d[:, :]), writes=["consts"], dsem=S.dsem("consts"))
    S.add("dve", lambda e: e.tensor_copy(out=ones_bf[:], in_=onesf), reads=["consts"], writes=["ones_bf"])
    bqs = A.alloc("bqs", [128, 8], F32)
    S.add("dve", lambda e: e.tensor_scalar(out=bqs[:], in0=cvec[:, C_BQ:C_BQ + 8], scalar1=QSCALE, scalar2=None,
                                           op0=ALU.mult), reads=["cvec"], writes=["bqs"])

    class Ring:
        def __init__(self, nf, nb):
            self.f = []
            for i in range(nf):
                self.f.append(A.alloc(f"wf{i}", [128, 2048], F32))
                if i == 0:
                    self.f_off = A.last_off
            self.b = [A.alloc(f"wb{i}", [128, 2048], BF16) for i in range(nb)]
            self.fi = 0
            self.bi = 0
            self.ci = 0
            self.cast_engs = ("dve", "act")

    ring = [None]

    def load_slab(parts, cache=None):
        R = ring[0]
        offs = []
        off = 0
        for (ap3, k, w) in parts:
            offs.append(off)
            off += k * w
        bi = R.bi % len(R.b)
        R.bi += 1
        wb = R.b[bi]
        if cache is not None and cache[2] == "load":
            sc, cid = cache[0], cache[1]
            S.add("sp", lambda e, o=off: e.dma_start(out=wb[:, 0:o], in_=sc[cid * 128:(cid + 1) * 128, 0:o]),
                  reads=[("wsc", cid)], writes=[("wb", bi)], dsem=S.dsem(f"wbl{bi}"))
            return wb, ("wb", bi), offs
        fi = R.fi % len(R.f)
        R.fi += 1
        wf = R.f[fi]
        for (ap3, k, w), o0 in zip(parts, offs):
            dst = wf[:, o0:o0 + k * w]
            S.add("sp", lambda e, dst=dst, ap3=ap3: e.dma_start(out=dst, in_=ap3),
                  writes=[("wf", fi)], dsem=S.dsem(f"wf{fi}"))
        ceng = R.cast_engs[R.ci % len(R.cast_engs)]
        R.ci += 1
        if ceng == "dve":
            S.add("dve", lambda e, o=off: e.tensor_copy(out=wb[:, 0:o], in_=wf[:, 0:o]),
                  reads=[("wf", fi)], writes=[("wb", bi)])
        else:
            S.add("act", lambda e, o=off: e.activation(out=wb[:, 0:o], in_=wf[:, 0:o], func=AF.Copy),
                  reads=[("wf", fi)], writes=[("wb", bi)])
        if cache is not None and cache[2] == "store":
            sc, cid = cache[0], cache[1]
            S.add("pool", lambda e, o=off: e.dma_start(out=sc[cid * 128:(cid + 1) * 128, 0:o], in_=wb[:, 0:o]),
                  reads=[("wb", bi)], writes=[("wsc", cid)], dsem=S.dsem(f"wst{bi}"))
        return wb, ("wb", bi), offs

    def slabv(w, j, nk, ncol, k0=0):
        return (w[j * 128:(j + 1) * 128, k0 * ncol:(k0 + nk) * ncol], nk, ncol)

    class Stream:
        def __init__(self, reqs, pf):
            self.reqs, self.pf, self.nxt, self.loaded = reqs, pf, 0, {}

        def get(self, i):
            hi = min(i + self.pf, len(self.reqs) - 1)
            while self.nxt <= hi:
                r_ = self.reqs[self.nxt]
                self.loaded[self.nxt] = load_slab(*r_) if isinstance(r_, tuple) else load_slab(r_)
                self.nxt += 1
            return self.loaded.pop(i)

    def rms_rows(src_tile, n, col, key_in, junk, junk_key):
        ss = stat[0:n, col:col + 1]
        rt = stat[0:n, 20 + col:21 + col]
        rs = stat[0:n, 40 + col:41 + col]
        S.add("act", lambda e: e.activation(out=junk[0:n, :], in_=src_tile, func=AF.Square, accum_out=ss),
              reads=[key_in], writes=[junk_key, ("ss", col)])
        S.add("act", lambda e: e.activation(out=rt, in_=ss, func=AF.Sqrt, scale=1.0 / D, bias=eps_t[0:n, :]),
              reads=[("ss", col), "consts"], writes=[("rt", col)])
        S.add("dve", lambda e: e.reciprocal(out=rs, in_=rt), reads=[("rt", col)], writes=[("rs", col)])
        return rs, ("rs", col)

    def to_feature_major(tile, n, tile_key, dstT, dst_key, col0, gcol, bankbase, evac_engs=("dve", "dve")):
        for cg in range(4):
            bk = bankbase + (cg % 2)

            def tr(e, cg=cg, bk=bk):
                ins = None
                for i in range(4):
                    c = cg * 4 + i
                    ins = e.transpose(out=bank(bk)[:, i * 128:i * 128 + n], in_=tile[0:n, c * 128:(c + 1) * 128],
                                      identity=identf[0:n, 0:n])
                return ins

            S.add("pe", tr, reads=[tile_key, "consts"], writes=[PS(bk)])
            for i in range(4):
                c = cg * 4 + i
                eng = evac_engs[i % 2]
                if eng == "dve":
                    S.add("dve", lambda e, c=c, i=i, bk=bk: e.tensor_scalar(
                        out=dstT[:, c, col0:col0 + n], in0=bank(bk)[:, i * 128:i * 128 + n],
                        scalar1=cvec[:, gcol + c:gcol + c + 1], scalar2=None, op0=ALU.mult),
                        reads=[PS(bk), "cvec"], writes=[dst_key])
                else:
                    S.add("act", lambda e, c=c, i=i, bk=bk: e.activation(
                        out=dstT[:, c, col0:col0 + n], in_=bank(bk)[:, i * 128:i * 128 + n], func=AF.Copy,
                        scale=cvec[:, gcol + c:gcol + c + 1]),
                        reads=[PS(bk), "cvec"], writes=[dst_key])

    def proj(wb, wkey, woffs, nk_list, ins_list, in_keys, m, bk, n, col_lists, first=True, last=True):
        def fn(e):
            ins = None
            tot = sum(nk_list)
            cnt = 0
            for wi, nk in enumerate(nk_list):
                for k in range(nk):
                    o = woffs[wi] + k * m
                    ins = e.matmul(bank(bk)[0:m, 0:n], lhsT=wb[:, o:o + m], rhs=ins_list[wi](k),
                                   start=(first and cnt == 0), stop=(last and cnt == tot - 1))
                    cnt += 1
            return ins
        S.add("pe", fn, reads=[wkey] + list(in_keys), writes=[PS(bk)])

    mA = A.mark()
    attnT = A.alloc("attnT", [128, 8, L], BF16)
    poolT = A.alloc("poolT", [128, 8, L], BF16)
    mA1 = A.mark()
    hT = A.alloc("hT", [128, 16, L], BF16)
    ring[0] = Ring(2, 3)
    R = ring[0]
    R.cast_engs = ("act",)
    reqA = [[slabv(w_f, 0, 16, 8)]]
    for h in range(8):
        for c0 in (0, 1024, 2048):
            reqA.append([slabv(w_in, (c0 // 1024) * 8 + h, 16, 128)])
    for c in range(8):
        reqA.append([slabv(w_in, 24 + c, 16, 128)])
    stA = Stream(reqA, 2)
    mA2 = A.mark()
    c8 = A.alloc("c8", [8, L], F32)
    negc = A.alloc("negc", [128, 17, 8], F32)
    mA2b = A.mark()

    for b, (t0, n) in enumerate(BLK):
        xt = R.f[b % 2]
        xk = ("wf", b % 2)
        src = meta[0:16, :] if b == 0 else x[t0 - 16:t0 - 16 + n, :]
        S.add("sp", lambda e, xt=xt, src=src, n=n: e.dma_start(out=xt[0:n, :], in_=src),
              writes=[xk], dsem=S.dsem(f"wf{b % 2}"))
        rs, rsk = rms_rows(xt[0:n, :], n, b, xk, R.b[0], ("wb", 0))
        S.add("act", lambda e, xt=xt, n=n, rs=rs: e.activation(out=xt[0:n, :], in_=xt[0:n, :], func=AF.Copy, scale=rs),
              reads=[xk, rsk], writes=[xk])
        to_feature_major(xt, n, xk, hT, ("hT", b), t0, C_G1, 0)
    HT_ALL = [("hT", b) for b in range(17)]

    def hT_keys(t0, n):
        return [("hT", b) for b, (b0, bn) in enumerate(BLK) if b0 < t0 + n and b0 + bn > t0]

    if debug:
        S.add("sp", lambda e: e.dma_start(out=dbg["hT"][:, :, :], in_=hT[:]), reads=HT_ALL, dsem=S.dsem("dbg"))

    wb, wk, wo = stA.get(0)
    lt = [A.alloc(f"lt{i}", [8, L], F32) for i in range(4)]
    for ti, (t0, n) in enumerate(TILES):
        bk = ti % 2
        proj(wb, wk, wo, [16], [lambda k, t0=t0, n=n: hT[:, k, t0:t0 + n]], hT_keys(t0, n), 8, bk, n, None)
        S.add("dve", lambda e, bk=bk, t0=t0, n=n: e.tensor_scalar(
            out=lt[0][:, t0:t0 + n], in0=bank(bk)[0:8, 0:n], scalar1=cvec[0:8, C_BF:C_BF + 1], scalar2=None,
            op0=ALU.add), reads=[PS(bk), "cvec"], writes=["lt0"])
    tf, ta, tb_, tc = lt
    V = lambda eng, fn, r, w: S.add(eng, fn, reads=r, writes=w)
    V("act", lambda e: e.activation(out=ta[:], in_=tf[:], func=AF.Abs), ["lt0"], ["lt1"])
    V("act", lambda e: e.activation(out=ta[:], in_=ta[:], func=AF.Exp, scale=-1.0), ["lt1"], ["lt1"])
    V("dve", lambda e: e.tensor_scalar(out=tb_[:], in0=ta[:], scalar1=2.0, scalar2=None, op0=ALU.add), ["lt1"], ["lt2"])
    V("dve", lambda e: e.reciprocal(out=tb_[:], in_=tb_[:]), ["lt2"], ["lt2"])
    V("dve", lambda e: e.tensor_tensor(out=ta[:], in0=ta[:], in1=tb_[:], op=ALU.mult), ["lt1", "lt2"], ["lt1"])
    V("dve", lambda e: e.tensor_tensor(out=tb_[:], in0=ta[:], in1=ta[:], op=ALU.mult), ["lt1"], ["lt2"])
    V("dve", lambda e: e.tensor_scalar(out=tc[:], in0=tb_[:], scalar1=1.0 / 9.0, scalar2=None, op0=ALU.mult), ["lt2"], ["lt3"])
    for cst in (1.0 / 7.0, 1.0 / 5.0, 1.0 / 3.0):
        V("dve", lambda e, cst=cst: e.scalar_tensor_tensor(out=tc[:], in0=tc[:], scalar=cst, in1=tb_[:],
                                                           op0=ALU.add, op1=ALU.mult), ["lt3", "lt2"], ["lt3"])
    V("dve", lambda e: e.scalar_tensor_tensor(out=tc[:], in0=tc[:], scalar=1.0, in1=ta[:], op0=ALU.add, op1=ALU.mult),
      ["lt3", "lt1"], ["lt3"])
    V("dve", lambda e: e.tensor_scalar(out=ta[:], in0=tf[:], scalar1=0.0, scalar2=None, op0=ALU.min), ["lt0", "lt1"], ["lt1"])
    V("dve", lambda e: e.scalar_tensor_tensor(out=tb_[:], in0=tc[:], scalar=-2.0, in1=ta[:], op0=ALU.mult, op1=ALU.add),
      ["lt3", "lt1", "lt2"], ["lt2"])
    V("dve", lambda e: e.memset(tc[:], 1.0), ["lt3"], ["lt3"])
    V("dve", lambda e: e.tensor_tensor_scan(out=c8[:], data0=tc[:], data1=tb_[:], initial=0.0, op0=ALU.mult, op1=ALU.add),
      ["lt3", "lt2"], ["c8"])
    for b, (t0, n) in enumerate(BLK):
        bk = b % 2
        S.add("pe", lambda e, bk=bk, t0=t0, n=n: e.transpose(out=bank(bk)[0:n, 0:8], in_=c8[0:8, t0:t0 + n],
                                                             identity=identf[0:8, 0:8]),
              reads=["c8", "consts"], writes=[PS(bk)])
        S.add("dve", lambda e, bk=bk, b=b, n=n: e.tensor_scalar(out=negc[0:n, b, :], in0=bank(bk)[0:n, 0:8], scalar1=-1.0,
                                                                scalar2=None, op0=ALU.mult),
              reads=[PS(bk)], writes=["negc"])
    if debug:
        S.add("sp", lambda e: e.dma_start(out=dbg["c8"][:, :], in_=c8[:]), reads=["c8"], dsem=S.dsem("dbg"))
    S.barrier(bar_scr[0:1, 0:1])
    A.reset(mA2b)

    if stop != "A0":
        qT = A.alloc("qT", [128, L], BF16)
        kT = A.alloc("kT", [128, L], BF16)
        vTf = A.alloc("vTf", [128, L], F32)
        Vtm = A.alloc("Vtm", [128, 17, 128], BF16)
        cq = [A.alloc(f"cq{i}", [128, 512], F32) for i in range(2)]
        c8h = A.alloc("c8h", [8, 512], F32)
        PT = [A.alloc(f"PT_{i}", [128, 512], BF16) for i in range(5)]
        SBANK = (2, 3, 6, 5, 0)
        NSD = len(SBANK)
        rden = A.alloc("rden", [128, 512], F32)
        for h in range(8):
            for which, c0, dstname in (("q", 0, "qT"), ("k", 1024, "kT"), ("v", 2048, "vTf")):
                wb, wk, wo = stA.get(1 + h * 3 + (c0 // 1024))
                for ti, (t0, n) in enumerate(TILES):
                    bk = ti % 2
                    proj(wb, wk, wo, [16], [lambda k, t0=t0, n=n: hT[:, k, t0:t0 + n]], hT_keys(t0, n), 128, bk, n, None)
                    if which == "q":
                        S.add("act", lambda e, bk=bk, t0=t0, n=n, h=h: e.activation(
                            out=qT[:, t0:t0 + n], in_=bank(bk)[:, 0:n], func=AF.Identity, scale=QSCALE,
                            bias=bqs[:, h:h + 1]), reads=[PS(bk), "bqs"], writes=[("qT", ti)])
                    elif which == "k":
                        S.add("act", lambda e, bk=bk, t0=t0, n=n, h=h: e.activation(
                            out=kT[:, t0:t0 + n], in_=bank(bk)[:, 0:n], func=AF.Identity,
                            bias=cvec[:, C_BK + h:C_BK + h + 1]), reads=[PS(bk), "cvec"], writes=[("kT", ti)])
                    else:
                        S.add("act", lambda e, bk=bk, t0=t0, n=n, h=h: e.activation(
                            out=vTf[:, t0:t0 + n], in_=bank(bk)[:, 0:n], func=AF.Identity,
                            bias=cvec[:, C_BV + h:C_BV + h + 1]), reads=[PS(bk), "cvec"], writes=[("vTf", ti)])
            for b, (t0, n) in enumerate(BLK):
                bk = b % 2
                ti = 0 if b == 0 else 1 + (b - 1) // 4
                S.add("pe", lambda e, bk=bk, t0=t0, n=n: e.transpose(out=bank(bk)[0:n, 0:128], in_=vTf[:, t0:t0 + n],
                                                                     identity=identf),
                      reads=[("vTf", ti), "consts"], writes=[PS(bk)])
                S.add("act", lambda e, bk=bk, b=b, n=n: e.activation(out=Vtm[0:n, b, :], in_=bank(bk)[0:n, 0:128], func=AF.Copy),
                      reads=[PS(bk)], writes=[("Vtm", b)])
            blocks = []
            for ti, (t0, n) in enumerate(TILES):
                kbs = [(b, k0, kn) for b, (k0, kn) in enumerate(BLK) if k0 < t0 + n]
                for bi_, (b, k0, kn) in enumerate(kbs):
                    qlo = max(t0, k0)
                    blocks.append(dict(ti=ti, t0=t0, n=n, b=b, k0=k0, kn=kn, qlo=qlo, N=t0 + n - qlo, off=qlo - t0,
                                       diag=k0 >= t0, first=bi_ == 0, last=bi_ == len(kbs) - 1,
                                       kti=0 if b == 0 else 1 + (b - 1) // 4))

            def emit_cq(ti, h=h):
                t0, n = TILES[ti]
                cqt = cq[ti % 2]
                S.add("dve", lambda e, t0=t0, n=n, h=h: e.tensor_scalar(
                    out=c8h[:, 0:n], in0=c8[:, t0:t0 + n], scalar1=identf[0:8, h:h + 1], scalar2=None, op0=ALU.mult),
                    reads=["c8", "consts"], writes=["c8h"])
                S.add("pe", lambda e, n=n, ti=ti: e.matmul(bank(1)[:, 0:n], lhsT=onesf[0:8, :], rhs=c8h[:, 0:n],
                                                           start=True, stop=True),
                      reads=["c8h", "consts"], writes=[PS(1)])
                S.add("act", lambda e, n=n, ti=ti, cqt=cqt: e.activation(out=cqt[:, 0:n], in_=bank(1)[:, 0:n], func=AF.Copy),
                      reads=[PS(1)], writes=[("cq", ti % 2)])

            def emit_S(idx):
                B_ = blocks[idx]
                sb = SBANK[idx % NSD]
                S.add("pe", lambda e, sb=sb, k0=B_["k0"], kn=B_["kn"], qlo=B_["qlo"], N=B_["N"]: e.matmul(
                    bank(sb)[0:kn, 0:N], lhsT=kT[:, k0:k0 + kn], rhs=qT[:, qlo:qlo + N], start=True, stop=True),
                    reads=[("kT", B_["kti"]), ("qT", B_["ti"])], writes=[PS(sb)])

            emit_cq(0)
            for _i in range(min(NSD, len(blocks))):
                emit_S(_i)
            for idx, B_ in enumerate(blocks):
                ti, t0, n, b, kn, N, off = B_["ti"], B_["t0"], B_["n"], B_["b"], B_["kn"], B_["N"], B_["off"]
                ob, db = 4, 7
                sb = SBANK[idx % NSD]
                tb = idx % NSD
                cqt = cq[ti % 2]
                if B_["first"] and ti + 1 < len(TILES):
                    emit_cq(ti + 1)
                S.add("dve", lambda e, sb=sb, kn=kn, N=N, off=off, cqt=cqt: e.tensor_tensor(
                    out=bank(sb)[0:kn, 0:N], in0=bank(sb)[0:kn, 0:N], in1=cqt[0:kn, off:off + N], op=ALU.add),
                    reads=[PS(sb), ("cq", ti % 2)], writes=[PS(sb)])
                if B_["diag"]:
                    S.add("dve", lambda e, sb=sb, kn=kn: e.tensor_tensor(
                        out=bank(sb)[0:kn, 0:kn], in0=bank(sb)[0:kn, 0:kn], in1=maskf[0:kn, 0:kn], op=ALU.add),
                        reads=[PS(sb), "consts"], writes=[PS(sb)])
                S.add("act", lambda e, sb=sb, tb=tb, kn=kn, N=N, b=b, h=h: e.activation(
                    out=PT[tb][0:kn, 0:N], in_=bank(sb)[0:kn, 0:N], func=AF.Exp, bias=negc[0:kn, b, h:h + 1]),
                    reads=[PS(sb), "negc"], writes=[("PT", tb)])

                def pv(e, tb=tb, kn=kn, N=N, off=off, b=b, ob=ob, db=db, first=B_["first"], lastb=B_["last"]):
                    e.matmul(bank(ob)[:, off:off + N], lhsT=Vtm[0:kn, b, :], rhs=PT[tb][0:kn, 0:N],
                             start=first, stop=lastb)
                    return e.matmul(bank(db)[:, off:off + N], lhsT=ones_bf[0:kn, :], rhs=PT[tb][0:kn, 0:N],
                                    start=first, stop=lastb)
                S.add("pe", pv, reads=[("PT", tb), ("Vtm", b), "ones_bf"], writes=[PS(ob), PS(db)])
                if idx + NSD < len(blocks):
                    emit_S(idx + NSD)
                if B_["last"]:
                    S.add("dve", lambda e, db=db, n=n: e.reciprocal(out=rden[:, 0:n], in_=bank(db)[:, 0:n]),
                          reads=[PS(db)], writes=["rden"])
                    S.add("dve", lambda e, ob=ob, t0=t0, n=n, h=h: e.tensor_tensor(
                        out=attnT[:, h, t0:t0 + n], in0=bank(ob)[:, 0:n], in1=rden[:, 0:n], op=ALU.mult),
                        reads=[PS(ob), "rden"], writes=[("attnT", ti)])
        S.barrier(bar_scr[0:1, 0:1])
        A.reset(mA2)

        ub = [A.alloc(f"ub{i}", [128, 16 + L], F32) for i in range(2)]
        tA = A.alloc("tA", [128, 16 + L], F32)
        tB = A.alloc("tB", [128, 16 + L], F32)
        dT = A.alloc("dT", [128, 2, L], BF16)
        t16 = A.alloc("t16", [128, 16], F32)
        for i, tt in enumerate((ub[0], ub[1], tA, tB)):
            S.add("pool", lambda e, tt=tt: e.memset(tt[:, 0:16], 0.0), writes=[("pad", i)])
        pwb = A.alloc("pwb", [128, 2048], BF16)
        pwk, pwo = "pwb", [0]
        _fi = R.fi % len(R.f)
        R.fi += 1
        S.add("sp", lambda e: e.dma_start(out=R.f[_fi][:, :], in_=pool_w[:, :]),
              writes=[("wf", _fi)], dsem=S.dsem(f"wf{_fi}"))
        S.add("pool", lambda e: e.tensor_copy(out=pwb[:], in_=R.f[_fi][:]), reads=[("wf", _fi)], writes=["pwb"])
        for c in range(8):
            g = c // 2
            w = POOL_WINDOWS[g]
            u = ub[c % 2]
            uk = ("ub", c % 2)
            wb, wk, wo = stA.get(25 + c)
            for ti, (t0, n) in enumerate(TILES):
                bk = ti % 2
                proj(wb, wk, wo, [16], [lambda k, t0=t0, n=n: hT[:, k, t0:t0 + n]], hT_keys(t0, n), 128, bk, n, None)
                S.add("act", lambda e, bk=bk, t0=t0, n=n, c=c, u=u: e.activation(
                    out=u[:, 16 + t0:16 + t0 + n], in_=bank(bk)[:, 0:n], func=AF.Identity,
                    bias=cvec[:, C_BU + c:C_BU + c + 1]), reads=[PS(bk), "cvec", ("pad", c % 2)], writes=[uk])
            src, srck, srcpad = u, uk, ("pad", c % 2)
            sh = 1
            pp = [(tA, "tA", ("pad", 2)), (tB, "tB", ("pad", 3))]
            pi = 0
            while sh < w:
                dst, dk, dpad = pp[pi % 2]
                S.add("dve", lambda e, dst=dst, src=src, sh=sh: e.tensor_tensor(
                    out=dst[:, 16:16 + L], in0=src[:, 16:16 + L], in1=src[:, 16 - sh:16 - sh + L], op=ALU.add),
                    reads=[srck, srcpad], writes=[dk])
                src, srck, srcpad = dst, dk, dpad
                sh *= 2
                pi += 1
            wi = POOL_WINDOWS.index(w)
            S.add("dve", lambda e, src=src, u=u, w=w, c=c: e.scalar_tensor_tensor(
                out=dT[:, c % 2, 16:L], in0=src[:, 32:16 + L], scalar=1.0 / w, in1=u[:, 32:16 + L],
                op0=ALU.mult, op1=ALU.subtract), reads=[srck, uk], writes=[("dT", c % 2)])
            S.add("dve", lambda e, src=src, wi=wi: e.tensor_tensor(
                out=t16[:], in0=src[:, 16:32], in1=consts[:, K_RCNT + wi * 16:K_RCNT + wi * 16 + 16], op=ALU.mult),
                reads=[srck, "consts"], writes=["t16"])
            S.add("dve", lambda e, u=u, c=c: e.tensor_tensor(
                out=dT[:, c % 2, 0:16], in0=t16[:], in1=u[:, 16:32], op=ALU.subtract),
                reads=["t16", uk], writes=[("dT", c % 2)])
            if c % 2 == 1:
                for ocl in range(2):
                    oc = 2 * g + ocl
                    for ti, (t0, n) in enumerate(TILES):
                        bk = ti % 2

                        def fn(e, g=g, ocl=ocl, bk=bk, t0=t0, n=n):
                            ins = None
                            for kl in range(2):
                                o = pwo[0] + (g * 2 + kl) * 256 + ocl * 128
                                ins = e.matmul(bank(bk)[:, 0:n], lhsT=pwb[:, o:o + 128], rhs=dT[:, kl, t0:t0 + n],
                                               start=(kl == 0), stop=(kl == 1))
                            return ins
                        S.add("pe", fn, reads=[pwk, ("dT", 0), ("dT", 1)], writes=[PS(bk)])
                        S.add("act", lambda e, bk=bk, oc=oc, t0=t0, n=n: e.activation(
                            out=poolT[:, oc, t0:t0 + n], in_=bank(bk)[:, 0:n], func=AF.Copy,
                            scale=cvec[:, C_PSC + oc:C_PSC + oc + 1]), reads=[PS(bk), "cvec"], writes=[("poolT", ti)])
        if debug:
            S.add("sp", lambda e: e.dma_start(out=dbg["attnT"][:, :, :], in_=attnT[:]),
                  reads=[("attnT", i) for i in range(5)], dsem=S.dsem("dbg"))
            S.add("sp", lambda e: e.dma_start(out=dbg["poolT"][:, :, :], in_=poolT[:]),
                  reads=[("poolT", i) for i in range(5)], dsem=S.dsem("dbg"))
        S.barrier(bar_scr[0:1, 0:1])
    A.reset(mA1)

    if stop not in ("A0", "A"):
        gpost = A.alloc("gpost", [128, D], F32)
        S.add("sp", lambda e: e.dma_start(out=gpost[:], in_=rowv_d[0:1, :].partition_broadcast(128)),
              writes=["gpost"], dsem=S.dsem("gpost"))
        xtB = A.alloc("xtB", [128, D], F32)
        junk = A.alloc("junk", [128, D], BF16)
        r1t = [A.alloc(f"r1t{i}", [128, D], F32) for i in range(2)]
        mTg = A.alloc("mTg", [128, 16, 528], BF16)
        ring[0] = Ring(4, 4)
        reqB = []
        for _gi in range(4):
            _m = "store" if _gi == 0 else "load"
            for c in range(16):
                reqB.append(([slabv(w_in, 32 + c, 16, 128)], (wscB, c * 3, _m)))
                reqB.append(([slabv(w_in, 48 + c, 16, 128)], (wscB, c * 3 + 1, _m)))
                reqB.append(([slabv(w_ap, c, 16, 128)], (wscB, c * 3 + 2, _m)))
            for oc in range(16):
                reqB.append(([slabv(w_out, oc, 16, 128)], (wscB, 48 + oc, _m)))
        stB = Stream(reqB, 3)
        mB = A.mark()
        mixT = A.alloc("mixT", [128, 16, 528], F32)
        A.reset(mB)
        hTg = A.alloc("hTg", [128, 16, 528], BF16)
        gt = [[A.alloc(f"gt{i}_{j}", [128, 512], F32) for j in range(4)] for i in range(2)]
        mixT1 = nc.alloc_sbuf_tensor_at("mixT1", [128, 16, 512], F32, offset=ring[0].f_off)
        rot = [(xtB, "xtB", "xtB"), (r1t[0], ("r1t", 0), "r1st0"), (r1t[1], ("r1t", 1), "r1st1")]

        def B0_block(gi, bi0, bankbase):
            G = GROUPS[gi]
            b = G["blocks"][bi0]
            t0, n = BLK[b]
            xt_, xk_, xs_ = rot[bi0 % 3]
            src = meta[0:16, :] if b == 0 else x[t0 - 16:t0 - 16 + n, :]
            S.add("sp", lambda e, src=src, n=n, xt_=xt_: e.dma_start(out=xt_[0:n, :], in_=src), writes=[xk_],
                  dsem=S.dsem(xs_))
            rs, rsk = rms_rows(xt_[0:n, :], n, b, xk_, junk, "junk")
            S.add("act", lambda e, n=n, rs=rs, xt_=xt_: e.activation(out=xt_[0:n, :], in_=xt_[0:n, :], func=AF.Copy, scale=rs),
                  reads=[xk_, rsk], writes=[xk_])
            to_feature_major(xt_, n, xk_, hTg, ("hTg", bi0), t0 - G["start"], C_G1, bankbase)

        def B1(gi):
            G = GROUPS[gi]
            gs = G["start"]
            hk = [("hTg", i) for i in range(len(G["blocks"]))]
            it = 0
            for c in range(16):
                tl = []
                for (t0, n) in G["tiles"]:
                    tl.append((t0, n, t0 - gs, 4 * (it % 2), it % 2, TILES.index((t0, n))))
                    it += 1
                wga = stB.get(gi * 64 + c * 3)
                for (t0, n, lo, pb, par, ti) in tl:
                    proj(wga[0], wga[1], wga[2], [16], [lambda k, lo=lo, n=n: hTg[:, k, lo:lo + n]], hk, 128, pb + 0, n, None)
                wgp = stB.get(gi * 64 + c * 3 + 1)
                for (t0, n, lo, pb, par, ti) in tl:
                    proj(wgp[0], wgp[1], wgp[2], [16], [lambda k, lo=lo, n=n: hTg[:, k, lo:lo + n]], hk, 128, pb + 1, n, None)
                wap = stB.get(gi * 64 + c * 3 + 2)
                for (t0, n, lo, pb, par, ti) in tl:
                    proj(wap[0], wap[1], [wap[2][0]], [8], [lambda k, t0=t0, n=n: attnT[:, k, t0:t0 + n]],
                         [("attnT", ti)], 128, pb + 2, n, None)
                    proj(wap[0], wap[1], [wap[2][0] + 1024], [8], [lambda k, t0=t0, n=n: poolT[:, k, t0:t0 + n]],
                         [("poolT", ti)], 128, pb + 3, n, None)
                for (t0, n, lo, pb, par, ti) in tl:
                    g4 = gt[par]
                    S.add("act", lambda e, pb=pb, g4=g4, n=n, c=c: e.activation(
                        out=g4[0][:, 0:n], in_=bank(pb)[:, 0:n], func=AF.Sigmoid, bias=cvec[:, C_BGA + c:C_BGA + c + 1]),
                        reads=[PS(pb), "cvec"], writes=[("gt", par, 0)])
                    S.add("act", lambda e, pb=pb, g4=g4, n=n, c=c: e.activation(
                        out=g4[1][:, 0:n], in_=bank(pb + 1)[:, 0:n], func=AF.Sigmoid, bias=cvec[:, C_BGP + c:C_BGP + c + 1]),
                        reads=[PS(pb + 1), "cvec"], writes=[("gt", par, 1)])
                    S.add("dve", lambda e, pb=pb, g4=g4, n=n: e.tensor_tensor(
                        out=g4[2][:, 0:n], in0=g4[0][:, 0:n], in1=bank(pb + 2)[:, 0:n], op=ALU.mult),
                        reads=[PS(pb + 2), ("gt", par, 0)], writes=[("gt", par, 2)])
                    S.add("dve", lambda e, pb=pb, g4=g4, n=n: e.tensor_tensor(
                        out=g4[3][:, 0:n], in0=g4[1][:, 0:n], in1=bank(pb + 3)[:, 0:n], op=ALU.mult),
                        reads=[PS(pb + 3), ("gt", par, 1)], writes=[("gt", par, 3)])
                    S.add("pool", lambda e, g4=g4, n=n, c=c, lo=lo: e.tensor_tensor(
                        out=mTg[:, c, lo:lo + n], in0=g4[2][:, 0:n], in1=g4[3][:, 0:n], op=ALU.add),
                        reads=[("gt", par, 2), ("gt", par, 3)], writes=[("mTg", c)])

        def B2_iter(gi, oc, itc):
            G = GROUPS[gi]
            gs = G["start"]
            mx = mixT if gi == 0 else mixT1
            mk = [("mTg", c) for c in range(16)]
            wo_ = stB.get(gi * 64 + 48 + oc)
            for (t0, n) in G["tiles"]:
                lo = t0 - gs
                bk = itc[0] % 2
                itc[0] += 1
                proj(wo_[0], wo_[1], wo_[2], [16], [lambda k, lo=lo, n=n: mTg[:, k, lo:lo + n]], mk, 128, bk, n, None)
                S.add("act", lambda e, bk=bk, oc=oc, lo=lo, n=n, mx=mx: e.activation(
                    out=mx[:, oc, lo:lo + n], in_=bank(bk)[:, 0:n], func=AF.Copy),
                    reads=[PS(bk)], writes=[("mixT", oc)])

        def B3(gi):
            G = GROUPS[gi]
            gs = G["start"]
            mx = mixT if gi == 0 else mixT1
            xk_ = [("mixT", oc) for oc in range(16)]
            for bi_, b in enumerate(G["blocks"]):
                t0, n = BLK[b]
                lo = t0 - gs
                half = 4 * (bi_ % 2)
                pst = psA if half == 0 else psB
                for cg in range(4):
                    def tr(e, cg=cg, lo=lo, n=n, half=half, mx=mx):
                        ins = None
                        for i in range(4):
                            oc = cg * 4 + i
                            ins = e.transpose(out=bank(half + cg)[0:n, i * 128:(i + 1) * 128], in_=mx[:, oc, lo:lo + n],
                                              identity=identf)
                        return ins
                    S.add("pe", tr, reads=xk_ + ["consts"], writes=[PS(half + cg)])
                pkeys = [PS(half + i) for i in range(4)]
                rt_ = r1t[bi_ % 2]
                rk = ("r1t", bi_ % 2)
                src = meta[0:16, :] if b == 0 else x[t0 - 16:t0 - 16 + n, :]
                S.add("sp", lambda e, src=src, n=n: e.dma_start(out=xtB[0:n, :], in_=src), writes=["xtB"], dsem=S.dsem("xtB"))
                ss = stat[0:n, b:b + 1]
                S.add("act", lambda e, pst=pst, n=n, ss=ss: e.activation(out=junk[0:n, :], in_=pst[0:n, :], func=AF.Square,
                                                                         accum_out=ss),
                      reads=pkeys, writes=["junk", ("ss", b)])
                rtt = stat[0:n, 20 + b:21 + b]
                rs = stat[0:n, 40 + b:41 + b]
                S.add("act", lambda e, rtt=rtt, ss=ss, n=n: e.activation(out=rtt, in_=ss, func=AF.Sqrt, scale=1.0 / D,
                                                                         bias=eps_t[0:n, :]),
                      reads=[("ss", b), "consts"], writes=[("rt", b)])
                S.add("dve", lambda e, rs=rs, rtt=rtt: e.reciprocal(out=rs, in_=rtt), reads=[("rt", b)], writes=[("rs", b)])
                S.add("dve", lambda e, pst=pst, n=n, rs=rs, rt_=rt_: e.scalar_tensor_tensor(
                    out=rt_[0:n, :], in0=pst[0:n, :], scalar=rs, in1=gpost[0:n, :], op0=ALU.mult, op1=ALU.mult),
                    reads=pkeys + [("rs", b), "gpost"], writes=[rk])
                S.add("pool", lambda e, n=n, rt_=rt_: e.tensor_tensor(out=rt_[0:n, :], in0=rt_[0:n, :], in1=xtB[0:n, :],
                                                                       op=ALU.add),
                      reads=[rk, "xtB"], writes=[rk])
                S.add("sp", lambda e, n=n, rt_=rt_, t0=t0: e.dma_start(out=r1s[t0:t0 + n, :], in_=rt_[0:n, :]),
                      reads=[rk], writes=[("r1s", b)], dsem=S.dsem(f"r1st{bi_ % 2}"))

        itc = [0]
        for bi0 in range(len(GROUPS[0]["blocks"])):
            B0_block(0, bi0, 0)
        B1(0)
        for oc in range(16):
            B2_iter(0, oc, itc)
        B3(0)
        S.barrier(bar_scr[0:1, 0:1])
        for bi0 in range(4):
            B0_block(1, bi0, 0)
        for gi in range(1, 4):
            B1(gi)
            b0_at = {2: 0, 5: 1, 8: 2, 11: 3} if gi < 3 else {}
            for oc in range(16):
                B2_iter(gi, oc, itc)
                if oc in b0_at:
                    B0_block(gi + 1, b0_at[oc], 2)
            B3(gi)
        S.barrier(bar_scr[0:1, 0:1])
    A.reset(mA)

    if stop not in ("A0", "A", "B"):
        gpost2 = A.alloc("gpost2", [128, D], F32)
        S.add("sp", lambda e: e.dma_start(out=gpost2[:], in_=rowv_d[1:2, :].partition_broadcast(128)),
              writes=["gpost2"], dsem=S.dsem("gpost2"))
        junkC = A.alloc("junkC", [128, D], BF16)
        r1c = [A.alloc(f"r1c{i}", [128, D], F32) for i in range(2)]
        ot = [A.alloc(f"ot{i}", [128, D], F32) for i in range(2)]
        carry = A.alloc("carry", [128, 88, 2], F32)
        S.add("pool", lambda e: e.memset(carry[:], 0.0), writes=["carry"])
        h2T = A.alloc("h2T", [128, 16, 528], BF16)
        actT = A.alloc("actT", [128, NJ, 528], BF16)
        ffT = A.alloc("ffT", [128, 16, 512], F32)
        upb = [[A.alloc(f"up{i}_{j}", [128, 530], F32) for j in range(2)] for i in range(2)]
        cv = [[A.alloc(f"cv{i}_{j}", [128, 528], F32) for j in range(3)] for i in range(2)]
        ring[0] = Ring(3, 4)
        reqC = []
        for _gi in range(4):
            _m = "store" if _gi == 0 else "load"
            for j in range(NJ):
                reqC.append(([slabv(w_up, j, 16, 128)], (wscC, j * 2, _m)))
                reqC.append(([slabv(w_up, NJ + j, 16, 128)], (wscC, j * 2 + 1, _m)))
            for oc in range(16):
                for pi_, (k0, nk) in enumerate(((0, 16), (16, 16), (32, 12))):
                    reqC.append(([slabv(w_down, oc, nk, 128, k0)], (wscC, 88 + oc * 3 + pi_, _m)))
        stC = Stream(reqC, 3)

        def C0_front(gi, bi_):
            G = GROUPS[gi]
            b = G["blocks"][bi_]
            t0, n = BLK[b]
            rc = r1c[bi_ % 2]
            rck = ("r1c", bi_ % 2)
            S.add("sp", lambda e, rc=rc, t0=t0, n=n: e.dma_start(out=rc[0:n, :], in_=r1s[t0:t0 + n, :]),
                  reads=[("r1s", b)], writes=[rck], dsem=S.dsem(f"r1c{bi_ % 2}"))
            rs, rsk = rms_rows(rc[0:n, :], n, b, rck, junkC, "junkC")
            S.add("act", lambda e, rc=rc, n=n, rs=rs: e.activation(out=rc[0:n, :], in_=rc[0:n, :], func=AF.Copy, scale=rs),
                  reads=[rck, rsk], writes=[rck])

        def C0_back(gi, bi_, bankbase):
            G = GROUPS[gi]
            b = G["blocks"][bi_]
            t0, n = BLK[b]
            to_feature_major(r1c[bi_ % 2], n, ("r1c", bi_ % 2), h2T, ("h2T", bi_), t0 - G["start"], C_G2, bankbase)

        def C0_block(gi, bi_, bankbase):
            C0_front(gi, bi_)
            C0_back(gi, bi_, bankbase)

        def C1_iter(gi, j):
            G = GROUPS[gi]
            gs, gn = G["start"], G["n"]
            hk = [("h2T", i) for i in range(len(G["blocks"]))]
            s2 = j % 2
            for half_, jj in ((0, j), (1, NJ + j)):
                wu = stC.get(gi * 136 + j * 2 + half_)
                ub_ = upb[s2][half_]
                ubk = ("upb", s2, half_)
                S.add("pool", lambda e, ub_=ub_, jj=jj: e.tensor_copy(out=ub_[:, 0:2], in_=carry[:, jj, :]),
                      reads=["carry"], writes=[ubk])
                for tix, (t0, n) in enumerate(G["tiles"]):
                    lo = t0 - gs
                    bk = 2 * half_ + (j + tix) % 2
                    proj(wu[0], wu[1], wu[2], [16], [lambda k, lo=lo, n=n: h2T[:, k, lo:lo + n]], hk, 128, bk, n, None)
                    S.add("act", lambda e, bk=bk, ub_=ub_, lo=lo, n=n: e.activation(
                        out=ub_[:, 2 + lo:2 + lo + n], in_=bank(bk)[:, 0:n], func=AF.Copy),
                        reads=[PS(bk)], writes=[ubk])
                S.add("pool", lambda e, ub_=ub_, jj=jj, gn=gn: e.tensor_copy(out=carry[:, jj, :], in_=ub_[:, gn:gn + 2]),
                      reads=[ubk], writes=["carry"])
                tg = cv[s2][half_]
                tgk = ("cv", s2, half_)
                S.add("dve", lambda e, tg=tg, ub_=ub_, jj=jj, gn=gn: e.tensor_scalar(
                    out=tg[:, 0:gn], in0=ub_[:, 0:gn], scalar1=cvec[:, C_CW0 + jj:C_CW0 + jj + 1],
                    scalar2=cvec[:, C_CB + jj:C_CB + jj + 1], op0=ALU.mult, op1=ALU.add),
                    reads=[ubk, "cvec"], writes=[tgk])
                S.add("dve", lambda e, tg=tg, ub_=ub_, jj=jj, gn=gn: e.scalar_tensor_tensor(
                    out=tg[:, 0:gn], in0=ub_[:, 1:gn + 1], scalar=cvec[:, C_CW1 + jj:C_CW1 + jj + 1], in1=tg[:, 0:gn],
                    op0=ALU.mult, op1=ALU.add), reads=[ubk, "cvec", tgk], writes=[tgk])
                S.add("dve", lambda e, tg=tg, ub_=ub_, jj=jj, gn=gn: e.scalar_tensor_tensor(
                    out=tg[:, 0:gn], in0=ub_[:, 2:gn + 2], scalar=cvec[:, C_CW2 + jj:C_CW2 + jj + 1], in1=tg[:, 0:gn],
                    op0=ALU.mult, op1=ALU.add), reads=[ubk, "cvec", tgk], writes=[tgk])
            gl = cv[s2][2]
            S.add("act", lambda e, gl=gl, s2=s2, gn=gn: e.activation(out=gl[:, 0:gn], in_=cv[s2][0][:, 0:gn],
                                                                      func=AF.Gelu_apprx_tanh),
                  reads=[("cv", s2, 0)], writes=[("cv", s2, 2)])
            S.add("pool", lambda e, gl=gl, s2=s2, gn=gn, j=j: e.tensor_tensor(
                out=actT[:, j, 0:gn], in0=gl[:, 0:gn], in1=cv[s2][1][:, 0:gn], op=ALU.mult),
                reads=[("cv", s2, 2), ("cv", s2, 1)], writes=[("actT", j)])

        def C2_iter(gi, oc):
            G = GROUPS[gi]
            t0r, nr = G["tiles"][-1]
            lo = t0r - G["start"]
            ak = [("actT", j) for j in range(NJ)]
            bk = oc % 2
            for pi, (k0, nk) in enumerate(((0, 16), (16, 16), (32, 12))):
                wd = stC.get(gi * 136 + 88 + oc * 3 + pi)
                proj(wd[0], wd[1], wd[2], [nk], [lambda k, k0=k0, lo=lo, nr=nr: actT[:, k0 + k, lo:lo + nr]],
                     ak, 128, bk, nr, None, first=(pi == 0), last=(pi == 2))
            S.add("act", lambda e, bk=bk, oc=oc, nr=nr: e.activation(out=ffT[:, oc, 0:nr], in_=bank(bk)[:, 0:nr], func=AF.Copy),
                  reads=[PS(bk)], writes=[("ffT", oc)])

        def C3_block(gi, bi_):
            G = GROUPS[gi]
            t0r, nr = G["tiles"][-1]
            rblocks = [b for b in G["blocks"] if b > 0]
            b = rblocks[bi_]
            t0, n = BLK[b]
            lo2 = t0 - t0r
            fk = [("ffT", oc) for oc in range(16)]
            half = 4
            pst = psB
            for cg in range(4):
                def tr(e, cg=cg, lo2=lo2, n=n, half=half):
                    ins = None
                    for i in range(4):
                        oc = cg * 4 + i
                        ins = e.transpose(out=bank(half + cg)[0:n, i * 128:(i + 1) * 128], in_=ffT[:, oc, lo2:lo2 + n],
                                          identity=identf)
                    return ins
                S.add("pe", tr, reads=fk + ["consts"], writes=[PS(half + cg)])
            pkeys = [PS(half + i) for i in range(4)]
            rc = r1c[bi_ % 2]
            rck = ("r1c", bi_ % 2)
            S.add("sp", lambda e, rc=rc, t0=t0, n=n: e.dma_start(out=rc[0:n, :], in_=r1s[t0:t0 + n, :]),
                  reads=[("r1s", b)], writes=[rck], dsem=S.dsem(f"r1c{bi_ % 2}"))
            ss = stat[0:n, b:b + 1]
            S.add("act", lambda e, pst=pst, n=n, ss=ss: e.activation(out=junkC[0:n, :], in_=pst[0:n, :], func=AF.Square,
                                                                     accum_out=ss),
                  reads=pkeys, writes=["junkC", ("ss", b)])
            rtt = stat[0:n, 20 + b:21 + b]
            rs = stat[0:n, 40 + b:41 + b]
            S.add("act", lambda e, rtt=rtt, ss=ss, n=n: e.activation(out=rtt, in_=ss, func=AF.Sqrt, scale=1.0 / D,
                                                                     bias=eps_t[0:n, :]),
                  reads=[("ss", b), "consts"], writes=[("rt", b)])
            S.add("dve", lambda e, rs=rs, rtt=rtt: e.reciprocal(out=rs, in_=rtt), reads=[("rt", b)], writes=[("rs", b)])
            o_ = ot[bi_ % 2]
            ok_ = ("ot", bi_ % 2)
            S.add("dve", lambda e, pst=pst, n=n, rs=rs, o_=o_: e.scalar_tensor_tensor(
                out=o_[0:n, :], in0=pst[0:n, :], scalar=rs, in1=gpost2[0:n, :], op0=ALU.mult, op1=ALU.mult),
                reads=pkeys + [("rs", b), "gpost2"], writes=[ok_])
            S.add("pool", lambda e, n=n, o_=o_, rc=rc: e.tensor_tensor(out=o_[0:n, :], in0=o_[0:n, :], in1=rc[0:n, :],
                                                                        op=ALU.add),
                  reads=[ok_, rck], writes=[ok_])
            S.add("sp", lambda e, n=n, o_=o_, t0=t0: e.dma_start(out=out[t0 - 16:t0 - 16 + n, :], in_=o_[0:n, :]),
                  reads=[ok_], dsem=S.dsem(f"ost{bi_ % 2}"))

        for bi_ in range(len(GROUPS[0]["blocks"])):
            C0_block(0, bi_, 0)
        for gi in range(4):
            c3_at = {4: 0, 12: 1, 20: 2, 28: 3} if gi > 0 else {}
            for j in range(NJ):
                C1_iter(gi, j)
                if j in c3_at:
                    C3_block(gi - 1, c3_at[j])
            c0f_at = {0: 0, 3: 1, 6: 2, 9: 3} if gi < 3 else {}
            c0b_at = {2: 0, 5: 1, 8: 2, 11: 3} if gi < 3 else {}
            for oc in range(16):
                C2_iter(gi, oc)
                if oc in c0f_at:
                    C0_front(gi + 1, c0f_at[oc])
                if oc in c0b_at:
                    C0_back(gi + 1, c0b_at[oc], 2)
        for bi_ in range(4):
            C3_block(3, bi_)
    S.emit()
    return nc, S, A


def host_layout(inputs):
    f32 = np.float32
    fm = lambda v: np.ascontiguousarray(np.asarray(v, f32).reshape(-1, 128).T)
    b_in = np.asarray(inputs["b_in"], f32)[0]
    cvec = np.zeros((128, NCV), f32)
    cvec[:, C_G1:C_G1 + 16] = fm(inputs["mix_pre_g"][0])
    cvec[:, C_BQ:C_BQ + 8] = fm(b_in[0:1024])
    cvec[:, C_BK:C_BK + 8] = fm(b_in[1024:2048])
    cvec[:, C_BV:C_BV + 8] = fm(b_in[2048:3072])
    cvec[:, C_BU:C_BU + 8] = fm(b_in[3080:4104])
    cvec[:, C_BGA:C_BGA + 16] = fm(b_in[4104:6152])
    cvec[:, C_BGP:C_BGP + 16] = fm(b_in[6152:8200])
    cvec[:, C_PSC:C_PSC + 8] = fm(inputs["pool_scale"][0])
    cvec[:, C_G2:C_G2 + 16] = fm(inputs["ffn_pre_g"][0])
    cvec[:, C_CB:C_CB + 88] = fm(inputs["ffn_conv_b"][0])
    cw = np.asarray(inputs["ffn_conv_w"], f32)[0]
    cvec[:, C_CW0:C_CW0 + 88] = fm(cw[0])
    cvec[:, C_CW1:C_CW1 + 88] = fm(cw[1])
    cvec[:, C_CW2:C_CW2 + 88] = fm(cw[2])
    cvec[0:8, C_BF] = b_in[3072:3080]
    rowv = np.stack([np.asarray(inputs["mix_post_g"], f32)[0], np.asarray(inputs["ffn_post_g"], f32)[0]])
    consts = np.zeros((128, NKC), f32)
    consts[:, K_ID:K_ID + 128] = np.eye(128, dtype=f32)
    p = np.arange(128)[:, None]
    j = np.arange(128)[None, :]
    consts[:, K_MASK:K_MASK + 128] = np.where(j >= p, 0.0, -30000.0)
    for wi, w in enumerate(POOL_WINDOWS):
        for t in range(16):
            consts[:, K_RCNT + wi * 16 + t] = 1.0 / min(t + 1, w)
    consts[:, K_ONES:K_ONES + 128] = 1.0
    consts[:, K_EPS] = EPS
    return cvec, np.ascontiguousarray(rowv), consts


_CACHE = {}


def kernel(**inputs):
    f32 = np.float32
    x = np.asarray(inputs["x"], f32)
    B = x.shape[0]
    cvec, rowv, consts = host_layout(inputs)
    if "nc" not in _CACHE:
        _CACHE["nc"] = build_program()[0]
    nc = _CACHE["nc"]
    def slabs(w):
        K_, N_ = w.shape
        t = w.reshape(K_ // 128, 128, N_ // 128, 128).transpose(2, 1, 0, 3)
        return np.ascontiguousarray(t).reshape(N_ // 128 * 128, K_)

    win = np.asarray(inputs["w_in"], f32)[0]
    win_main = np.concatenate([win[:, 0:3072], win[:, 3080:8200]], axis=1)
    wf = np.ascontiguousarray(win[:, 3072:3080].reshape(16, 128, 8).transpose(1, 0, 2)).reshape(128, 128)
    wao = slabs(np.asarray(inputs["w_attn_o"], f32)[0])
    wpo = slabs(np.asarray(inputs["w_pool_o"], f32)[0])
    pw = np.asarray(inputs["pool_w"], f32)[0].reshape(4, 2, 128, 256).transpose(2, 0, 1, 3)
    shared = {
        "meta": np.ascontiguousarray(np.asarray(inputs["meta_tokens"], f32)),
        "w_in": slabs(win_main),
        "w_f": wf,
        "w_ap": np.ascontiguousarray(np.concatenate([wao, wpo], axis=1)),
        "pool_w": np.ascontiguousarray(pw).reshape(128, 2048),
        "w_out": slabs(np.asarray(inputs["w_out"], f32)[0]),
        "w_up": slabs(np.asarray(inputs["w_ffn_up"], f32)[0]),
        "w_down": slabs(np.asarray(inputs["w_ffn_down"], f32)[0]),
        "cvec": cvec, "rowv": rowv, "consts": consts,
    }
    in_maps = []
    for b in range(B):
        m = dict(shared)
        m["x"] = np.ascontiguousarray(x[b])
        in_maps.append(m)
    res = run_bass_kernel_spmd(nc, in_maps, core_ids=list(range(B)))
    return np.stack([np.asarray(r["out"], f32) for r in res.results], axis=0)
```

```python
import numpy as np
import concourse.bass as bass
import concourse.mybir as mybir
from concourse.bass_utils import run_bass_kernel_spmd

F32 = mybir.dt.float32
BF16 = mybir.dt.bfloat16
AF = mybir.ActivationFunctionType
ALU = mybir.AluOpType

D = 2048
SEQ = 2048
NMETA = 16
L = SEQ + NMETA
DIN = 8200
DFF = 5632
NJ = DFF // 128
EPS = 1e-6
QSCALE = 128 ** -0.5
POOL_WINDOWS = (2, 4, 8, 16)

BLK = [(0, 16)] + [(16 + 128 * i, 128) for i in range(16)]
TILES = [(0, 16)] + [(16 + 512 * i, 512) for i in range(4)]
GROUPS = [dict(start=0, n=528, blocks=list(range(0, 5)), tiles=[(0, 16), (16, 512)])]
for _g in range(1, 4):
    GROUPS.append(dict(start=16 + 512 * _g, n=512, blocks=list(range(1 + 4 * _g, 5 + 4 * _g)),
                       tiles=[(16 + 512 * _g, 512)]))

C_G1, C_BQ, C_BK, C_BV, C_BU, C_BGA, C_BGP, C_PSC, C_G2 = 0, 16, 24, 32, 40, 48, 64, 80, 88
C_CB, C_CW0, C_CW1, C_CW2, C_BF, NCV = 104, 192, 280, 368, 456, 457
K_ID, K_MASK, K_RCNT, K_ONES, K_EPS, NKC = 0, 128, 256, 320, 448, 449


class _Op:
    __slots__ = ("eng", "fn", "deps", "dsem", "ticket", "observed", "idx", "waits")


class Sched:
    def __init__(self, nc):
        self.nc = nc
        self.ops = []
        self.lastw = {}
        self.readers = {}
        self.dma_tot = {}
        self.esem = {}
        self._dsems = {}
        self.bar_op = None
        self.last_on = {}
        self.bg_sems = set()

    def dsem(self, name):
        if name not in self._dsems:
            self._dsems[name] = self.nc.alloc_semaphore("d_" + name)
            self.dma_tot[self._dsems[name]] = 0
        return self._dsems[name]

    def add(self, eng, fn, reads=(), writes=(), dsem=None):
        op = _Op()
        op.eng, op.fn, op.dsem = eng, fn, dsem
        op.idx = len(self.ops)
        op.observed = False
        deps = {}
        for k in reads:
            w = self.lastw.get(k)
            if w is not None:
                deps[w] = deps.get(w, 0) | 1
        for k in writes:
            w = self.lastw.get(k)
            if w is not None:
                deps[w] = deps.get(w, 0) | 1
            for r in self.readers.get(k, ()):
                deps[r] = deps.get(r, 0) | 2
        op.deps = []
        for d, kind in deps.items():
            dop = self.ops[d]
            if dop.dsem is not None:
                op.deps.append(("dma", dop.dsem, self.dma_tot[dop.dsem]))
            else:
                if dop.eng == eng and eng == "pe":
                    continue
                op.deps.append(("eng", d))
        if self.bar_op is not None and eng != "dve":
            op.deps.append(("eng", self.bar_op))
        if dsem is not None:
            self.dma_tot[dsem] += 16
        else:
            self.last_on[eng] = op.idx
        for k in reads:
            self.readers.setdefault(k, []).append(op.idx)
        for k in writes:
            self.lastw[k] = op.idx
            self.readers[k] = []
        self.ops.append(op)
        return op

    def barrier(self, scratch):
        op = _Op()
        op.eng, op.dsem = "dve", None
        op.fn = lambda e: e.memset(scratch, 0.0)
        op.idx = len(self.ops)
        op.observed = False
        op.deps = [("eng", i) for e, i in self.last_on.items() if e != "dve"]
        op.deps += [("dma", s, t) for s, t in self.dma_tot.items() if t > 0 and s not in self.bg_sems]
        self.ops.append(op)
        self.bar_op = op.idx
        self.last_on["dve"] = op.idx
        self.lastw = {k: v for k, v in self.lastw.items() if isinstance(k, tuple) and k[0] == "wsc"}
        self.readers = {}

    def emit(self):
        nc = self.nc
        for o in self.ops:
            for d in o.deps:
                if d[0] == "eng":
                    self.ops[d[1]].observed = True
        cnt = {}
        for o in self.ops:
            if o.dsem is None and o.observed:
                cnt[o.eng] = cnt.get(o.eng, 0) + 1
                o.ticket = cnt[o.eng]
        for e in cnt:
            self.esem[e] = nc.alloc_semaphore("e_" + e)
        for o in self.ops:
            w = {}
            for d in o.deps:
                if d[0] == "dma":
                    sem, val = d[1], d[2]
                else:
                    dop = self.ops[d[1]]
                    sem, val = self.esem[dop.eng], dop.ticket
                if w.get(sem, 0) < val:
                    w[sem] = val
            o.waits = w
        self.stats = dict(cnt)
        with nc.Block() as block:
            for engname, deco in (("sp", block.sync), ("act", block.scalar), ("pe", block.tensor),
                                  ("dve", block.vector), ("pool", block.gpsimd)):
                ops = [o for o in self.ops if o.eng == engname]

                def body(e, ops=ops, engname=engname):
                    waited = {}
                    for o in ops:
                        for sem, val in o.waits.items():
                            if waited.get(sem, 0) < val:
                                e.wait_ge(sem, val)
                                waited[sem] = val
                        ins = o.fn(e)
                        if o.dsem is not None:
                            ins.then_inc(o.dsem, 16)
                        elif o.observed:
                            ins.then_inc(self.esem[o.eng], 1)
                    if engname == "sp":
                        for sem, tot in self.dma_tot.items():
                            if tot > 0:
                                e.wait_ge(sem, tot)

                deco(body)


class Arena:
    def __init__(self, nc):
        self.nc = nc
        self.base = (nc.sbuf_base + 63) // 64 * 64
        self.top = nc.sbuf_top
        self.cur = self.base
        self.n = 0
        self.peak = 0

    def alloc(self, name, shape, dtype):
        esz = 2 if dtype == BF16 else 4
        size = esz
        for s in shape[1:]:
            size *= s
        off = (self.cur + 63) // 64 * 64
        assert off + size <= self.top, f"SBUF overflow allocating {name}: need {off + size - self.top} more bytes"
        self.cur = off + size
        self.peak = max(self.peak, self.cur)
        self.n += 1
        self.last_off = off
        return self.nc.alloc_sbuf_tensor_at(f"{name}_{self.n}", list(shape), dtype, offset=off)

    def mark(self):
        return self.cur

    def reset(self, m):
        self.cur = m


def build_program(stop=None, debug=False):
    nc = bass.Bass("TRN2", target_bir_lowering=False)
    dt = lambda name, shape, kind, dtp=F32: nc.dram_tensor(name, list(shape), dtp, kind=kind).ap()
    x = dt("x", [SEQ, D], "ExternalInput")
    meta = dt("meta", [NMETA, D], "ExternalInput")
    w_in = dt("w_in", [64 * 128, 2048], "ExternalInput")
    w_f = dt("w_f", [128, 128], "ExternalInput")
    w_ap = dt("w_ap", [16 * 128, 2048], "ExternalInput")
    pool_w = dt("pool_w", [128, 2048], "ExternalInput")
    w_out = dt("w_out", [16 * 128, 2048], "ExternalInput")
    w_up = dt("w_up", [88 * 128, 2048], "ExternalInput")
    w_down = dt("w_down", [16 * 128, DFF], "ExternalInput")
    cvec_d = dt("cvec", [128, NCV], "ExternalInput")
    rowv_d = dt("rowv", [2, D], "ExternalInput")
    consts_d = dt("consts", [128, NKC], "ExternalInput")
    out = dt("out", [SEQ, D], "ExternalOutput")
    r1s = dt("r1s", [L, D], "Internal")
    wscB = dt("wscB", [64 * 128, 2048], "Internal", BF16)
    wscC = dt("wscC", [136 * 128, 2048], "Internal", BF16)
    dbg = {}
    if debug:
        dbg["attnT"] = dt("dbg_attnT", [128, 8, L], "ExternalOutput", BF16)
        dbg["poolT"] = dt("dbg_poolT", [128, 8, L], "ExternalOutput", BF16)
        dbg["hT"] = dt("dbg_hT", [128, 16, L], "ExternalOutput", BF16)
        dbg["c8"] = dt("dbg_c8", [8, L], "ExternalOutput")

    S = Sched(nc)
    A = Arena(nc)
    psA = nc.alloc_psum_tensor("psA", [128, 2048], F32)
    psB = nc.alloc_psum_tensor("psB", [128, 2048], F32)

    def bank(i):
        t = psA if i < 4 else psB
        return t[:, (i % 4) * 512:(i % 4 + 1) * 512]

    PS = lambda i: ("ps", i)

    cvec = A.alloc("cvec", [128, NCV], F32)
    consts = A.alloc("consts", [128, NKC], F32)
    ones_bf = A.alloc("ones_bf", [128, 128], BF16)
    stat = A.alloc("stat", [128, 64], F32)
    bar_scr = A.alloc("bar", [128, 8], F32)
    identf = consts[:, K_ID:K_ID + 128]
    maskf = consts[:, K_MASK:K_MASK + 128]
    onesf = consts[:, K_ONES:K_ONES + 128]
    eps_t = consts[:, K_EPS:K_EPS + 1]

    S.add("sp", lambda e: e.dma_start(out=cvec[:], in_=cvec_d[:, :]), writes=["cvec"], dsem=S.dsem("cvec"))
    S.add("sp", lambda e: e.dma_start(out=consts[:], in_=consts_d[:, :]), writes=["consts"], dsem=S.dsem("consts"))
    S.add("dve", lambda e: e.tensor_copy(out=ones_bf[:], in_=onesf), reads=["consts"], writes=["ones_bf"])
    bqs = A.alloc("bqs", [128, 8], F32)
    S.add("dve", lambda e: e.tensor_scalar(out=bqs[:], in0=cvec[:, C_BQ:C_BQ + 8], scalar1=QSCALE, scalar2=None,
                                           op0=ALU.mult), reads=["cvec"], writes=["bqs"])

    def slabsrc(w, j, nk, ncol, k0=0):
        return w[j * 128:(j + 1) * 128, k0 * ncol:(k0 + nk) * ncol]

    bgB = []
    for c in range(16):
        bgB += [slabsrc(w_in, 32 + c, 16, 128), slabsrc(w_in, 48 + c, 16, 128), slabsrc(w_ap, c, 16, 128)]
    for oc in range(16):
        bgB.append(slabsrc(w_out, oc, 16, 128))
    bgC = []
    for j in range(NJ):
        bgC += [slabsrc(w_up, j, 16, 128), slabsrc(w_up, NJ + j, 16, 128)]
    for oc in range(16):
        for (k0, nk) in ((0, 16), (16, 16), (32, 12)):
            bgC.append(slabsrc(w_down, oc, nk, 128, k0))
    if stop not in ("A0", "A"):
        nbg = 0
        for nm, scr, lst in (("B", wscB, bgB), ("C", wscC, bgC)):
            for cid, src in enumerate(lst):
                ncol = src.shape[1]
                sem = S.dsem(f"bg{nbg // 8}")
                S.bg_sems.add(sem)
                nbg += 1
                S.add("pool", lambda e, scr=scr, cid=cid, src=src, ncol=ncol: e.dma_start(
                    out=scr[cid * 128:(cid + 1) * 128, 0:ncol], in_=src), writes=[("wsc", nm, cid)], dsem=sem)

    class Ring:
        def __init__(self, nf, nb):
            self.f = []
            self.f_off = None
            for i in range(nf):
                self.f.append(A.alloc(f"wf{i}", [128, 2048], F32))
                if i == 0:
                    self.f_off = A.last_off
            self.b = [A.alloc(f"wb{i}", [128, 2048], BF16) for i in range(nb)]
            self.fi = 0
            self.bi = 0
            self.ci = 0
            self.cast_engs = ("dve", "act")

    ring = [None]

    def load_slab(parts, cache=None):
        R = ring[0]
        offs = []
        off = 0
        for (ap3, k, w) in parts:
            offs.append(off)
            off += k * w
        bi = R.bi % len(R.b)
        R.bi += 1
        wb = R.b[bi]
        if cache is not None and cache[2] == "load":
            sc, cid = cache[0], cache[1]
            S.add("sp", lambda e, o=off: e.dma_start(out=wb[:, 0:o], in_=sc[cid * 128:(cid + 1) * 128, 0:o]),
                  reads=[("wsc", cache[3], cid)], writes=[("wb", bi)], dsem=S.dsem(f"wbl{bi}"))
            return wb, ("wb", bi), offs
        fi = R.fi % len(R.f)
        R.fi += 1
        wf = R.f[fi]
        for (ap3, k, w), o0 in zip(parts, offs):
            dst = wf[:, o0:o0 + k * w]
            S.add("sp", lambda e, dst=dst, ap3=ap3: e.dma_start(out=dst, in_=ap3),
                  writes=[("wf", fi)], dsem=S.dsem(f"wf{fi}"))
        ceng = R.cast_engs[R.ci % len(R.cast_engs)]
        R.ci += 1
        if ceng == "dve":
            S.add("dve", lambda e, o=off: e.tensor_copy(out=wb[:, 0:o], in_=wf[:, 0:o]),
                  reads=[("wf", fi)], writes=[("wb", bi)])
        else:
            S.add("act", lambda e, o=off: e.activation(out=wb[:, 0:o], in_=wf[:, 0:o], func=AF.Copy),
                  reads=[("wf", fi)], writes=[("wb", bi)])
        if cache is not None and cache[2] == "store":
            sc, cid = cache[0], cache[1]
            S.add("pool", lambda e, o=off: e.dma_start(out=sc[cid * 128:(cid + 1) * 128, 0:o], in_=wb[:, 0:o]),
                  reads=[("wb", bi)], writes=[("wsc", cid)], dsem=S.dsem(f"wst{bi}"))
        return wb, ("wb", bi), offs

    def slabv(w, j, nk, ncol, k0=0):
        return (w[j * 128:(j + 1) * 128, k0 * ncol:(k0 + nk) * ncol], nk, ncol)

    class Stream:
        def __init__(self, reqs, pf):
            self.reqs, self.pf, self.nxt, self.loaded = reqs, pf, 0, {}

        def get(self, i):
            hi = min(i + self.pf, len(self.reqs) - 1)
            while self.nxt <= hi:
                r_ = self.reqs[self.nxt]
                self.loaded[self.nxt] = load_slab(*r_) if isinstance(r_, tuple) else load_slab(r_)
                self.nxt += 1
            return self.loaded.pop(i)

    def rms_rows(src_tile, n, col, key_in, junk, junk_key):
        ss = stat[0:n, col:col + 1]
        rt = stat[0:n, 20 + col:21 + col]
        rs = stat[0:n, 40 + col:41 + col]
        S.add("act", lambda e: e.activation(out=junk[0:n, :], in_=src_tile, func=AF.Square, accum_out=ss),
              reads=[key_in], writes=[junk_key, ("ss", col)])
        S.add("act", lambda e: e.activation(out=rt, in_=ss, func=AF.Sqrt, scale=1.0 / D, bias=eps_t[0:n, :]),
              reads=[("ss", col), "consts"], writes=[("rt", col)])
        S.add("dve", lambda e: e.reciprocal(out=rs, in_=rt), reads=[("rt", col)], writes=[("rs", col)])
        return rs, ("rs", col)

    def to_feature_major(tile, n, tile_key, dstT, dst_key, col0, gcol, bankbase, evac_engs=("dve", "dve")):
        for cg in range(4):
            bk = bankbase + (cg % 2)

            def tr(e, cg=cg, bk=bk):
                ins = None
                for i in range(4):
                    c = cg * 4 + i
                    ins = e.transpose(out=bank(bk)[:, i * 128:i * 128 + n], in_=tile[0:n, c * 128:(c + 1) * 128],
                                      identity=identf[0:n, 0:n])
                return ins

            S.add("pe", tr, reads=[tile_key, "consts"], writes=[PS(bk)])
            for i in range(4):
                c = cg * 4 + i
                eng = evac_engs[i % 2]
                if eng == "dve":
                    S.add("dve", lambda e, c=c, i=i, bk=bk: e.tensor_scalar(
                        out=dstT[:, c, col0:col0 + n], in0=bank(bk)[:, i * 128:i * 128 + n],
                        scalar1=cvec[:, gcol + c:gcol + c + 1], scalar2=None, op0=ALU.mult),
                        reads=[PS(bk), "cvec"], writes=[dst_key])
                else:
                    S.add("act", lambda e, c=c, i=i, bk=bk: e.activation(
                        out=dstT[:, c, col0:col0 + n], in_=bank(bk)[:, i * 128:i * 128 + n], func=AF.Copy,
                        scale=cvec[:, gcol + c:gcol + c + 1]),
                        reads=[PS(bk), "cvec"], writes=[dst_key])

    def proj(wb, wkey, woffs, nk_list, ins_list, in_keys, m, bk, n, col_lists, first=True, last=True):
        def fn(e):
            ins = None
            tot = sum(nk_list)
            cnt = 0
            for wi, nk in enumerate(nk_list):
                for k in range(nk):
                    o = woffs[wi] + k * m
                    ins = e.matmul(bank(bk)[0:m, 0:n], lhsT=wb[:, o:o + m], rhs=ins_list[wi](k),
                                   start=(first and cnt == 0), stop=(last and cnt == tot - 1))
                    cnt += 1
            return ins
        S.add("pe", fn, reads=[wkey] + list(in_keys), writes=[PS(bk)])

    mA = A.mark()
    attnT = A.alloc("attnT", [128, 8, L], BF16)
    poolT = A.alloc("poolT", [128, 8, L], BF16)
    mA1 = A.mark()
    hT = A.alloc("hT", [128, 16, L], BF16)
    ring[0] = Ring(2, 3)
    R = ring[0]
    R.cast_engs = ("act",)
    reqA = [[slabv(w_f, 0, 16, 8)]]
    for h in range(8):
        for c0 in (0, 1024, 2048):
            reqA.append([slabv(w_in, (c0 // 1024) * 8 + h, 16, 128)])
    for c in range(8):
        reqA.append([slabv(w_in, 24 + c, 16, 128)])
    stA = Stream(reqA, 2)
    mA2 = A.mark()
    c8 = A.alloc("c8", [8, L], F32)
    negc = A.alloc("negc", [128, 17, 8], F32)
    mA2b = A.mark()

    for b, (t0, n) in enumerate(BLK):
        xt = R.f[b % 2]
        xk = ("wf", b % 2)
        src = meta[0:16, :] if b == 0 else x[t0 - 16:t0 - 16 + n, :]
        S.add("sp", lambda e, xt=xt, src=src, n=n: e.dma_start(out=xt[0:n, :], in_=src),
              writes=[xk], dsem=S.dsem(f"wf{b % 2}"))
        rs, rsk = rms_rows(xt[0:n, :], n, b, xk, R.b[0], ("wb", 0))
        S.add("act", lambda e, xt=xt, n=n, rs=rs: e.activation(out=xt[0:n, :], in_=xt[0:n, :], func=AF.Copy, scale=rs),
              reads=[xk, rsk], writes=[xk])
        to_feature_major(xt, n, xk, hT, ("hT", b), t0, C_G1, 0)
    HT_ALL = [("hT", b) for b in range(17)]

    def hT_keys(t0, n):
        return [("hT", b) for b, (b0, bn) in enumerate(BLK) if b0 < t0 + n and b0 + bn > t0]

    if debug:
        S.add("sp", lambda e: e.dma_start(out=dbg["hT"][:, :, :], in_=hT[:]), reads=HT_ALL, dsem=S.dsem("dbg"))

    wb, wk, wo = stA.get(0)
    lt = [A.alloc(f"lt{i}", [8, L], F32) for i in range(4)]
    for ti, (t0, n) in enumerate(TILES):
        bk = ti % 2
        proj(wb, wk, wo, [16], [lambda k, t0=t0, n=n: hT[:, k, t0:t0 + n]], hT_keys(t0, n), 8, bk, n, None)
        S.add("dve", lambda e, bk=bk, t0=t0, n=n: e.tensor_scalar(
            out=lt[0][:, t0:t0 + n], in0=bank(bk)[0:8, 0:n], scalar1=cvec[0:8, C_BF:C_BF + 1], scalar2=None,
            op0=ALU.add), reads=[PS(bk), "cvec"], writes=["lt0"])
    tf, ta, tb_, tc = lt
    V = lambda eng, fn, r, w: S.add(eng, fn, reads=r, writes=w)
    V("act", lambda e: e.activation(out=ta[:], in_=tf[:], func=AF.Abs), ["lt0"], ["lt1"])
    V("act", lambda e: e.activation(out=ta[:], in_=ta[:], func=AF.Exp, scale=-1.0), ["lt1"], ["lt1"])
    V("dve", lambda e: e.tensor_scalar(out=tb_[:], in0=ta[:], scalar1=2.0, scalar2=None, op0=ALU.add), ["lt1"], ["lt2"])
    V("dve", lambda e: e.reciprocal(out=tb_[:], in_=tb_[:]), ["lt2"], ["lt2"])
    V("dve", lambda e: e.tensor_tensor(out=ta[:], in0=ta[:], in1=tb_[:], op=ALU.mult), ["lt1", "lt2"], ["lt1"])
    V("dve", lambda e: e.tensor_tensor(out=tb_[:], in0=ta[:], in1=ta[:], op=ALU.mult), ["lt1"], ["lt2"])
    V("dve", lambda e: e.tensor_scalar(out=tc[:], in0=tb_[:], scalar1=1.0 / 9.0, scalar2=None, op0=ALU.mult), ["lt2"], ["lt3"])
    for cst in (1.0 / 7.0, 1.0 / 5.0, 1.0 / 3.0):
        V("dve", lambda e, cst=cst: e.scalar_tensor_tensor(out=tc[:], in0=tc[:], scalar=cst, in1=tb_[:],
                                                           op0=ALU.add, op1=ALU.mult), ["lt3", "lt2"], ["lt3"])
    V("dve", lambda e: e.scalar_tensor_tensor(out=tc[:], in0=tc[:], scalar=1.0, in1=ta[:], op0=ALU.add, op1=ALU.mult),
      ["lt3", "lt1"], ["lt3"])
    V("dve", lambda e: e.tensor_scalar(out=ta[:], in0=tf[:], scalar1=0.0, scalar2=None, op0=ALU.min), ["lt0", "lt1"], ["lt1"])
    V("dve", lambda e: e.scalar_tensor_tensor(out=tb_[:], in0=tc[:], scalar=-2.0, in1=ta[:], op0=ALU.mult, op1=ALU.add),
      ["lt3", "lt1", "lt2"], ["lt2"])
    V("dve", lambda e: e.memset(tc[:], 1.0), ["lt3"], ["lt3"])
    V("dve", lambda e: e.tensor_tensor_scan(out=c8[:], data0=tc[:], data1=tb_[:], initial=0.0, op0=ALU.mult, op1=ALU.add),
      ["lt3", "lt2"], ["c8"])
    for b, (t0, n) in enumerate(BLK):
        bk = b % 2
        S.add("pe", lambda e, bk=bk, t0=t0, n=n: e.transpose(out=bank(bk)[0:n, 0:8], in_=c8[0:8, t0:t0 + n],
                                                             identity=identf[0:8, 0:8]),
              reads=["c8", "consts"], writes=[PS(bk)])
        S.add("dve", lambda e, bk=bk, b=b, n=n: e.tensor_scalar(out=negc[0:n, b, :], in0=bank(bk)[0:n, 0:8], scalar1=-1.0,
                                                                scalar2=None, op0=ALU.mult),
              reads=[PS(bk)], writes=["negc"])
    if debug:
        S.add("sp", lambda e: e.dma_start(out=dbg["c8"][:, :], in_=c8[:]), reads=["c8"], dsem=S.dsem("dbg"))
    S.barrier(bar_scr[0:1, 0:1])
    A.reset(mA2b)

    if stop != "A0":
        qT = A.alloc("qT", [128, L], BF16)
        kT = A.alloc("kT", [128, L], BF16)
        vTf = A.alloc("vTf", [128, L], F32)
        Vtm = A.alloc("Vtm", [128, 17, 128], BF16)
        cq = [A.alloc(f"cq{i}", [128, 512], F32) for i in range(2)]
        c8h = A.alloc("c8h", [8, 512], F32)
        PT = [A.alloc(f"PT_{i}", [128, 512], BF16) for i in range(5)]
        SBANK = (2, 3, 6, 5, 0)
        NSD = len(SBANK)
        rden = A.alloc("rden", [128, 512], F32)
        for h in range(8):
            for which, c0, dstname in (("q", 0, "qT"), ("k", 1024, "kT"), ("v", 2048, "vTf")):
                wb, wk, wo = stA.get(1 + h * 3 + (c0 // 1024))
                for ti, (t0, n) in enumerate(TILES):
                    bk = ti % 2
                    proj(wb, wk, wo, [16], [lambda k, t0=t0, n=n: hT[:, k, t0:t0 + n]], hT_keys(t0, n), 128, bk, n, None)
                    if which == "q":
                        S.add("act", lambda e, bk=bk, t0=t0, n=n, h=h: e.activation(
                            out=qT[:, t0:t0 + n], in_=bank(bk)[:, 0:n], func=AF.Identity, scale=QSCALE,
                            bias=bqs[:, h:h + 1]), reads=[PS(bk), "bqs"], writes=[("qT", ti)])
                    elif which == "k":
                        S.add("act", lambda e, bk=bk, t0=t0, n=n, h=h: e.activation(
                            out=kT[:, t0:t0 + n], in_=bank(bk)[:, 0:n], func=AF.Identity,
                            bias=cvec[:, C_BK + h:C_BK + h + 1]), reads=[PS(bk), "cvec"], writes=[("kT", ti)])
                    else:
                        S.add("act", lambda e, bk=bk, t0=t0, n=n, h=h: e.activation(
                            out=vTf[:, t0:t0 + n], in_=bank(bk)[:, 0:n], func=AF.Identity,
                            bias=cvec[:, C_BV + h:C_BV + h + 1]), reads=[PS(bk), "cvec"], writes=[("vTf", ti)])
            for b, (t0, n) in enumerate(BLK):
                bk = b % 2
                ti = 0 if b == 0 else 1 + (b - 1) // 4
                S.add("pe", lambda e, bk=bk, t0=t0, n=n: e.transpose(out=bank(bk)[0:n, 0:128], in_=vTf[:, t0:t0 + n],
                                                                     identity=identf),
                      reads=[("vTf", ti), "consts"], writes=[PS(bk)])
                S.add("act", lambda e, bk=bk, b=b, n=n: e.activation(out=Vtm[0:n, b, :], in_=bank(bk)[0:n, 0:128], func=AF.Copy),
                      reads=[PS(bk)], writes=[("Vtm", b)])
            blocks = []
            for ti, (t0, n) in enumerate(TILES):
                kbs = [(b, k0, kn) for b, (k0, kn) in enumerate(BLK) if k0 < t0 + n]
                for bi_, (b, k0, kn) in enumerate(kbs):
                    qlo = max(t0, k0)
                    blocks.append(dict(ti=ti, t0=t0, n=n, b=b, k0=k0, kn=kn, qlo=qlo, N=t0 + n - qlo, off=qlo - t0,
                                       diag=k0 >= t0, first=bi_ == 0, last=bi_ == len(kbs) - 1,
                                       kti=0 if b == 0 else 1 + (b - 1) // 4))

            def emit_cq(ti, h=h):
                t0, n = TILES[ti]
                cqt = cq[ti % 2]
                S.add("dve", lambda e, t0=t0, n=n, h=h: e.tensor_scalar(
                    out=c8h[:, 0:n], in0=c8[:, t0:t0 + n], scalar1=identf[0:8, h:h + 1], scalar2=None, op0=ALU.mult),
                    reads=["c8", "consts"], writes=["c8h"])
                S.add("pe", lambda e, n=n, ti=ti: e.matmul(bank(1)[:, 0:n], lhsT=onesf[0:8, :], rhs=c8h[:, 0:n],
                                                           start=True, stop=True),
                      reads=["c8h", "consts"], writes=[PS(1)])
                S.add("act", lambda e, n=n, ti=ti, cqt=cqt: e.activation(out=cqt[:, 0:n], in_=bank(1)[:, 0:n], func=AF.Copy),
                      reads=[PS(1)], writes=[("cq", ti % 2)])

            def emit_S(idx):
                B_ = blocks[idx]
                sb = SBANK[idx % NSD]
                S.add("pe", lambda e, sb=sb, k0=B_["k0"], kn=B_["kn"], qlo=B_["qlo"], N=B_["N"]: e.matmul(
                    bank(sb)[0:kn, 0:N], lhsT=kT[:, k0:k0 + kn], rhs=qT[:, qlo:qlo + N], start=True, stop=True),
                    reads=[("kT", B_["kti"]), ("qT", B_["ti"])], writes=[PS(sb)])

            emit_cq(0)
            for _i in range(min(NSD, len(blocks))):
                emit_S(_i)
            for idx, B_ in enumerate(blocks):
                ti, t0, n, b, kn, N, off = B_["ti"], B_["t0"], B_["n"], B_["b"], B_["kn"], B_["N"], B_["off"]
                ob, db = 4, 7
                sb = SBANK[idx % NSD]
                tb = idx % NSD
                cqt = cq[ti % 2]
                if B_["first"] and ti + 1 < len(TILES):
                    emit_cq(ti + 1)
                S.add("dve", lambda e, sb=sb, kn=kn, N=N, off=off, cqt=cqt: e.tensor_tensor(
                    out=bank(sb)[0:kn, 0:N], in0=bank(sb)[0:kn, 0:N], in1=cqt[0:kn, off:off + N], op=ALU.add),
                    reads=[PS(sb), ("cq", ti % 2)], writes=[PS(sb)])
                if B_["diag"]:
                    S.add("dve", lambda e, sb=sb, kn=kn: e.tensor_tensor(
                        out=bank(sb)[0:kn, 0:kn], in0=bank(sb)[0:kn, 0:kn], in1=maskf[0:kn, 0:kn], op=ALU.add),
                        reads=[PS(sb), "consts"], writes=[PS(sb)])
                S.add("act", lambda e, sb=sb, tb=tb, kn=kn, N=N, b=b, h=h: e.activation(
                    out=PT[tb][0:kn, 0:N], in_=bank(sb)[0:kn, 0:N], func=AF.Exp, bias=negc[0:kn, b, h:h + 1]),
                    reads=[PS(sb), "negc"], writes=[("PT", tb)])

                def pv(e, tb=tb, kn=kn, N=N, off=off, b=b, ob=ob, db=db, first=B_["first"], lastb=B_["last"]):
                    e.matmul(bank(ob)[:, off:off + N], lhsT=Vtm[0:kn, b, :], rhs=PT[tb][0:kn, 0:N],
                             start=first, stop=lastb)
                    return e.matmul(bank(db)[:, off:off + N], lhsT=ones_bf[0:kn, :], rhs=PT[tb][0:kn, 0:N],
                                    start=first, stop=lastb)
                S.add("pe", pv, reads=[("PT", tb), ("Vtm", b), "ones_bf"], writes=[PS(ob), PS(db)])
                if idx + NSD < len(blocks):
                    emit_S(idx + NSD)
                if B_["last"]:
                    S.add("dve", lambda e, db=db, n=n: e.reciprocal(out=rden[:, 0:n], in_=bank(db)[:, 0:n]),
                          reads=[PS(db)], writes=["rden"])
                    S.add("dve", lambda e, ob=ob, t0=t0, n=n, h=h: e.tensor_tensor(
                        out=attnT[:, h, t0:t0 + n], in0=bank(ob)[:, 0:n], in1=rden[:, 0:n], op=ALU.mult),
                        reads=[PS(ob), "rden"], writes=[("attnT", ti)])
        S.barrier(bar_scr[0:1, 0:1])
        A.reset(mA2)

        ub = [A.alloc(f"ub{i}", [128, 16 + L], F32) for i in range(2)]
        tA = A.alloc("tA", [128, 16 + L], F32)
        tB = A.alloc("tB", [128, 16 + L], F32)
        dT = A.alloc("dT", [128, 2, L], BF16)
        t16 = A.alloc("t16", [128, 16], F32)
        for i, tt in enumerate((ub[0], ub[1], tA, tB)):
            S.add("pool", lambda e, tt=tt: e.memset(tt[:, 0:16], 0.0), writes=[("pad", i)])
        pwb = A.alloc("pwb", [128, 2048], BF16)
        pwk, pwo = "pwb", [0]
        _fi = R.fi % len(R.f)
        R.fi += 1
        S.add("sp", lambda e: e.dma_start(out=R.f[_fi][:, :], in_=pool_w[:, :]),
              writes=[("wf", _fi)], dsem=S.dsem(f"wf{_fi}"))
        S.add("pool", lambda e: e.tensor_copy(out=pwb[:], in_=R.f[_fi][:]), reads=[("wf", _fi)], writes=["pwb"])
        for c in range(8):
            g = c // 2
            w = POOL_WINDOWS[g]
            u = ub[c % 2]
            uk = ("ub", c % 2)
            wb, wk, wo = stA.get(25 + c)
            for ti, (t0, n) in enumerate(TILES):
                bk = ti % 2
                proj(wb, wk, wo, [16], [lambda k, t0=t0, n=n: hT[:, k, t0:t0 + n]], hT_keys(t0, n), 128, bk, n, None)
                S.add("act", lambda e, bk=bk, t0=t0, n=n, c=c, u=u: e.activation(
                    out=u[:, 16 + t0:16 + t0 + n], in_=bank(bk)[:, 0:n], func=AF.Identity,
                    bias=cvec[:, C_BU + c:C_BU + c + 1]), reads=[PS(bk), "cvec", ("pad", c % 2)], writes=[uk])
            src, srck, srcpad = u, uk, ("pad", c % 2)
            sh = 1
            pp = [(tA, "tA", ("pad", 2)), (tB, "tB", ("pad", 3))]
            pi = 0
            while sh < w:
                dst, dk, dpad = pp[pi % 2]
                S.add("dve", lambda e, dst=dst, src=src, sh=sh: e.tensor_tensor(
                    out=dst[:, 16:16 + L], in0=src[:, 16:16 + L], in1=src[:, 16 - sh:16 - sh + L], op=ALU.add),
                    reads=[srck, srcpad], writes=[dk])
                src, srck, srcpad = dst, dk, dpad
                sh *= 2
                pi += 1
            wi = POOL_WINDOWS.index(w)
            S.add("dve", lambda e, src=src, u=u, w=w, c=c: e.scalar_tensor_tensor(
                out=dT[:, c % 2, 16:L], in0=src[:, 32:16 + L], scalar=1.0 / w, in1=u[:, 32:16 + L],
                op0=ALU.mult, op1=ALU.subtract), reads=[srck, uk], writes=[("dT", c % 2)])
            S.add("dve", lambda e, src=src, wi=wi: e.tensor_tensor(
                out=t16[:], in0=src[:, 16:32], in1=consts[:, K_RCNT + wi * 16:K_RCNT + wi * 16 + 16], op=ALU.mult),
                reads=[srck, "consts"], writes=["t16"])
            S.add("dve", lambda e, u=u, c=c: e.tensor_tensor(
                out=dT[:, c % 2, 0:16], in0=t16[:], in1=u[:, 16:32], op=ALU.subtract),
                reads=["t16", uk], writes=[("dT", c % 2)])
            if c % 2 == 1:
                for ocl in range(2):
                    oc = 2 * g + ocl
                    for ti, (t0, n) in enumerate(TILES):
                        bk = ti % 2

                        def fn(e, g=g, ocl=ocl, bk=bk, t0=t0, n=n):
                            ins = None
                            for kl in range(2):
                                o = pwo[0] + (g * 2 + kl) * 256 + ocl * 128
                                ins = e.matmul(bank(bk)[:, 0:n], lhsT=pwb[:, o:o + 128], rhs=dT[:, kl, t0:t0 + n],
                                               start=(kl == 0), stop=(kl == 1))
                            return ins
                        S.add("pe", fn, reads=[pwk, ("dT", 0), ("dT", 1)], writes=[PS(bk)])
                        S.add("act", lambda e, bk=bk, oc=oc, t0=t0, n=n: e.activation(
                            out=poolT[:, oc, t0:t0 + n], in_=bank(bk)[:, 0:n], func=AF.Copy,
                            scale=cvec[:, C_PSC + oc:C_PSC + oc + 1]), reads=[PS(bk), "cvec"], writes=[("poolT", ti)])
        if debug:
            S.add("sp", lambda e: e.dma_start(out=dbg["attnT"][:, :, :], in_=attnT[:]),
                  reads=[("attnT", i) for i in range(5)], dsem=S.dsem("dbg"))
            S.add("sp", lambda e: e.dma_start(out=dbg["poolT"][:, :, :], in_=poolT[:]),
                  reads=[("poolT", i) for i in range(5)], dsem=S.dsem("dbg"))
        S.barrier(bar_scr[0:1, 0:1])
    A.reset(mA1)

    if stop not in ("A0", "A"):
        gpost = A.alloc("gpost", [128, D], F32)
        S.add("sp", lambda e: e.dma_start(out=gpost[:], in_=rowv_d[0:1, :].partition_broadcast(128)),
              writes=["gpost"], dsem=S.dsem("gpost"))
        xtB = A.alloc("xtB", [128, D], F32)
        junk = A.alloc("junk", [128, D], BF16)
        r1t = [A.alloc(f"r1t{i}", [128, D], F32) for i in range(2)]
        mTg = A.alloc("mTg", [128, 16, 528], BF16)
        ring[0] = Ring(0, 5)
        reqB = []
        for _gi in range(4):
            for c in range(16):
                reqB.append(([slabv(w_in, 32 + c, 16, 128)], (wscB, c * 3, "load", "B")))
                reqB.append(([slabv(w_in, 48 + c, 16, 128)], (wscB, c * 3 + 1, "load", "B")))
                reqB.append(([slabv(w_ap, c, 16, 128)], (wscB, c * 3 + 2, "load", "B")))
            for oc in range(16):
                reqB.append(([slabv(w_out, oc, 16, 128)], (wscB, 48 + oc, "load", "B")))
        stB = Stream(reqB, 4)
        mixT = A.alloc("mixT", [128, 16, 528], F32)
        hTg = A.alloc("hTg", [128, 16, 528], BF16)
        gt = [[A.alloc(f"gt{i}_{j}", [128, 512], F32) for j in range(4)] for i in range(2)]
        rot = [(xtB, "xtB", "xtB"), (r1t[0], ("r1t", 0), "r1st0"), (r1t[1], ("r1t", 1), "r1st1")]

        def B0_block(gi, bi0, bankbase):
            G = GROUPS[gi]
            b = G["blocks"][bi0]
            t0, n = BLK[b]
            xt_, xk_, xs_ = rot[bi0 % 3]
            src = meta[0:16, :] if b == 0 else x[t0 - 16:t0 - 16 + n, :]
            S.add("sp", lambda e, src=src, n=n, xt_=xt_: e.dma_start(out=xt_[0:n, :], in_=src), writes=[xk_],
                  dsem=S.dsem(xs_))
            rs, rsk = rms_rows(xt_[0:n, :], n, b, xk_, junk, "junk")
            S.add("act", lambda e, n=n, rs=rs, xt_=xt_: e.activation(out=xt_[0:n, :], in_=xt_[0:n, :], func=AF.Copy, scale=rs),
                  reads=[xk_, rsk], writes=[xk_])
            to_feature_major(xt_, n, xk_, hTg, ("hTg", bi0), t0 - G["start"], C_G1, bankbase)

        def B1(gi):
            G = GROUPS[gi]
            gs = G["start"]
            hk = [("hTg", i) for i in range(len(G["blocks"]))]
            it = 0
            for c in range(16):
                tl = []
                for (t0, n) in G["tiles"]:
                    tl.append((t0, n, t0 - gs, 4 * (it % 2), it % 2, TILES.index((t0, n))))
                    it += 1
                wga = stB.get(gi * 64 + c * 3)
                for (t0, n, lo, pb, par, ti) in tl:
                    proj(wga[0], wga[1], wga[2], [16], [lambda k, lo=lo, n=n: hTg[:, k, lo:lo + n]], hk, 128, pb + 0, n, None)
                wgp = stB.get(gi * 64 + c * 3 + 1)
                for (t0, n, lo, pb, par, ti) in tl:
                    proj(wgp[0], wgp[1], wgp[2], [16], [lambda k, lo=lo, n=n: hTg[:, k, lo:lo + n]], hk, 128, pb + 1, n, None)
                wap = stB.get(gi * 64 + c * 3 + 2)
                for (t0, n, lo, pb, par, ti) in tl:
                    proj(wap[0], wap[1], [wap[2][0]], [8], [lambda k, t0=t0, n=n: attnT[:, k, t0:t0 + n]],
                         [("attnT", ti)], 128, pb + 2, n, None)
                    proj(wap[0], wap[1], [wap[2][0] + 1024], [8], [lambda k, t0=t0, n=n: poolT[:, k, t0:t0 + n]],
                         [("poolT", ti)], 128, pb + 3, n, None)
                for (t0, n, lo, pb, par, ti) in tl:
                    g4 = gt[par]
                    S.add("act", lambda e, pb=pb, g4=g4, n=n, c=c: e.activation(
                        out=g4[0][:, 0:n], in_=bank(pb)[:, 0:n], func=AF.Sigmoid, bias=cvec[:, C_BGA + c:C_BGA + c + 1]),
                        reads=[PS(pb), "cvec"], writes=[("gt", par, 0)])
                    S.add("act", lambda e, pb=pb, g4=g4, n=n, c=c: e.activation(
                        out=g4[1][:, 0:n], in_=bank(pb + 1)[:, 0:n], func=AF.Sigmoid, bias=cvec[:, C_BGP + c:C_BGP + c + 1]),
                        reads=[PS(pb + 1), "cvec"], writes=[("gt", par, 1)])
                    S.add("dve", lambda e, pb=pb, g4=g4, n=n: e.tensor_tensor(
                        out=g4[2][:, 0:n], in0=g4[0][:, 0:n], in1=bank(pb + 2)[:, 0:n], op=ALU.mult),
                        reads=[PS(pb + 2), ("gt", par, 0)], writes=[("gt", par, 2)])
                    S.add("dve", lambda e, pb=pb, g4=g4, n=n: e.tensor_tensor(
                        out=g4[3][:, 0:n], in0=g4[1][:, 0:n], in1=bank(pb + 3)[:, 0:n], op=ALU.mult),
                        reads=[PS(pb + 3), ("gt", par, 1)], writes=[("gt", par, 3)])
                    S.add("pool", lambda e, g4=g4, n=n, c=c, lo=lo: e.tensor_tensor(
                        out=mTg[:, c, lo:lo + n], in0=g4[2][:, 0:n], in1=g4[3][:, 0:n], op=ALU.add),
                        reads=[("gt", par, 2), ("gt", par, 3)], writes=[("mTg", c)])

        def B2_iter(gi, oc, itc):
            G = GROUPS[gi]
            gs = G["start"]
            mx = mixT
            mk = [("mTg", c) for c in range(16)]
            wo_ = stB.get(gi * 64 + 48 + oc)
            for (t0, n) in G["tiles"]:
                lo = t0 - gs
                bk = itc[0] % 2
                itc[0] += 1
                proj(wo_[0], wo_[1], wo_[2], [16], [lambda k, lo=lo, n=n: mTg[:, k, lo:lo + n]], mk, 128, bk, n, None)
                S.add("act", lambda e, bk=bk, oc=oc, lo=lo, n=n, mx=mx: e.activation(
                    out=mx[:, oc, lo:lo + n], in_=bank(bk)[:, 0:n], func=AF.Copy),
                    reads=[PS(bk)], writes=[("mixT", oc)])

        def B3(gi):
            G = GROUPS[gi]
            gs = G["start"]
            mx = mixT
            xk_ = [("mixT", oc) for oc in range(16)]
            for bi_, b in enumerate(G["blocks"]):
                t0, n = BLK[b]
                lo = t0 - gs
                half = 4 * (bi_ % 2)
                pst = psA if half == 0 else psB
                for cg in range(4):
                    def tr(e, cg=cg, lo=lo, n=n, half=half, mx=mx):
                        ins = None
                        for i in range(4):
                            oc = cg * 4 + i
                            ins = e.transpose(out=bank(half + cg)[0:n, i * 128:(i + 1) * 128], in_=mx[:, oc, lo:lo + n],
                                              identity=identf)
                        return ins
                    S.add("pe", tr, reads=xk_ + ["consts"], writes=[PS(half + cg)])
                pkeys = [PS(half + i) for i in range(4)]
                rt_ = r1t[bi_ % 2]
                rk = ("r1t", bi_ % 2)
                src = meta[0:16, :] if b == 0 else x[t0 - 16:t0 - 16 + n, :]
                S.add("sp", lambda e, src=src, n=n: e.dma_start(out=xtB[0:n, :], in_=src), writes=["xtB"], dsem=S.dsem("xtB"))
                ss = stat[0:n, b:b + 1]
                S.add("act", lambda e, pst=pst, n=n, ss=ss: e.activation(out=junk[0:n, :], in_=pst[0:n, :], func=AF.Square,
                                                                         accum_out=ss),
                      reads=pkeys, writes=["junk", ("ss", b)])
                rtt = stat[0:n, 20 + b:21 + b]
                rs = stat[0:n, 40 + b:41 + b]
                S.add("act", lambda e, rtt=rtt, ss=ss, n=n: e.activation(out=rtt, in_=ss, func=AF.Sqrt, scale=1.0 / D,
                                                                         bias=eps_t[0:n, :]),
                      reads=[("ss", b), "consts"], writes=[("rt", b)])
                S.add("dve", lambda e, rs=rs, rtt=rtt: e.reciprocal(out=rs, in_=rtt), reads=[("rt", b)], writes=[("rs", b)])
                S.add("dve", lambda e, pst=pst, n=n, rs=rs, rt_=rt_: e.scalar_tensor_tensor(
                    out=rt_[0:n, :], in0=pst[0:n, :], scalar=rs, in1=gpost[0:n, :], op0=ALU.mult, op1=ALU.mult),
                    reads=pkeys + [("rs", b), "gpost"], writes=[rk])
                S.add("pool", lambda e, n=n, rt_=rt_: e.tensor_tensor(out=rt_[0:n, :], in0=rt_[0:n, :], in1=xtB[0:n, :],
                                                                       op=ALU.add),
                      reads=[rk, "xtB"], writes=[rk])
                S.add("sp", lambda e, n=n, rt_=rt_, t0=t0: e.dma_start(out=r1s[t0:t0 + n, :], in_=rt_[0:n, :]),
                      reads=[rk], writes=[("r1s", b)], dsem=S.dsem(f"r1st{bi_ % 2}"))

        itc = [0]
        for bi0 in range(len(GROUPS[0]["blocks"])):
            B0_block(0, bi0, 0)
        for gi in range(0, 4):
            B1(gi)
            b0_at = {2: 0, 5: 1, 8: 2, 11: 3} if gi < 3 else {}
            for oc in range(16):
                B2_iter(gi, oc, itc)
                if oc in b0_at:
                    B0_block(gi + 1, b0_at[oc], 2)
            B3(gi)
        S.barrier(bar_scr[0:1, 0:1])
    A.reset(mA)

    if stop not in ("A0", "A", "B"):
        gpost2 = A.alloc("gpost2", [128, D], F32)
        S.add("sp", lambda e: e.dma_start(out=gpost2[:], in_=rowv_d[1:2, :].partition_broadcast(128)),
              writes=["gpost2"], dsem=S.dsem("gpost2"))
        junkC = A.alloc("junkC", [128, D], BF16)
        r1c = [A.alloc(f"r1c{i}", [128, D], F32) for i in range(2)]
        ot = [A.alloc(f"ot{i}", [128, D], F32) for i in range(2)]
        carry = A.alloc("carry", [128, 88, 2], F32)
        S.add("pool", lambda e: e.memset(carry[:], 0.0), writes=["carry"])
        h2T = A.alloc("h2T", [128, 16, 528], BF16)
        actT = A.alloc("actT", [128, NJ, 528], BF16)
        ffT = A.alloc("ffT", [128, 16, 512], F32)
        upb = [[A.alloc(f"up{i}_{j}", [128, 530], F32) for j in range(2)] for i in range(2)]
        cv = [[A.alloc(f"cv{i}_{j}", [128, 528], F32) for j in range(3)] for i in range(2)]
        ring[0] = Ring(0, 6)
        reqC = []
        for _gi in range(4):
            for j in range(NJ):
                reqC.append(([slabv(w_up, j, 16, 128)], (wscC, j * 2, "load", "C")))
                reqC.append(([slabv(w_up, NJ + j, 16, 128)], (wscC, j * 2 + 1, "load", "C")))
            for oc in range(16):
                for pi_, (k0, nk) in enumerate(((0, 16), (16, 16), (32, 12))):
                    reqC.append(([slabv(w_down, oc, nk, 128, k0)], (wscC, 88 + oc * 3 + pi_, "load", "C")))
        stC = Stream(reqC, 5)

        def C0_front(gi, bi_):
            G = GROUPS[gi]
            b = G["blocks"][bi_]
            t0, n = BLK[b]
            rc = r1c[bi_ % 2]
            rck = ("r1c", bi_ % 2)
            S.add("sp", lambda e, rc=rc, t0=t0, n=n: e.dma_start(out=rc[0:n, :], in_=r1s[t0:t0 + n, :]),
                  reads=[("r1s", b)], writes=[rck], dsem=S.dsem(f"r1c{bi_ % 2}"))
            rs, rsk = rms_rows(rc[0:n, :], n, b, rck, junkC, "junkC")
            S.add("act", lambda e, rc=rc, n=n, rs=rs: e.activation(out=rc[0:n, :], in_=rc[0:n, :], func=AF.Copy, scale=rs),
                  reads=[rck, rsk], writes=[rck])

        def C0_back(gi, bi_, bankbase):
            G = GROUPS[gi]
            b = G["blocks"][bi_]
            t0, n = BLK[b]
            to_feature_major(r1c[bi_ % 2], n, ("r1c", bi_ % 2), h2T, ("h2T", bi_), t0 - G["start"], C_G2, bankbase)

        def C0_block(gi, bi_, bankbase):
            C0_front(gi, bi_)
            C0_back(gi, bi_, bankbase)

        def C1_iter(gi, j):
            G = GROUPS[gi]
            gs, gn = G["start"], G["n"]
            hk = [("h2T", i) for i in range(len(G["blocks"]))]
            s2 = j % 2
            for half_, jj in ((0, j), (1, NJ + j)):
                wu = stC.get(gi * 136 + j * 2 + half_)
                ub_ = upb[s2][half_]
                ubk = ("upb", s2, half_)
                S.add("pool", lambda e, ub_=ub_, jj=jj: e.tensor_copy(out=ub_[:, 0:2], in_=carry[:, jj, :]),
                      reads=["carry"], writes=[ubk])
                for tix, (t0, n) in enumerate(G["tiles"]):
                    lo = t0 - gs
                    bk = 2 * half_ + (j + tix) % 2
                    proj(wu[0], wu[1], wu[2], [16], [lambda k, lo=lo, n=n: h2T[:, k, lo:lo + n]], hk, 128, bk, n, None)
                    S.add("act", lambda e, bk=bk, ub_=ub_, lo=lo, n=n: e.activation(
                        out=ub_[:, 2 + lo:2 + lo + n], in_=bank(bk)[:, 0:n], func=AF.Copy),
                        reads=[PS(bk)], writes=[ubk])
                S.add("pool", lambda e, ub_=ub_, jj=jj, gn=gn: e.tensor_copy(out=carry[:, jj, :], in_=ub_[:, gn:gn + 2]),
                      reads=[ubk], writes=["carry"])
                tg = cv[s2][half_]
                tgk = ("cv", s2, half_)
                S.add("dve", lambda e, tg=tg, ub_=ub_, jj=jj, gn=gn: e.tensor_scalar(
                    out=tg[:, 0:gn], in0=ub_[:, 0:gn], scalar1=cvec[:, C_CW0 + jj:C_CW0 + jj + 1],
                    scalar2=cvec[:, C_CB + jj:C_CB + jj + 1], op0=ALU.mult, op1=ALU.add),
                    reads=[ubk, "cvec"], writes=[tgk])
                S.add("dve", lambda e, tg=tg, ub_=ub_, jj=jj, gn=gn: e.scalar_tensor_tensor(
                    out=tg[:, 0:gn], in0=ub_[:, 1:gn + 1], scalar=cvec[:, C_CW1 + jj:C_CW1 + jj + 1], in1=tg[:, 0:gn],
                    op0=ALU.mult, op1=ALU.add), reads=[ubk, "cvec", tgk], writes=[tgk])
                S.add("dve", lambda e, tg=tg, ub_=ub_, jj=jj, gn=gn: e.scalar_tensor_tensor(
                    out=tg[:, 0:gn], in0=ub_[:, 2:gn + 2], scalar=cvec[:, C_CW2 + jj:C_CW2 + jj + 1], in1=tg[:, 0:gn],
                    op0=ALU.mult, op1=ALU.add), reads=[ubk, "cvec", tgk], writes=[tgk])
            gl = cv[s2][2]
            S.add("act", lambda e, gl=gl, s2=s2, gn=gn: e.activation(out=gl[:, 0:gn], in_=cv[s2][0][:, 0:gn],
                                                                      func=AF.Gelu_apprx_tanh),
                  reads=[("cv", s2, 0)], writes=[("cv", s2, 2)])
            S.add("pool", lambda e, gl=gl, s2=s2, gn=gn, j=j: e.tensor_tensor(
                out=actT[:, j, 0:gn], in0=gl[:, 0:gn], in1=cv[s2][1][:, 0:gn], op=ALU.mult),
                reads=[("cv", s2, 2), ("cv", s2, 1)], writes=[("actT", j)])

        def C2_iter(gi, oc):
            G = GROUPS[gi]
            t0r, nr = G["tiles"][-1]
            lo = t0r - G["start"]
            ak = [("actT", j) for j in range(NJ)]
            bk = oc % 2
            for pi, (k0, nk) in enumerate(((0, 16), (16, 16), (32, 12))):
                wd = stC.get(gi * 136 + 88 + oc * 3 + pi)
                proj(wd[0], wd[1], wd[2], [nk], [lambda k, k0=k0, lo=lo, nr=nr: actT[:, k0 + k, lo:lo + nr]],
                     ak, 128, bk, nr, None, first=(pi == 0), last=(pi == 2))
            S.add("act", lambda e, bk=bk, oc=oc, nr=nr: e.activation(out=ffT[:, oc, 0:nr], in_=bank(bk)[:, 0:nr], func=AF.Copy),
                  reads=[PS(bk)], writes=[("ffT", oc)])

        def C3_block(gi, bi_):
            G = GROUPS[gi]
            t0r, nr = G["tiles"][-1]
            rblocks = [b for b in G["blocks"] if b > 0]
            b = rblocks[bi_]
            t0, n = BLK[b]
            lo2 = t0 - t0r
            fk = [("ffT", oc) for oc in range(16)]
            half = 4
            pst = psB
            for cg in range(4):
                def tr(e, cg=cg, lo2=lo2, n=n, half=half):
                    ins = None
                    for i in range(4):
                        oc = cg * 4 + i
                        ins = e.transpose(out=bank(half + cg)[0:n, i * 128:(i + 1) * 128], in_=ffT[:, oc, lo2:lo2 + n],
                                          identity=identf)
                    return ins
                S.add("pe", tr, reads=fk + ["consts"], writes=[PS(half + cg)])
            pkeys = [PS(half + i) for i in range(4)]
            rc = r1c[bi_ % 2]
            rck = ("r1c", bi_ % 2)
            S.add("sp", lambda e, rc=rc, t0=t0, n=n: e.dma_start(out=rc[0:n, :], in_=r1s[t0:t0 + n, :]),
                  reads=[("r1s", b)], writes=[rck], dsem=S.dsem(f"r1c{bi_ % 2}"))
            ss = stat[0:n, b:b + 1]
            S.add("act", lambda e, pst=pst, n=n, ss=ss: e.activation(out=junkC[0:n, :], in_=pst[0:n, :], func=AF.Square,
                                                                     accum_out=ss),
                  reads=pkeys, writes=["junkC", ("ss", b)])
            rtt = stat[0:n, 20 + b:21 + b]
            rs = stat[0:n, 40 + b:41 + b]
            S.add("act", lambda e, rtt=rtt, ss=ss, n=n: e.activation(out=rtt, in_=ss, func=AF.Sqrt, scale=1.0 / D,
                                                                     bias=eps_t[0:n, :]),
                  reads=[("ss", b), "consts"], writes=[("rt", b)])
            S.add("dve", lambda e, rs=rs, rtt=rtt: e.reciprocal(out=rs, in_=rtt), reads=[("rt", b)], writes=[("rs", b)])
            o_ = ot[bi_ % 2]
            ok_ = ("ot", bi_ % 2)
            S.add("dve", lambda e, pst=pst, n=n, rs=rs, o_=o_: e.scalar_tensor_tensor(
                out=o_[0:n, :], in0=pst[0:n, :], scalar=rs, in1=gpost2[0:n, :], op0=ALU.mult, op1=ALU.mult),
                reads=pkeys + [("rs", b), "gpost2"], writes=[ok_])
            S.add("pool", lambda e, n=n, o_=o_, rc=rc: e.tensor_tensor(out=o_[0:n, :], in0=o_[0:n, :], in1=rc[0:n, :],
                                                                        op=ALU.add),
                  reads=[ok_, rck], writes=[ok_])
            S.add("sp", lambda e, n=n, o_=o_, t0=t0: e.dma_start(out=out[t0 - 16:t0 - 16 + n, :], in_=o_[0:n, :]),
                  reads=[ok_], dsem=S.dsem(f"ost{bi_ % 2}"))

        for bi_ in range(len(GROUPS[0]["blocks"])):
            C0_block(0, bi_, 0)
        for gi in range(4):
            c3_at = {4: 0, 12: 1, 20: 2, 28: 3} if gi > 0 else {}
            for j in range(NJ):
                C1_iter(gi, j)
                if j in c3_at:
                    C3_block(gi - 1, c3_at[j])
            c0f_at = {0: 0, 3: 1, 6: 2, 9: 3} if gi < 3 else {}
            c0b_at = {2: 0, 5: 1, 8: 2, 11: 3} if gi < 3 else {}
            for oc in range(16):
                C2_iter(gi, oc)
                if oc in c0f_at:
                    C0_front(gi + 1, c0f_at[oc])
                if oc in c0b_at:
                    C0_back(gi + 1, c0b_at[oc], 2)
        for bi_ in range(4):
            C3_block(3, bi_)
    S.emit()
    return nc, S, A


def host_layout(inputs):
    f32 = np.float32
    fm = lambda v: np.ascontiguousarray(np.asarray(v, f32).reshape(-1, 128).T)
    b_in = np.asarray(inputs["b_in"], f32)[0]
    cvec = np.zeros((128, NCV), f32)
    cvec[:, C_G1:C_G1 + 16] = fm(inputs["mix_pre_g"][0])
    cvec[:, C_BQ:C_BQ + 8] = fm(b_in[0:1024])
    cvec[:, C_BK:C_BK + 8] = fm(b_in[1024:2048])
    cvec[:, C_BV:C_BV + 8] = fm(b_in[2048:3072])
    cvec[:, C_BU:C_BU + 8] = fm(b_in[3080:4104])
    cvec[:, C_BGA:C_BGA + 16] = fm(b_in[4104:6152])
    cvec[:, C_BGP:C_BGP + 16] = fm(b_in[6152:8200])
    cvec[:, C_PSC:C_PSC + 8] = fm(inputs["pool_scale"][0])
    cvec[:, C_G2:C_G2 + 16] = fm(inputs["ffn_pre_g"][0])
    cvec[:, C_CB:C_CB + 88] = fm(inputs["ffn_conv_b"][0])
    cw = np.asarray(inputs["ffn_conv_w"], f32)[0]
    cvec[:, C_CW0:C_CW0 + 88] = fm(cw[0])
    cvec[:, C_CW1:C_CW1 + 88] = fm(cw[1])
    cvec[:, C_CW2:C_CW2 + 88] = fm(cw[2])
    cvec[0:8, C_BF] = b_in[3072:3080]
    rowv = np.stack([np.asarray(inputs["mix_post_g"], f32)[0], np.asarray(inputs["ffn_post_g"], f32)[0]])
    consts = np.zeros((128, NKC), f32)
    consts[:, K_ID:K_ID + 128] = np.eye(128, dtype=f32)
    p = np.arange(128)[:, None]
    j = np.arange(128)[None, :]
    consts[:, K_MASK:K_MASK + 128] = np.where(j >= p, 0.0, -30000.0)
    for wi, w in enumerate(POOL_WINDOWS):
        for t in range(16):
            consts[:, K_RCNT + wi * 16 + t] = 1.0 / min(t + 1, w)
    consts[:, K_ONES:K_ONES + 128] = 1.0
    consts[:, K_EPS] = EPS
    return cvec, np.ascontiguousarray(rowv), consts


_CACHE = {}


def kernel(**inputs):
    f32 = np.float32
    x = np.asarray(inputs["x"], f32)
    B = x.shape[0]
    cvec, rowv, consts = host_layout(inputs)
    if "nc" not in _CACHE:
        _CACHE["nc"] = build_program()[0]
    nc = _CACHE["nc"]
    def slabs(w):
        K_, N_ = w.shape
        t = w.reshape(K_ // 128, 128, N_ // 128, 128).transpose(2, 1, 0, 3)
        return np.ascontiguousarray(t).reshape(N_ // 128 * 128, K_)

    win = np.asarray(inputs["w_in"], f32)[0]
    win_main = np.concatenate([win[:, 0:3072], win[:, 3080:8200]], axis=1)
    wf = np.ascontiguousarray(win[:, 3072:3080].reshape(16, 128, 8).transpose(1, 0, 2)).reshape(128, 128)
    wao = slabs(np.asarray(inputs["w_attn_o"], f32)[0])
    wpo = slabs(np.asarray(inputs["w_pool_o"], f32)[0])
    pw = np.asarray(inputs["pool_w"], f32)[0].reshape(4, 2, 128, 256).transpose(2, 0, 1, 3)
    shared = {
        "meta": np.ascontiguousarray(np.asarray(inputs["meta_tokens"], f32)),
        "w_in": slabs(win_main),
        "w_f": wf,
        "w_ap": np.ascontiguousarray(np.concatenate([wao, wpo], axis=1)),
        "pool_w": np.ascontiguousarray(pw).reshape(128, 2048),
        "w_out": slabs(np.asarray(inputs["w_out"], f32)[0]),
        "w_up": slabs(np.asarray(inputs["w_ffn_up"], f32)[0]),
        "w_down": slabs(np.asarray(inputs["w_ffn_down"], f32)[0]),
        "cvec": cvec, "rowv": rowv, "consts": consts,
    }
    in_maps = []
    for b in range(B):
        m = dict(shared)
        m["x"] = np.ascontiguousarray(x[b])
        in_maps.append(m)
    res = run_bass_kernel_spmd(nc, in_maps, core_ids=list(range(B)))
    return np.stack([np.asarray(r["out"], f32) for r in res.results], axis=0)
```

```python
import numpy as np
import concourse.bass as bass
import concourse.mybir as mybir
from concourse.bass_utils import run_bass_kernel_spmd

F32 = mybir.dt.float32
BF16 = mybir.dt.bfloat16
AF = mybir.ActivationFunctionType
ALU = mybir.AluOpType

D = 2048
SEQ = 2048
NMETA = 16
L = SEQ + NMETA
DIN = 8200
DFF = 5632
NJ = DFF // 128
EPS = 1e-6
QSCALE = 128 ** -0.5
POOL_WINDOWS = (2, 4, 8, 16)

BLK = [(0, 16)] + [(16 + 128 * i, 128) for i in range(16)]
TILES = [(0, 16)] + [(16 + 512 * i, 512) for i in range(4)]
GROUPS = [dict(start=0, n=528, blocks=list(range(0, 5)), tiles=[(0, 16), (16, 512)])]
for _g in range(1, 4):
    GROUPS.append(dict(start=16 + 512 * _g, n=512, blocks=list(range(1 + 4 * _g, 5 + 4 * _g)),
                       tiles=[(16 + 512 * _g, 512)]))

C_G1, C_BQ, C_BK, C_BV, C_BU, C_BGA, C_BGP, C_PSC, C_G2 = 0, 16, 24, 32, 40, 48, 64, 80, 88
C_CB, C_CW0, C_CW1, C_CW2, C_BF, NCV = 104, 192, 280, 368, 456, 457
K_ID, K_MASK, K_RCNT, K_ONES, K_EPS, NKC = 0, 128, 256, 320, 448, 449


class _Op:
    __slots__ = ("eng", "fn", "deps", "dsem", "ticket", "observed", "idx", "waits")


class Sched:
    def __init__(self, nc):
        self.nc = nc
        self.ops = []
        self.lastw = {}
        self.readers = {}
        self.dma_tot = {}
        self.esem = {}
        self._dsems = {}
        self.bar_op = None
        self.last_on = {}
        self.bg_sems = set()

    def dsem(self, name):
        if name not in self._dsems:
            self._dsems[name] = self.nc.alloc_semaphore("d_" + name)
            self.dma_tot[self._dsems[name]] = 0
        return self._dsems[name]

    def add(self, eng, fn, reads=(), writes=(), dsem=None):
        op = _Op()
        op.eng, op.fn, op.dsem = eng, fn, dsem
        op.idx = len(self.ops)
        op.observed = False
        deps = {}
        for k in reads:
            w = self.lastw.get(k)
            if w is not None:
                deps[w] = deps.get(w, 0) | 1
        for k in writes:
            w = self.lastw.get(k)
            if w is not None:
                deps[w] = deps.get(w, 0) | 1
            for r in self.readers.get(k, ()):
                deps[r] = deps.get(r, 0) | 2
        op.deps = []
        for d, kind in deps.items():
            dop = self.ops[d]
            if dop.dsem is not None:
                op.deps.append(("dma", dop.dsem, self.dma_tot[dop.dsem]))
            else:
                if dop.eng == eng and eng == "pe":
                    continue
                op.deps.append(("eng", d))
        if self.bar_op is not None and eng != "dve":
            op.deps.append(("eng", self.bar_op))
        if dsem is not None:
            self.dma_tot[dsem] += 16
        else:
            self.last_on[eng] = op.idx
        for k in reads:
            self.readers.setdefault(k, []).append(op.idx)
        for k in writes:
            self.lastw[k] = op.idx
            self.readers[k] = []
        self.ops.append(op)
        return op

    def barrier(self, scratch):
        op = _Op()
        op.eng, op.dsem = "dve", None
        op.fn = lambda e: e.memset(scratch, 0.0)
        op.idx = len(self.ops)
        op.observed = False
        op.deps = [("eng", i) for e, i in self.last_on.items() if e != "dve"]
        op.deps += [("dma", s, t) for s, t in self.dma_tot.items() if t > 0 and s not in self.bg_sems]
        self.ops.append(op)
        self.bar_op = op.idx
        self.last_on["dve"] = op.idx
        self.lastw = {k: v for k, v in self.lastw.items() if isinstance(k, tuple) and k[0] == "wsc"}
        self.readers = {}

    def emit(self):
        nc = self.nc
        for o in self.ops:
            for d in o.deps:
                if d[0] == "eng":
                    self.ops[d[1]].observed = True
        cnt = {}
        for o in self.ops:
            if o.dsem is None and o.observed:
                cnt[o.eng] = cnt.get(o.eng, 0) + 1
                o.ticket = cnt[o.eng]
        for e in cnt:
            self.esem[e] = nc.alloc_semaphore("e_" + e)
        for o in self.ops:
            w = {}
            for d in o.deps:
                if d[0] == "dma":
                    sem, val = d[1], d[2]
                else:
                    dop = self.ops[d[1]]
                    sem, val = self.esem[dop.eng], dop.ticket
                if w.get(sem, 0) < val:
                    w[sem] = val
            o.waits = w
        self.stats = dict(cnt)
        with nc.Block() as block:
            for engname, deco in (("sp", block.sync), ("act", block.scalar), ("pe", block.tensor),
                                  ("dve", block.vector), ("pool", block.gpsimd)):
                ops = [o for o in self.ops if o.eng == engname]

                def body(e, ops=ops, engname=engname):
                    waited = {}
                    for o in ops:
                        for sem, val in o.waits.items():
                            if waited.get(sem, 0) < val:
                                e.wait_ge(sem, val)
                                waited[sem] = val
                        ins = o.fn(e)
                        if o.dsem is not None:
                            ins.then_inc(o.dsem, 16)
                        elif o.observed:
                            ins.then_inc(self.esem[o.eng], 1)
                    if engname == "sp":
                        for sem, tot in self.dma_tot.items():
                            if tot > 0:
                                e.wait_ge(sem, tot)

                deco(body)


class Arena:
    def __init__(self, nc):
        self.nc = nc
        self.base = (nc.sbuf_base + 63) // 64 * 64
        self.top = nc.sbuf_top
        self.cur = self.base
        self.n = 0
        self.peak = 0

    def alloc(self, name, shape, dtype):
        esz = 2 if dtype == BF16 else 4
        size = esz
        for s in shape[1:]:
            size *= s
        off = (self.cur + 63) // 64 * 64
        assert off + size <= self.top, f"SBUF overflow allocating {name}: need {off + size - self.top} more bytes"
        self.cur = off + size
        self.peak = max(self.peak, self.cur)
        self.n += 1
        self.last_off = off
        return self.nc.alloc_sbuf_tensor_at(f"{name}_{self.n}", list(shape), dtype, offset=off)

    def mark(self):
        return self.cur

    def reset(self, m):
        self.cur = m


def build_program(stop=None, debug=False):
    nc = bass.Bass("TRN2", target_bir_lowering=False)
    dt = lambda name, shape, kind, dtp=F32: nc.dram_tensor(name, list(shape), dtp, kind=kind).ap()
    x = dt("x", [SEQ, D], "ExternalInput")
    meta = dt("meta", [NMETA, D], "ExternalInput")
    w_in = dt("w_in", [64 * 128, 2048], "ExternalInput")
    w_f = dt("w_f", [128, 128], "ExternalInput")
    w_ap = dt("w_ap", [16 * 128, 2048], "ExternalInput")
    pool_w = dt("pool_w", [128, 2048], "ExternalInput")
    w_out = dt("w_out", [16 * 128, 2048], "ExternalInput")
    w_up = dt("w_up", [88 * 128, 2048], "ExternalInput")
    w_down = dt("w_down", [16 * 128, DFF], "ExternalInput")
    cvec_d = dt("cvec", [128, NCV], "ExternalInput")
    rowv_d = dt("rowv", [2, D], "ExternalInput")
    consts_d = dt("consts", [128, NKC], "ExternalInput")
    out = dt("out", [SEQ, D], "ExternalOutput")
    r1s = dt("r1s", [L, D], "Internal")
    wscB = dt("wscB", [64 * 128, 2048], "Internal", BF16)
    wscC = dt("wscC", [136 * 128, 2048], "Internal", BF16)
    dbg = {}
    if debug:
        dbg["attnT"] = dt("dbg_attnT", [128, 8, L], "ExternalOutput", BF16)
        dbg["poolT"] = dt("dbg_poolT", [128, 8, L], "ExternalOutput", BF16)
        dbg["hT"] = dt("dbg_hT", [128, 16, L], "ExternalOutput", BF16)
        dbg["c8"] = dt("dbg_c8", [8, L], "ExternalOutput")

    S = Sched(nc)
    A = Arena(nc)
    psA = nc.alloc_psum_tensor("psA", [128, 2048], F32)
    psB = nc.alloc_psum_tensor("psB", [128, 2048], F32)

    def bank(i):
        t = psA if i < 4 else psB
        return t[:, (i % 4) * 512:(i % 4 + 1) * 512]

    PS = lambda i: ("ps", i)

    cvec = A.alloc("cvec", [128, NCV], F32)
    consts = A.alloc("consts", [128, NKC], F32)
    ones_bf = A.alloc("ones_bf", [128, 128], BF16)
    stat = A.alloc("stat", [128, 64], F32)
    bar_scr = A.alloc("bar", [128, 8], F32)
    identf = consts[:, K_ID:K_ID + 128]
    maskf = consts[:, K_MASK:K_MASK + 128]
    onesf = consts[:, K_ONES:K_ONES + 128]
    eps_t = consts[:, K_EPS:K_EPS + 1]

    S.add("sp", lambda e: e.dma_start(out=cvec[:], in_=cvec_d[:, :]), writes=["cvec"], dsem=S.dsem("cvec"))
    S.add("sp", lambda e: e.dma_start(out=consts[:], in_=consts_d[:, :]), writes=["consts"], dsem=S.dsem("consts"))
    S.add("dve", lambda e: e.tensor_copy(out=ones_bf[:], in_=onesf), reads=["consts"], writes=["ones_bf"])
    bqs = A.alloc("bqs", [128, 8], F32)
    S.add("dve", lambda e: e.tensor_scalar(out=bqs[:], in0=cvec[:, C_BQ:C_BQ + 8], scalar1=QSCALE, scalar2=None,
                                           op0=ALU.mult), reads=["cvec"], writes=["bqs"])

    def slabsrc(w, j, nk, ncol, k0=0):
        return w[j * 128:(j + 1) * 128, k0 * ncol:(k0 + nk) * ncol]

    bgB = []
    for c in range(16):
        bgB += [slabsrc(w_in, 32 + c, 16, 128), slabsrc(w_in, 48 + c, 16, 128), slabsrc(w_ap, c, 16, 128)]
    for oc in range(16):
        bgB.append(slabsrc(w_out, oc, 16, 128))
    bgC = []
    for j in range(NJ):
        bgC += [slabsrc(w_up, j, 16, 128), slabsrc(w_up, NJ + j, 16, 128)]
    for oc in range(16):
        for (k0, nk) in ((0, 16), (16, 16), (32, 12)):
            bgC.append(slabsrc(w_down, oc, nk, 128, k0))
    class Ring:
        def __init__(self, nf, nb):
            self.f = []
            self.f_off = None
            for i in range(nf):
                self.f.append(A.alloc(f"wf{i}", [128, 2048], F32))
                if i == 0:
                    self.f_off = A.last_off
            self.b = [A.alloc(f"wb{i}", [128, 2048], BF16) for i in range(nb)]
            self.fi = 0
            self.bi = 0
            self.ci = 0
            self.cast_engs = ("dve", "act")

    ring = [None]

    def load_slab(parts, cache=None):
        R = ring[0]
        offs = []
        off = 0
        for (ap3, k, w) in parts:
            offs.append(off)
            off += k * w
        bi = R.bi % len(R.b)
        R.bi += 1
        wb = R.b[bi]
        if cache is not None and cache[2] == "load":
            sc, cid = cache[0], cache[1]
            S.add("sp", lambda e, o=off: e.dma_start(out=wb[:, 0:o], in_=sc[cid * 128:(cid + 1) * 128, 0:o]),
                  reads=[("wsc", cache[3], cid)], writes=[("wb", bi)], dsem=S.dsem(f"wbl{bi}"))
            return wb, ("wb", bi), offs
        fi = R.fi % len(R.f)
        R.fi += 1
        wf = R.f[fi]
        for (ap3, k, w), o0 in zip(parts, offs):
            dst = wf[:, o0:o0 + k * w]
            S.add("sp", lambda e, dst=dst, ap3=ap3: e.dma_start(out=dst, in_=ap3),
                  writes=[("wf", fi)], dsem=S.dsem(f"wf{fi}"))
        ceng = R.cast_engs[R.ci % len(R.cast_engs)]
        R.ci += 1
        if ceng == "dve":
            S.add("dve", lambda e, o=off: e.tensor_copy(out=wb[:, 0:o], in_=wf[:, 0:o]),
                  reads=[("wf", fi)], writes=[("wb", bi)])
        else:
            S.add("act", lambda e, o=off: e.activation(out=wb[:, 0:o], in_=wf[:, 0:o], func=AF.Copy),
                  reads=[("wf", fi)], writes=[("wb", bi)])
        if cache is not None and cache[2] == "store":
            sc, cid = cache[0], cache[1]
            S.add("pool", lambda e, o=off: e.dma_start(out=sc[cid * 128:(cid + 1) * 128, 0:o], in_=wb[:, 0:o]),
                  reads=[("wb", bi)], writes=[("wsc", cid)], dsem=S.dsem(f"wst{bi}"))
        return wb, ("wb", bi), offs

    def slabv(w, j, nk, ncol, k0=0):
        return (w[j * 128:(j + 1) * 128, k0 * ncol:(k0 + nk) * ncol], nk, ncol)

    class Stream:
        def __init__(self, reqs, pf):
            self.reqs, self.pf, self.nxt, self.loaded = reqs, pf, 0, {}

        def get(self, i):
            hi = min(i + self.pf, len(self.reqs) - 1)
            while self.nxt <= hi:
                r_ = self.reqs[self.nxt]
                self.loaded[self.nxt] = load_slab(*r_) if isinstance(r_, tuple) else load_slab(r_)
                self.nxt += 1
            return self.loaded.pop(i)

    def rms_rows(src_tile, n, col, key_in, junk, junk_key):
        ss = stat[0:n, col:col + 1]
        rt = stat[0:n, 20 + col:21 + col]
        rs = stat[0:n, 40 + col:41 + col]
        S.add("act", lambda e: e.activation(out=junk[0:n, :], in_=src_tile, func=AF.Square, accum_out=ss),
              reads=[key_in], writes=[junk_key, ("ss", col)])
        S.add("act", lambda e: e.activation(out=rt, in_=ss, func=AF.Sqrt, scale=1.0 / D, bias=eps_t[0:n, :]),
              reads=[("ss", col), "consts"], writes=[("rt", col)])
        S.add("dve", lambda e: e.reciprocal(out=rs, in_=rt), reads=[("rt", col)], writes=[("rs", col)])
        return rs, ("rs", col)

    def to_feature_major(tile, n, tile_key, dstT, dst_key, col0, gcol, bankbase, evac_engs=("dve", "dve")):
        for cg in range(4):
            bk = bankbase + (cg % 2)

            def tr(e, cg=cg, bk=bk):
                ins = None
                for i in range(4):
                    c = cg * 4 + i
                    ins = e.transpose(out=bank(bk)[:, i * 128:i * 128 + n], in_=tile[0:n, c * 128:(c + 1) * 128],
                                      identity=identf[0:n, 0:n])
                return ins

            S.add("pe", tr, reads=[tile_key, "consts"], writes=[PS(bk)])
            for i in range(4):
                c = cg * 4 + i
                eng = evac_engs[i % 2]
                if eng == "dve":
                    S.add("dve", lambda e, c=c, i=i, bk=bk: e.tensor_scalar(
                        out=dstT[:, c, col0:col0 + n], in0=bank(bk)[:, i * 128:i * 128 + n],
                        scalar1=cvec[:, gcol + c:gcol + c + 1], scalar2=None, op0=ALU.mult),
                        reads=[PS(bk), "cvec"], writes=[dst_key])
                else:
                    S.add("act", lambda e, c=c, i=i, bk=bk: e.activation(
                        out=dstT[:, c, col0:col0 + n], in_=bank(bk)[:, i * 128:i * 128 + n], func=AF.Copy,
                        scale=cvec[:, gcol + c:gcol + c + 1]),
                        reads=[PS(bk), "cvec"], writes=[dst_key])

    def proj(wb, wkey, woffs, nk_list, ins_list, in_keys, m, bk, n, col_lists, first=True, last=True):
        def fn(e):
            ins = None
            tot = sum(nk_list)
            cnt = 0
            for wi, nk in enumerate(nk_list):
                for k in range(nk):
                    o = woffs[wi] + k * m
                    ins = e.matmul(bank(bk)[0:m, 0:n], lhsT=wb[:, o:o + m], rhs=ins_list[wi](k),
                                   start=(first and cnt == 0), stop=(last and cnt == tot - 1))
                    cnt += 1
            return ins
        S.add("pe", fn, reads=[wkey] + list(in_keys), writes=[PS(bk)])

    mA = A.mark()
    attnT = A.alloc("attnT", [128, 8, L], BF16)
    poolT = A.alloc("poolT", [128, 8, L], BF16)
    mA1 = A.mark()
    hT = A.alloc("hT", [128, 16, L], BF16)
    ring[0] = Ring(2, 3)
    R = ring[0]
    R.cast_engs = ("act",)
    reqA = [[slabv(w_f, 0, 16, 8)]]
    for h in range(8):
        for c0 in (0, 1024, 2048):
            reqA.append([slabv(w_in, (c0 // 1024) * 8 + h, 16, 128)])
    for c in range(8):
        reqA.append([slabv(w_in, 24 + c, 16, 128)])
    stA = Stream(reqA, 2)
    mA2 = A.mark()
    c8 = A.alloc("c8", [8, L], F32)
    negc = A.alloc("negc", [128, 17, 8], F32)
    mA2b = A.mark()

    for b, (t0, n) in enumerate(BLK):
        xt = R.f[b % 2]
        xk = ("wf", b % 2)
        src = meta[0:16, :] if b == 0 else x[t0 - 16:t0 - 16 + n, :]
        S.add("sp", lambda e, xt=xt, src=src, n=n: e.dma_start(out=xt[0:n, :], in_=src),
              writes=[xk], dsem=S.dsem(f"wf{b % 2}"))
        rs, rsk = rms_rows(xt[0:n, :], n, b, xk, R.b[0], ("wb", 0))
        S.add("act", lambda e, xt=xt, n=n, rs=rs: e.activation(out=xt[0:n, :], in_=xt[0:n, :], func=AF.Copy, scale=rs),
              reads=[xk, rsk], writes=[xk])
        to_feature_major(xt, n, xk, hT, ("hT", b), t0, C_G1, 0)
    HT_ALL = [("hT", b) for b in range(17)]

    def hT_keys(t0, n):
        return [("hT", b) for b, (b0, bn) in enumerate(BLK) if b0 < t0 + n and b0 + bn > t0]

    if debug:
        S.add("sp", lambda e: e.dma_start(out=dbg["hT"][:, :, :], in_=hT[:]), reads=HT_ALL, dsem=S.dsem("dbg"))

    wb, wk, wo = stA.get(0)
    lt = [A.alloc(f"lt{i}", [8, L], F32) for i in range(4)]
    for ti, (t0, n) in enumerate(TILES):
        bk = ti % 2
        proj(wb, wk, wo, [16], [lambda k, t0=t0, n=n: hT[:, k, t0:t0 + n]], hT_keys(t0, n), 8, bk, n, None)
        S.add("dve", lambda e, bk=bk, t0=t0, n=n: e.tensor_scalar(
            out=lt[0][:, t0:t0 + n], in0=bank(bk)[0:8, 0:n], scalar1=cvec[0:8, C_BF:C_BF + 1], scalar2=None,
            op0=ALU.add), reads=[PS(bk), "cvec"], writes=["lt0"])
    tf, ta, tb_, tc = lt
    V = lambda eng, fn, r, w: S.add(eng, fn, reads=r, writes=w)
    V("act", lambda e: e.activation(out=ta[:], in_=tf[:], func=AF.Abs), ["lt0"], ["lt1"])
    V("act", lambda e: e.activation(out=ta[:], in_=ta[:], func=AF.Exp, scale=-1.0), ["lt1"], ["lt1"])
    V("dve", lambda e: e.tensor_scalar(out=tb_[:], in0=ta[:], scalar1=2.0, scalar2=None, op0=ALU.add), ["lt1"], ["lt2"])
    V("dve", lambda e: e.reciprocal(out=tb_[:], in_=tb_[:]), ["lt2"], ["lt2"])
    V("dve", lambda e: e.tensor_tensor(out=ta[:], in0=ta[:], in1=tb_[:], op=ALU.mult), ["lt1", "lt2"], ["lt1"])
    V("dve", lambda e: e.tensor_tensor(out=tb_[:], in0=ta[:], in1=ta[:], op=ALU.mult), ["lt1"], ["lt2"])
    V("dve", lambda e: e.tensor_scalar(out=tc[:], in0=tb_[:], scalar1=1.0 / 9.0, scalar2=None, op0=ALU.mult), ["lt2"], ["lt3"])
    for cst in (1.0 / 7.0, 1.0 / 5.0, 1.0 / 3.0):
        V("dve", lambda e, cst=cst: e.scalar_tensor_tensor(out=tc[:], in0=tc[:], scalar=cst, in1=tb_[:],
                                                           op0=ALU.add, op1=ALU.mult), ["lt3", "lt2"], ["lt3"])
    V("dve", lambda e: e.scalar_tensor_tensor(out=tc[:], in0=tc[:], scalar=1.0, in1=ta[:], op0=ALU.add, op1=ALU.mult),
      ["lt3", "lt1"], ["lt3"])
    V("dve", lambda e: e.tensor_scalar(out=ta[:], in0=tf[:], scalar1=0.0, scalar2=None, op0=ALU.min), ["lt0", "lt1"], ["lt1"])
    V("dve", lambda e: e.scalar_tensor_tensor(out=tb_[:], in0=tc[:], scalar=-2.0, in1=ta[:], op0=ALU.mult, op1=ALU.add),
      ["lt3", "lt1", "lt2"], ["lt2"])
    V("dve", lambda e: e.memset(tc[:], 1.0), ["lt3"], ["lt3"])
    V("dve", lambda e: e.tensor_tensor_scan(out=c8[:], data0=tc[:], data1=tb_[:], initial=0.0, op0=ALU.mult, op1=ALU.add),
      ["lt3", "lt2"], ["c8"])
    for b, (t0, n) in enumerate(BLK):
        bk = b % 2
        S.add("pe", lambda e, bk=bk, t0=t0, n=n: e.transpose(out=bank(bk)[0:n, 0:8], in_=c8[0:8, t0:t0 + n],
                                                             identity=identf[0:8, 0:8]),
              reads=["c8", "consts"], writes=[PS(bk)])
        S.add("dve", lambda e, bk=bk, b=b, n=n: e.tensor_scalar(out=negc[0:n, b, :], in0=bank(bk)[0:n, 0:8], scalar1=-1.0,
                                                                scalar2=None, op0=ALU.mult),
              reads=[PS(bk)], writes=["negc"])
    if stop not in ("A0", "A"):
        nbg = 0
        for nm, scr, lst in (("B", wscB, bgB), ("C", wscC, bgC)):
            for cid, src in enumerate(lst):
                ncol = src.shape[1]
                sem = S.dsem(f"bg{nbg // 8}")
                S.bg_sems.add(sem)
                nbg += 1
                S.add("pool", lambda e, scr=scr, cid=cid, src=src, ncol=ncol: e.dma_start(
                    out=scr[cid * 128:(cid + 1) * 128, 0:ncol], in_=src), reads=["c8"], writes=[("wsc", nm, cid)], dsem=sem)

    if debug:
        S.add("sp", lambda e: e.dma_start(out=dbg["c8"][:, :], in_=c8[:]), reads=["c8"], dsem=S.dsem("dbg"))
    S.barrier(bar_scr[0:1, 0:1])
    A.reset(mA2b)

    if stop != "A0":
        qT = A.alloc("qT", [128, L], BF16)
        kT = A.alloc("kT", [128, L], BF16)
        vTf = A.alloc("vTf", [128, L], F32)
        Vtm = A.alloc("Vtm", [128, 17, 128], BF16)
        cq = [A.alloc(f"cq{i}", [128, 512], F32) for i in range(2)]
        c8h = A.alloc("c8h", [8, 512], F32)
        PT = [A.alloc(f"PT_{i}", [128, 512], BF16) for i in range(5)]
        SBANK = (2, 3, 6, 5, 0)
        NSD = len(SBANK)
        rden = A.alloc("rden", [128, 512], F32)
        for h in range(8):
            for which, c0, dstname in (("q", 0, "qT"), ("k", 1024, "kT"), ("v", 2048, "vTf")):
                wb, wk, wo = stA.get(1 + h * 3 + (c0 // 1024))
                for ti, (t0, n) in enumerate(TILES):
                    bk = ti % 2
                    proj(wb, wk, wo, [16], [lambda k, t0=t0, n=n: hT[:, k, t0:t0 + n]], hT_keys(t0, n), 128, bk, n, None)
                    if which == "q":
                        S.add("act", lambda e, bk=bk, t0=t0, n=n, h=h: e.activation(
                            out=qT[:, t0:t0 + n], in_=bank(bk)[:, 0:n], func=AF.Identity, scale=QSCALE,
                            bias=bqs[:, h:h + 1]), reads=[PS(bk), "bqs"], writes=[("qT", ti)])
                    elif which == "k":
                        S.add("act", lambda e, bk=bk, t0=t0, n=n, h=h: e.activation(
                            out=kT[:, t0:t0 + n], in_=bank(bk)[:, 0:n], func=AF.Identity,
                            bias=cvec[:, C_BK + h:C_BK + h + 1]), reads=[PS(bk), "cvec"], writes=[("kT", ti)])
                    else:
                        S.add("act", lambda e, bk=bk, t0=t0, n=n, h=h: e.activation(
                            out=vTf[:, t0:t0 + n], in_=bank(bk)[:, 0:n], func=AF.Identity,
                            bias=cvec[:, C_BV + h:C_BV + h + 1]), reads=[PS(bk), "cvec"], writes=[("vTf", ti)])
            for b, (t0, n) in enumerate(BLK):
                bk = b % 2
                ti = 0 if b == 0 else 1 + (b - 1) // 4
                S.add("pe", lambda e, bk=bk, t0=t0, n=n: e.transpose(out=bank(bk)[0:n, 0:128], in_=vTf[:, t0:t0 + n],
                                                                     identity=identf),
                      reads=[("vTf", ti), "consts"], writes=[PS(bk)])
                S.add("act", lambda e, bk=bk, b=b, n=n: e.activation(out=Vtm[0:n, b, :], in_=bank(bk)[0:n, 0:128], func=AF.Copy),
                      reads=[PS(bk)], writes=[("Vtm", b)])
            blocks = []
            for ti, (t0, n) in enumerate(TILES):
                kbs = [(b, k0, kn) for b, (k0, kn) in enumerate(BLK) if k0 < t0 + n]
                for bi_, (b, k0, kn) in enumerate(kbs):
                    qlo = max(t0, k0)
                    blocks.append(dict(ti=ti, t0=t0, n=n, b=b, k0=k0, kn=kn, qlo=qlo, N=t0 + n - qlo, off=qlo - t0,
                                       diag=k0 >= t0, first=bi_ == 0, last=bi_ == len(kbs) - 1,
                                       kti=0 if b == 0 else 1 + (b - 1) // 4))

            def emit_cq(ti, h=h):
                t0, n = TILES[ti]
                cqt = cq[ti % 2]
                S.add("dve", lambda e, t0=t0, n=n, h=h: e.tensor_scalar(
                    out=c8h[:, 0:n], in0=c8[:, t0:t0 + n], scalar1=identf[0:8, h:h + 1], scalar2=None, op0=ALU.mult),
                    reads=["c8", "consts"], writes=["c8h"])
                S.add("pe", lambda e, n=n, ti=ti: e.matmul(bank(1)[:, 0:n], lhsT=onesf[0:8, :], rhs=c8h[:, 0:n],
                                                           start=True, stop=True),
                      reads=["c8h", "consts"], writes=[PS(1)])
                S.add("act", lambda e, n=n, ti=ti, cqt=cqt: e.activation(out=cqt[:, 0:n], in_=bank(1)[:, 0:n], func=AF.Copy),
                      reads=[PS(1)], writes=[("cq", ti % 2)])

            def emit_S(idx):
                B_ = blocks[idx]
                sb = SBANK[idx % NSD]
                S.add("pe", lambda e, sb=sb, k0=B_["k0"], kn=B_["kn"], qlo=B_["qlo"], N=B_["N"]: e.matmul(
                    bank(sb)[0:kn, 0:N], lhsT=kT[:, k0:k0 + kn], rhs=qT[:, qlo:qlo + N], start=True, stop=True),
                    reads=[("kT", B_["kti"]), ("qT", B_["ti"])], writes=[PS(sb)])

            emit_cq(0)
            for _i in range(min(NSD, len(blocks))):
                emit_S(_i)
            for idx, B_ in enumerate(blocks):
                ti, t0, n, b, kn, N, off = B_["ti"], B_["t0"], B_["n"], B_["b"], B_["kn"], B_["N"], B_["off"]
                ob, db = 4, 7
                sb = SBANK[idx % NSD]
                tb = idx % NSD
                cqt = cq[ti % 2]
                if B_["first"] and ti + 1 < len(TILES):
                    emit_cq(ti + 1)
                S.add("dve", lambda e, sb=sb, kn=kn, N=N, off=off, cqt=cqt: e.tensor_tensor(
                    out=bank(sb)[0:kn, 0:N], in0=bank(sb)[0:kn, 0:N], in1=cqt[0:kn, off:off + N], op=ALU.add),
                    reads=[PS(sb), ("cq", ti % 2)], writes=[PS(sb)])
                if B_["diag"]:
                    S.add("dve", lambda e, sb=sb, kn=kn: e.tensor_tensor(
                        out=bank(sb)[0:kn, 0:kn], in0=bank(sb)[0:kn, 0:kn], in1=maskf[0:kn, 0:kn], op=ALU.add),
                        reads=[PS(sb), "consts"], writes=[PS(sb)])
                S.add("act", lambda e, sb=sb, tb=tb, kn=kn, N=N, b=b, h=h: e.activation(
                    out=PT[tb][0:kn, 0:N], in_=bank(sb)[0:kn, 0:N], func=AF.Exp, bias=negc[0:kn, b, h:h + 1]),
                    reads=[PS(sb), "negc"], writes=[("PT", tb)])

                def pv(e, tb=tb, kn=kn, N=N, off=off, b=b, ob=ob, db=db, first=B_["first"], lastb=B_["last"]):
                    e.matmul(bank(ob)[:, off:off + N], lhsT=Vtm[0:kn, b, :], rhs=PT[tb][0:kn, 0:N],
                             start=first, stop=lastb)
                    return e.matmul(bank(db)[:, off:off + N], lhsT=ones_bf[0:kn, :], rhs=PT[tb][0:kn, 0:N],
                                    start=first, stop=lastb)
                S.add("pe", pv, reads=[("PT", tb), ("Vtm", b), "ones_bf"], writes=[PS(ob), PS(db)])
                if idx + NSD < len(blocks):
                    emit_S(idx + NSD)
                if B_["last"]:
                    S.add("dve", lambda e, db=db, n=n: e.reciprocal(out=rden[:, 0:n], in_=bank(db)[:, 0:n]),
                          reads=[PS(db)], writes=["rden"])
                    S.add("dve", lambda e, ob=ob, t0=t0, n=n, h=h: e.tensor_tensor(
                        out=attnT[:, h, t0:t0 + n], in0=bank(ob)[:, 0:n], in1=rden[:, 0:n], op=ALU.mult),
                        reads=[PS(ob), "rden"], writes=[("attnT", ti)])
        S.barrier(bar_scr[0:1, 0:1])
        A.reset(mA2)

        ub = [A.alloc(f"ub{i}", [128, 16 + L], F32) for i in range(2)]
        tA = A.alloc("tA", [128, 16 + L], F32)
        tB = A.alloc("tB", [128, 16 + L], F32)
        dT = A.alloc("dT", [128, 2, L], BF16)
        t16 = A.alloc("t16", [128, 16], F32)
        for i, tt in enumerate((ub[0], ub[1], tA, tB)):
            S.add("pool", lambda e, tt=tt: e.memset(tt[:, 0:16], 0.0), writes=[("pad", i)])
        pwb = A.alloc("pwb", [128, 2048], BF16)
        pwk, pwo = "pwb", [0]
        _fi = R.fi % len(R.f)
        R.fi += 1
        S.add("sp", lambda e: e.dma_start(out=R.f[_fi][:, :], in_=pool_w[:, :]),
              writes=[("wf", _fi)], dsem=S.dsem(f"wf{_fi}"))
        S.add("pool", lambda e: e.tensor_copy(out=pwb[:], in_=R.f[_fi][:]), reads=[("wf", _fi)], writes=["pwb"])
        for c in range(8):
            g = c // 2
            w = POOL_WINDOWS[g]
            u = ub[c % 2]
            uk = ("ub", c % 2)
            wb, wk, wo = stA.get(25 + c)
            for ti, (t0, n) in enumerate(TILES):
                bk = ti % 2
                proj(wb, wk, wo, [16], [lambda k, t0=t0, n=n: hT[:, k, t0:t0 + n]], hT_keys(t0, n), 128, bk, n, None)
                S.add("act", lambda e, bk=bk, t0=t0, n=n, c=c, u=u: e.activation(
                    out=u[:, 16 + t0:16 + t0 + n], in_=bank(bk)[:, 0:n], func=AF.Identity,
                    bias=cvec[:, C_BU + c:C_BU + c + 1]), reads=[PS(bk), "cvec", ("pad", c % 2)], writes=[uk])
            src, srck, srcpad = u, uk, ("pad", c % 2)
            sh = 1
            pp = [(tA, "tA", ("pad", 2)), (tB, "tB", ("pad", 3))]
            pi = 0
            while sh < w:
                dst, dk, dpad = pp[pi % 2]
                S.add("dve", lambda e, dst=dst, src=src, sh=sh: e.tensor_tensor(
                    out=dst[:, 16:16 + L], in0=src[:, 16:16 + L], in1=src[:, 16 - sh:16 - sh + L], op=ALU.add),
                    reads=[srck, srcpad], writes=[dk])
                src, srck, srcpad = dst, dk, dpad
                sh *= 2
                pi += 1
            wi = POOL_WINDOWS.index(w)
            S.add("dve", lambda e, src=src, u=u, w=w, c=c: e.scalar_tensor_tensor(
                out=dT[:, c % 2, 16:L], in0=src[:, 32:16 + L], scalar=1.0 / w, in1=u[:, 32:16 + L],
                op0=ALU.mult, op1=ALU.subtract), reads=[srck, uk], writes=[("dT", c % 2)])
            S.add("dve", lambda e, src=src, wi=wi: e.tensor_tensor(
                out=t16[:], in0=src[:, 16:32], in1=consts[:, K_RCNT + wi * 16:K_RCNT + wi * 16 + 16], op=ALU.mult),
                reads=[srck, "consts"], writes=["t16"])
            S.add("dve", lambda e, u=u, c=c: e.tensor_tensor(
                out=dT[:, c % 2, 0:16], in0=t16[:], in1=u[:, 16:32], op=ALU.subtract),
                reads=["t16", uk], writes=[("dT", c % 2)])
            if c % 2 == 1:
                for ocl in range(2):
                    oc = 2 * g + ocl
                    for ti, (t0, n) in enumerate(TILES):
                        bk = ti % 2

                        def fn(e, g=g, ocl=ocl, bk=bk, t0=t0, n=n):
                            ins = None
                            for kl in range(2):
                                o = pwo[0] + (g * 2 + kl) * 256 + ocl * 128
                                ins = e.matmul(bank(bk)[:, 0:n], lhsT=pwb[:, o:o + 128], rhs=dT[:, kl, t0:t0 + n],
                                               start=(kl == 0), stop=(kl == 1))
                            return ins
                        S.add("pe", fn, reads=[pwk, ("dT", 0), ("dT", 1)], writes=[PS(bk)])
                        S.add("act", lambda e, bk=bk, oc=oc, t0=t0, n=n: e.activation(
                            out=poolT[:, oc, t0:t0 + n], in_=bank(bk)[:, 0:n], func=AF.Copy,
                            scale=cvec[:, C_PSC + oc:C_PSC + oc + 1]), reads=[PS(bk), "cvec"], writes=[("poolT", ti)])
        if debug:
            S.add("sp", lambda e: e.dma_start(out=dbg["attnT"][:, :, :], in_=attnT[:]),
                  reads=[("attnT", i) for i in range(5)], dsem=S.dsem("dbg"))
            S.add("sp", lambda e: e.dma_start(out=dbg["poolT"][:, :, :], in_=poolT[:]),
                  reads=[("poolT", i) for i in range(5)], dsem=S.dsem("dbg"))
        S.barrier(bar_scr[0:1, 0:1])
    A.reset(mA1)

    if stop not in ("A0", "A"):
        gpost = A.alloc("gpost", [128, D], F32)
        S.add("sp", lambda e: e.dma_start(out=gpost[:], in_=rowv_d[0:1, :].partition_broadcast(128)),
              writes=["gpost"], dsem=S.dsem("gpost"))
        xtB = A.alloc("xtB", [128, D], F32)
        junk = A.alloc("junk", [128, D], BF16)
        r1t = [A.alloc(f"r1t{i}", [128, D], F32) for i in range(2)]
        mTg = A.alloc("mTg", [128, 16, 528], BF16)
        ring[0] = Ring(0, 5)
        reqB = []
        for _gi in range(4):
            for c in range(16):
                reqB.append(([slabv(w_in, 32 + c, 16, 128)], (wscB, c * 3, "load", "B")))
                reqB.append(([slabv(w_in, 48 + c, 16, 128)], (wscB, c * 3 + 1, "load", "B")))
                reqB.append(([slabv(w_ap, c, 16, 128)], (wscB, c * 3 + 2, "load", "B")))
            for oc in range(16):
                reqB.append(([slabv(w_out, oc, 16, 128)], (wscB, 48 + oc, "load", "B")))
        stB = Stream(reqB, 4)
        mixT = A.alloc("mixT", [128, 16, 528], F32)
        hTg = A.alloc("hTg", [128, 16, 528], BF16)
        gt = [[A.alloc(f"gt{i}_{j}", [128, 512], F32) for j in range(4)] for i in range(2)]
        rot = [(xtB, "xtB", "xtB"), (r1t[0], ("r1t", 0), "r1st0"), (r1t[1], ("r1t", 1), "r1st1")]

        def B0_block(gi, bi0, bankbase):
            G = GROUPS[gi]
            b = G["blocks"][bi0]
            t0, n = BLK[b]
            xt_, xk_, xs_ = rot[bi0 % 3]
            src = meta[0:16, :] if b == 0 else x[t0 - 16:t0 - 16 + n, :]
            S.add("sp", lambda e, src=src, n=n, xt_=xt_: e.dma_start(out=xt_[0:n, :], in_=src), writes=[xk_],
                  dsem=S.dsem(xs_))
            rs, rsk = rms_rows(xt_[0:n, :], n, b, xk_, junk, "junk")
            S.add("act", lambda e, n=n, rs=rs, xt_=xt_: e.activation(out=xt_[0:n, :], in_=xt_[0:n, :], func=AF.Copy, scale=rs),
                  reads=[xk_, rsk], writes=[xk_])
            to_feature_major(xt_, n, xk_, hTg, ("hTg", bi0), t0 - G["start"], C_G1, bankbase)

        def B1(gi):
            G = GROUPS[gi]
            gs = G["start"]
            hk = [("hTg", i) for i in range(len(G["blocks"]))]
            it = 0
            for c in range(16):
                tl = []
                for (t0, n) in G["tiles"]:
                    tl.append((t0, n, t0 - gs, 4 * (it % 2), it % 2, TILES.index((t0, n))))
                    it += 1
                wga = stB.get(gi * 64 + c * 3)
                for (t0, n, lo, pb, par, ti) in tl:
                    proj(wga[0], wga[1], wga[2], [16], [lambda k, lo=lo, n=n: hTg[:, k, lo:lo + n]], hk, 128, pb + 0, n, None)
                wgp = stB.get(gi * 64 + c * 3 + 1)
                for (t0, n, lo, pb, par, ti) in tl:
                    proj(wgp[0], wgp[1], wgp[2], [16], [lambda k, lo=lo, n=n: hTg[:, k, lo:lo + n]], hk, 128, pb + 1, n, None)
                wap = stB.get(gi * 64 + c * 3 + 2)
                for (t0, n, lo, pb, par, ti) in tl:
                    proj(wap[0], wap[1], [wap[2][0]], [8], [lambda k, t0=t0, n=n: attnT[:, k, t0:t0 + n]],
                         [("attnT", ti)], 128, pb + 2, n, None)
                    proj(wap[0], wap[1], [wap[2][0] + 1024], [8], [lambda k, t0=t0, n=n: poolT[:, k, t0:t0 + n]],
                         [("poolT", ti)], 128, pb + 3, n, None)
                for (t0, n, lo, pb, par, ti) in tl:
                    g4 = gt[par]
                    S.add("act", lambda e, pb=pb, g4=g4, n=n, c=c: e.activation(
                        out=g4[0][:, 0:n], in_=bank(pb)[:, 0:n], func=AF.Sigmoid, bias=cvec[:, C_BGA + c:C_BGA + c + 1]),
                        reads=[PS(pb), "cvec"], writes=[("gt", par, 0)])
                    S.add("act", lambda e, pb=pb, g4=g4, n=n, c=c: e.activation(
                        out=g4[1][:, 0:n], in_=bank(pb + 1)[:, 0:n], func=AF.Sigmoid, bias=cvec[:, C_BGP + c:C_BGP + c + 1]),
                        reads=[PS(pb + 1), "cvec"], writes=[("gt", par, 1)])
                    S.add("dve", lambda e, pb=pb, g4=g4, n=n: e.tensor_tensor(
                        out=g4[2][:, 0:n], in0=g4[0][:, 0:n], in1=bank(pb + 2)[:, 0:n], op=ALU.mult),
                        reads=[PS(pb + 2), ("gt", par, 0)], writes=[("gt", par, 2)])
                    S.add("dve", lambda e, pb=pb, g4=g4, n=n: e.tensor_tensor(
                        out=g4[3][:, 0:n], in0=g4[1][:, 0:n], in1=bank(pb + 3)[:, 0:n], op=ALU.mult),
                        reads=[PS(pb + 3), ("gt", par, 1)], writes=[("gt", par, 3)])
                    S.add("pool", lambda e, g4=g4, n=n, c=c, lo=lo: e.tensor_tensor(
                        out=mTg[:, c, lo:lo + n], in0=g4[2][:, 0:n], in1=g4[3][:, 0:n], op=ALU.add),
                        reads=[("gt", par, 2), ("gt", par, 3)], writes=[("mTg", c)])

        def B2_iter(gi, oc, itc):
            G = GROUPS[gi]
            gs = G["start"]
            mx = mixT
            mk = [("mTg", c) for c in range(16)]
            wo_ = stB.get(gi * 64 + 48 + oc)
            for (t0, n) in G["tiles"]:
                lo = t0 - gs
                bk = itc[0] % 2
                itc[0] += 1
                proj(wo_[0], wo_[1], wo_[2], [16], [lambda k, lo=lo, n=n: mTg[:, k, lo:lo + n]], mk, 128, bk, n, None)
                S.add("act", lambda e, bk=bk, oc=oc, lo=lo, n=n, mx=mx: e.activation(
                    out=mx[:, oc, lo:lo + n], in_=bank(bk)[:, 0:n], func=AF.Copy),
                    reads=[PS(bk)], writes=[("mixT", oc)])

        def B3(gi):
            G = GROUPS[gi]
            gs = G["start"]
            mx = mixT
            xk_ = [("mixT", oc) for oc in range(16)]
            for bi_, b in enumerate(G["blocks"]):
                t0, n = BLK[b]
                lo = t0 - gs
                half = 4 * (bi_ % 2)
                pst = psA if half == 0 else psB
                for cg in range(4):
                    def tr(e, cg=cg, lo=lo, n=n, half=half, mx=mx):
                        ins = None
                        for i in range(4):
                            oc = cg * 4 + i
                            ins = e.transpose(out=bank(half + cg)[0:n, i * 128:(i + 1) * 128], in_=mx[:, oc, lo:lo + n],
                                              identity=identf)
                        return ins
                    S.add("pe", tr, reads=xk_ + ["consts"], writes=[PS(half + cg)])
                pkeys = [PS(half + i) for i in range(4)]
                rt_ = r1t[bi_ % 2]
                rk = ("r1t", bi_ % 2)
                src = meta[0:16, :] if b == 0 else x[t0 - 16:t0 - 16 + n, :]
                S.add("sp", lambda e, src=src, n=n: e.dma_start(out=xtB[0:n, :], in_=src), writes=["xtB"], dsem=S.dsem("xtB"))
                ss = stat[0:n, b:b + 1]
                S.add("act", lambda e, pst=pst, n=n, ss=ss: e.activation(out=junk[0:n, :], in_=pst[0:n, :], func=AF.Square,
                                                                         accum_out=ss),
                      reads=pkeys, writes=["junk", ("ss", b)])
                rtt = stat[0:n, 20 + b:21 + b]
                rs = stat[0:n, 40 + b:41 + b]
                S.add("act", lambda e, rtt=rtt, ss=ss, n=n: e.activation(out=rtt, in_=ss, func=AF.Sqrt, scale=1.0 / D,
                                                                         bias=eps_t[0:n, :]),
                      reads=[("ss", b), "consts"], writes=[("rt", b)])
                S.add("dve", lambda e, rs=rs, rtt=rtt: e.reciprocal(out=rs, in_=rtt), reads=[("rt", b)], writes=[("rs", b)])
                S.add("dve", lambda e, pst=pst, n=n, rs=rs, rt_=rt_: e.scalar_tensor_tensor(
                    out=rt_[0:n, :], in0=pst[0:n, :], scalar=rs, in1=gpost[0:n, :], op0=ALU.mult, op1=ALU.mult),
                    reads=pkeys + [("rs", b), "gpost"], writes=[rk])
                S.add("pool", lambda e, n=n, rt_=rt_: e.tensor_tensor(out=rt_[0:n, :], in0=rt_[0:n, :], in1=xtB[0:n, :],
                                                                       op=ALU.add),
                      reads=[rk, "xtB"], writes=[rk])
                S.add("sp", lambda e, n=n, rt_=rt_, t0=t0: e.dma_start(out=r1s[t0:t0 + n, :], in_=rt_[0:n, :]),
                      reads=[rk], writes=[("r1s", b)], dsem=S.dsem(f"r1st{bi_ % 2}"))

        itc = [0]
        for bi0 in range(len(GROUPS[0]["blocks"])):
            B0_block(0, bi0, 0)
        for gi in range(0, 4):
            B1(gi)
            b0_at = {2: 0, 5: 1, 8: 2, 11: 3} if gi < 3 else {}
            for oc in range(16):
                B2_iter(gi, oc, itc)
                if oc in b0_at:
                    B0_block(gi + 1, b0_at[oc], 2)
            B3(gi)
        S.barrier(bar_scr[0:1, 0:1])
    A.reset(mA)

    if stop not in ("A0", "A", "B"):
        gpost2 = A.alloc("gpost2", [128, D], F32)
        S.add("sp", lambda e: e.dma_start(out=gpost2[:], in_=rowv_d[1:2, :].partition_broadcast(128)),
              writes=["gpost2"], dsem=S.dsem("gpost2"))
        junkC = A.alloc("junkC", [128, D], BF16)
        r1c = [A.alloc(f"r1c{i}", [128, D], F32) for i in range(2)]
        ot = [A.alloc(f"ot{i}", [128, D], F32) for i in range(2)]
        carry = A.alloc("carry", [128, 88, 2], F32)
        S.add("pool", lambda e: e.memset(carry[:], 0.0), writes=["carry"])
        h2T = A.alloc("h2T", [128, 16, 528], BF16)
        actT = A.alloc("actT", [128, NJ, 528], BF16)
        ffT = A.alloc("ffT", [128, 16, 512], F32)
        upb = [[A.alloc(f"up{i}_{j}", [128, 530], F32) for j in range(2)] for i in range(2)]
        cv = [[A.alloc(f"cv{i}_{j}", [128, 528], F32) for j in range(3)] for i in range(2)]
        ring[0] = Ring(0, 6)
        reqC = []
        for _gi in range(4):
            for j in range(NJ):
                reqC.append(([slabv(w_up, j, 16, 128)], (wscC, j * 2, "load", "C")))
                reqC.append(([slabv(w_up, NJ + j, 16, 128)], (wscC, j * 2 + 1, "load", "C")))
            for oc in range(16):
                for pi_, (k0, nk) in enumerate(((0, 16), (16, 16), (32, 12))):
                    reqC.append(([slabv(w_down, oc, nk, 128, k0)], (wscC, 88 + oc * 3 + pi_, "load", "C")))
        stC = Stream(reqC, 5)

        def C0_front(gi, bi_):
            G = GROUPS[gi]
            b = G["blocks"][bi_]
            t0, n = BLK[b]
            rc = r1c[bi_ % 2]
            rck = ("r1c", bi_ % 2)
            S.add("sp", lambda e, rc=rc, t0=t0, n=n: e.dma_start(out=rc[0:n, :], in_=r1s[t0:t0 + n, :]),
                  reads=[("r1s", b)], writes=[rck], dsem=S.dsem(f"r1c{bi_ % 2}"))
            rs, rsk = rms_rows(rc[0:n, :], n, b, rck, junkC, "junkC")
            S.add("act", lambda e, rc=rc, n=n, rs=rs: e.activation(out=rc[0:n, :], in_=rc[0:n, :], func=AF.Copy, scale=rs),
                  reads=[rck, rsk], writes=[rck])

        def C0_back(gi, bi_, bankbase):
            G = GROUPS[gi]
            b = G["blocks"][bi_]
            t0, n = BLK[b]
            to_feature_major(r1c[bi_ % 2], n, ("r1c", bi_ % 2), h2T, ("h2T", bi_), t0 - G["start"], C_G2, bankbase)

        def C0_block(gi, bi_, bankbase):
            C0_front(gi, bi_)
            C0_back(gi, bi_, bankbase)

        def C1_iter(gi, j):
            G = GROUPS[gi]
            gs, gn = G["start"], G["n"]
            hk = [("h2T", i) for i in range(len(G["blocks"]))]
            s2 = j % 2
            for half_, jj in ((0, j), (1, NJ + j)):
                wu = stC.get(gi * 136 + j * 2 + half_)
                ub_ = upb[s2][half_]
                ubk = ("upb", s2, half_)
                S.add("pool", lambda e, ub_=ub_, jj=jj: e.tensor_copy(out=ub_[:, 0:2], in_=carry[:, jj, :]),
                      reads=["carry"], writes=[ubk])
                for tix, (t0, n) in enumerate(G["tiles"]):
                    lo = t0 - gs
                    bk = 2 * half_ + (j + tix) % 2
                    proj(wu[0], wu[1], wu[2], [16], [lambda k, lo=lo, n=n: h2T[:, k, lo:lo + n]], hk, 128, bk, n, None)
                    S.add("act", lambda e, bk=bk, ub_=ub_, lo=lo, n=n: e.activation(
                        out=ub_[:, 2 + lo:2 + lo + n], in_=bank(bk)[:, 0:n], func=AF.Copy),
                        reads=[PS(bk)], writes=[ubk])
                S.add("pool", lambda e, ub_=ub_, jj=jj, gn=gn: e.tensor_copy(out=carry[:, jj, :], in_=ub_[:, gn:gn + 2]),
                      reads=[ubk], writes=["carry"])
                tg = cv[s2][half_]
                tgk = ("cv", s2, half_)
                S.add("dve", lambda e, tg=tg, ub_=ub_, jj=jj, gn=gn: e.tensor_scalar(
                    out=tg[:, 0:gn], in0=ub_[:, 0:gn], scalar1=cvec[:, C_CW0 + jj:C_CW0 + jj + 1],
                    scalar2=cvec[:, C_CB + jj:C_CB + jj + 1], op0=ALU.mult, op1=ALU.add),
                    reads=[ubk, "cvec"], writes=[tgk])
                S.add("dve", lambda e, tg=tg, ub_=ub_, jj=jj, gn=gn: e.scalar_tensor_tensor(
                    out=tg[:, 0:gn], in0=ub_[:, 1:gn + 1], scalar=cvec[:, C_CW1 + jj:C_CW1 + jj + 1], in1=tg[:, 0:gn],
                    op0=ALU.mult, op1=ALU.add), reads=[ubk, "cvec", tgk], writes=[tgk])
                S.add("dve", lambda e, tg=tg, ub_=ub_, jj=jj, gn=gn: e.scalar_tensor_tensor(
                    out=tg[:, 0:gn], in0=ub_[:, 2:gn + 2], scalar=cvec[:, C_CW2 + jj:C_CW2 + jj + 1], in1=tg[:, 0:gn],
                    op0=ALU.mult, op1=ALU.add), reads=[ubk, "cvec", tgk], writes=[tgk])
            gl = cv[s2][2]
            S.add("act", lambda e, gl=gl, s2=s2, gn=gn: e.activation(out=gl[:, 0:gn], in_=cv[s2][0][:, 0:gn],
                                                                      func=AF.Gelu_apprx_tanh),
                  reads=[("cv", s2, 0)], writes=[("cv", s2, 2)])
            S.add("pool", lambda e, gl=gl, s2=s2, gn=gn, j=j: e.tensor_tensor(
                out=actT[:, j, 0:gn], in0=gl[:, 0:gn], in1=cv[s2][1][:, 0:gn], op=ALU.mult),
                reads=[("cv", s2, 2), ("cv", s2, 1)], writes=[("actT", j)])

        def C2_iter(gi, oc):
            G = GROUPS[gi]
            t0r, nr = G["tiles"][-1]
            lo = t0r - G["start"]
            ak = [("actT", j) for j in range(NJ)]
            bk = oc % 2
            for pi, (k0, nk) in enumerate(((0, 16), (16, 16), (32, 12))):
                wd = stC.get(gi * 136 + 88 + oc * 3 + pi)
                proj(wd[0], wd[1], wd[2], [nk], [lambda k, k0=k0, lo=lo, nr=nr: actT[:, k0 + k, lo:lo + nr]],
                     ak, 128, bk, nr, None, first=(pi == 0), last=(pi == 2))
            S.add("act", lambda e, bk=bk, oc=oc, nr=nr: e.activation(out=ffT[:, oc, 0:nr], in_=bank(bk)[:, 0:nr], func=AF.Copy),
                  reads=[PS(bk)], writes=[("ffT", oc)])

        def C3_block(gi, bi_):
            G = GROUPS[gi]
            t0r, nr = G["tiles"][-1]
            rblocks = [b for b in G["blocks"] if b > 0]
            b = rblocks[bi_]
            t0, n = BLK[b]
            lo2 = t0 - t0r
            fk = [("ffT", oc) for oc in range(16)]
            half = 4
            pst = psB
            for cg in range(4):
                def tr(e, cg=cg, lo2=lo2, n=n, half=half):
                    ins = None
                    for i in range(4):
                        oc = cg * 4 + i
                        ins = e.transpose(out=bank(half + cg)[0:n, i * 128:(i + 1) * 128], in_=ffT[:, oc, lo2:lo2 + n],
                                          identity=identf)
                    return ins
                S.add("pe", tr, reads=fk + ["consts"], writes=[PS(half + cg)])
            pkeys = [PS(half + i) for i in range(4)]
            rc = r1c[bi_ % 2]
            rck = ("r1c", bi_ % 2)
            S.add("sp", lambda e, rc=rc, t0=t0, n=n: e.dma_start(out=rc[0:n, :], in_=r1s[t0:t0 + n, :]),
                  reads=[("r1s", b)], writes=[rck], dsem=S.dsem(f"r1c{bi_ % 2}"))
            ss = stat[0:n, b:b + 1]
            S.add("act", lambda e, pst=pst, n=n, ss=ss: e.activation(out=junkC[0:n, :], in_=pst[0:n, :], func=AF.Square,
                                                                     accum_out=ss),
                  reads=pkeys, writes=["junkC", ("ss", b)])
            rtt = stat[0:n, 20 + b:21 + b]
            rs = stat[0:n, 40 + b:41 + b]
            S.add("act", lambda e, rtt=rtt, ss=ss, n=n: e.activation(out=rtt, in_=ss, func=AF.Sqrt, scale=1.0 / D,
                                                                     bias=eps_t[0:n, :]),
                  reads=[("ss", b), "consts"], writes=[("rt", b)])
            S.add("dve", lambda e, rs=rs, rtt=rtt: e.reciprocal(out=rs, in_=rtt), reads=[("rt", b)], writes=[("rs", b)])
            o_ = ot[bi_ % 2]
            ok_ = ("ot", bi_ % 2)
            S.add("dve", lambda e, pst=pst, n=n, rs=rs, o_=o_: e.scalar_tensor_tensor(
                out=o_[0:n, :], in0=pst[0:n, :], scalar=rs, in1=gpost2[0:n, :], op0=ALU.mult, op1=ALU.mult),
                reads=pkeys + [("rs", b), "gpost2"], writes=[ok_])
            S.add("pool", lambda e, n=n, o_=o_, rc=rc: e.tensor_tensor(out=o_[0:n, :], in0=o_[0:n, :], in1=rc[0:n, :],
                                                                        op=ALU.add),
                  reads=[ok_, rck], writes=[ok_])
            S.add("sp", lambda e, n=n, o_=o_, t0=t0: e.dma_start(out=out[t0 - 16:t0 - 16 + n, :], in_=o_[0:n, :]),
                  reads=[ok_], dsem=S.dsem(f"ost{bi_ % 2}"))

        for bi_ in range(len(GROUPS[0]["blocks"])):
            C0_block(0, bi_, 0)
        for gi in range(4):
            c3_at = {4: 0, 12: 1, 20: 2, 28: 3} if gi > 0 else {}
            for j in range(NJ):
                C1_iter(gi, j)
                if j in c3_at:
                    C3_block(gi - 1, c3_at[j])
            c0f_at = {0: 0, 3: 1, 6: 2, 9: 3} if gi < 3 else {}
            c0b_at = {2: 0, 5: 1, 8: 2, 11: 3} if gi < 3 else {}
            for oc in range(16):
                C2_iter(gi, oc)
                if oc in c0f_at:
                    C0_front(gi + 1, c0f_at[oc])
                if oc in c0b_at:
                    C0_back(gi + 1, c0b_at[oc], 2)
        for bi_ in range(4):
            C3_block(3, bi_)
    S.emit()
    return nc, S, A


def host_layout(inputs):
    f32 = np.float32
    fm = lambda v: np.ascontiguousarray(np.asarray(v, f32).reshape(-1, 128).T)
    b_in = np.asarray(inputs["b_in"], f32)[0]
    cvec = np.zeros((128, NCV), f32)
    cvec[:, C_G1:C_G1 + 16] = fm(inputs["mix_pre_g"][0])
    cvec[:, C_BQ:C_BQ + 8] = fm(b_in[0:1024])
    cvec[:, C_BK:C_BK + 8] = fm(b_in[1024:2048])
    cvec[:, C_BV:C_BV + 8] = fm(b_in[2048:3072])
    cvec[:, C_BU:C_BU + 8] = fm(b_in[3080:4104])
    cvec[:, C_BGA:C_BGA + 16] = fm(b_in[4104:6152])
    cvec[:, C_BGP:C_BGP + 16] = fm(b_in[6152:8200])
    cvec[:, C_PSC:C_PSC + 8] = fm(inputs["pool_scale"][0])
    cvec[:, C_G2:C_G2 + 16] = fm(inputs["ffn_pre_g"][0])
    cvec[:, C_CB:C_CB + 88] = fm(inputs["ffn_conv_b"][0])
    cw = np.asarray(inputs["ffn_conv_w"], f32)[0]
    cvec[:, C_CW0:C_CW0 + 88] = fm(cw[0])
    cvec[:, C_CW1:C_CW1 + 88] = fm(cw[1])
    cvec[:, C_CW2:C_CW2 + 88] = fm(cw[2])
    cvec[0:8, C_BF] = b_in[3072:3080]
    rowv = np.stack([np.asarray(inputs["mix_post_g"], f32)[0], np.asarray(inputs["ffn_post_g"], f32)[0]])
    consts = np.zeros((128, NKC), f32)
    consts[:, K_ID:K_ID + 128] = np.eye(128, dtype=f32)
    p = np.arange(128)[:, None]
    j = np.arange(128)[None, :]
    consts[:, K_MASK:K_MASK + 128] = np.where(j >= p, 0.0, -30000.0)
    for wi, w in enumerate(POOL_WINDOWS):
        for t in range(16):
            consts[:, K_RCNT + wi * 16 + t] = 1.0 / min(t + 1, w)
    consts[:, K_ONES:K_ONES + 128] = 1.0
    consts[:, K_EPS] = EPS
    return cvec, np.ascontiguousarray(rowv), consts


_CACHE = {}


def kernel(**inputs):
    f32 = np.float32
    x = np.asarray(inputs["x"], f32)
    B = x.shape[0]
    cvec, rowv, consts = host_layout(inputs)
    if "nc" not in _CACHE:
        _CACHE["nc"] = build_program()[0]
    nc = _CACHE["nc"]
    def slabs(w):
        K_, N_ = w.shape
        t = w.reshape(K_ // 128, 128, N_ // 128, 128).transpose(2, 1, 0, 3)
        return np.ascontiguousarray(t).reshape(N_ // 128 * 128, K_)

    win = np.asarray(inputs["w_in"], f32)[0]
    win_main = np.concatenate([win[:, 0:3072], win[:, 3080:8200]], axis=1)
    wf = np.ascontiguousarray(win[:, 3072:3080].reshape(16, 128, 8).transpose(1, 0, 2)).reshape(128, 128)
    wao = slabs(np.asarray(inputs["w_attn_o"], f32)[0])
    wpo = slabs(np.asarray(inputs["w_pool_o"], f32)[0])
    pw = np.asarray(inputs["pool_w"], f32)[0].reshape(4, 2, 128, 256).transpose(2, 0, 1, 3)
    shared = {
        "meta": np.ascontiguousarray(np.asarray(inputs["meta_tokens"], f32)),
        "w_in": slabs(win_main),
        "w_f": wf,
        "w_ap": np.ascontiguousarray(np.concatenate([wao, wpo], axis=1)),
        "pool_w": np.ascontiguousarray(pw).reshape(128, 2048),
        "w_out": slabs(np.asarray(inputs["w_out"], f32)[0]),
        "w_up": slabs(np.asarray(inputs["w_ffn_up"], f32)[0]),
        "w_down": slabs(np.asarray(inputs["w_ffn_down"], f32)[0]),
        "cvec": cvec, "rowv": rowv, "consts": consts,
    }
    in_maps = []
    for b in range(B):
        m = dict(shared)
        m["x"] = np.ascontiguousarray(x[b])
        in_maps.append(m)
    res = run_bass_kernel_spmd(nc, in_maps, core_ids=list(range(B)))
    return np.stack([np.asarray(r["out"], f32) for r in res.results], axis=0)
```

```python
import numpy as np
import concourse.bass as bass
import concourse.mybir as mybir
from concourse.bass_utils import run_bass_kernel_spmd

F32 = mybir.dt.float32
BF16 = mybir.dt.bfloat16
AF = mybir.ActivationFunctionType
ALU = mybir.AluOpType

D = 2048
SEQ = 2048
NMETA = 16
L = SEQ + NMETA
DIN = 8200
DFF = 5632
NJ = DFF // 128
EPS = 1e-6
QSCALE = 128 ** -0.5
POOL_WINDOWS = (2, 4, 8, 16)

BLK = [(0, 16)] + [(16 + 128 * i, 128) for i in range(16)]
TILES = [(0, 16)] + [(16 + 512 * i, 512) for i in range(4)]
GROUPS = [dict(start=0, n=528, blocks=list(range(0, 5)), tiles=[(0, 16), (16, 512)])]
for _g in range(1, 4):
    GROUPS.append(dict(start=16 + 512 * _g, n=512, blocks=list(range(1 + 4 * _g, 5 + 4 * _g)),
                       tiles=[(16 + 512 * _g, 512)]))

C_G1, C_BQ, C_BK, C_BV, C_BU, C_BGA, C_BGP, C_PSC, C_G2 = 0, 16, 24, 32, 40, 48, 64, 80, 88
C_CB, C_CW0, C_CW1, C_CW2, C_BF, NCV = 104, 192, 280, 368, 456, 457
K_ID, K_MASK, K_RCNT, K_ONES, K_EPS, NKC = 0, 128, 256, 320, 448, 449


class _Op:
    __slots__ = ("eng", "fn", "deps", "dsem", "ticket", "observed", "idx", "waits")


class Sched:
    def __init__(self, nc):
        self.nc = nc
        self.ops = []
        self.lastw = {}
        self.readers = {}
        self.dma_tot = {}
        self.esem = {}
        self._dsems = {}
        self.bar_op = None
        self.last_on = {}
        self.bg_sems = set()

    def dsem(self, name):
        if name not in self._dsems:
            self._dsems[name] = self.nc.alloc_semaphore("d_" + name)
            self.dma_tot[self._dsems[name]] = 0
        return self._dsems[name]

    def add(self, eng, fn, reads=(), writes=(), dsem=None):
        op = _Op()
        op.eng, op.fn, op.dsem = eng, fn, dsem
        op.idx = len(self.ops)
        op.observed = False
        deps = {}
        for k in reads:
            w = self.lastw.get(k)
            if w is not None:
                deps[w] = deps.get(w, 0) | 1
        for k in writes:
            w = self.lastw.get(k)
            if w is not None:
                deps[w] = deps.get(w, 0) | 1
            for r in self.readers.get(k, ()):
                deps[r] = deps.get(r, 0) | 2
        op.deps = []
        for d, kind in deps.items():
            dop = self.ops[d]
            if dop.dsem is not None:
                op.deps.append(("dma", dop.dsem, self.dma_tot[dop.dsem]))
            else:
                if dop.eng == eng and eng == "pe":
                    continue
                op.deps.append(("eng", d))
        if self.bar_op is not None and eng != "dve":
            op.deps.append(("eng", self.bar_op))
        if dsem is not None:
            self.dma_tot[dsem] += 16
        else:
            self.last_on[eng] = op.idx
        for k in reads:
            self.readers.setdefault(k, []).append(op.idx)
        for k in writes:
            self.lastw[k] = op.idx
            self.readers[k] = []
        self.ops.append(op)
        return op

    def barrier(self, scratch):
        op = _Op()
        op.eng, op.dsem = "dve", None
        op.fn = lambda e: e.memset(scratch, 0.0)
        op.idx = len(self.ops)
        op.observed = False
        op.deps = [("eng", i) for e, i in self.last_on.items() if e != "dve"]
        op.deps += [("dma", s, t) for s, t in self.dma_tot.items() if t > 0 and s not in self.bg_sems]
        self.ops.append(op)
        self.bar_op = op.idx
        self.last_on["dve"] = op.idx
        self.lastw = {k: v for k, v in self.lastw.items() if isinstance(k, tuple) and k[0] == "wsc"}
        self.readers = {}

    def emit(self):
        nc = self.nc
        for o in self.ops:
            for d in o.deps:
                if d[0] == "eng":
                    self.ops[d[1]].observed = True
        cnt = {}
        for o in self.ops:
            if o.dsem is None and o.observed:
                cnt[o.eng] = cnt.get(o.eng, 0) + 1
                o.ticket = cnt[o.eng]
        for e in cnt:
            self.esem[e] = nc.alloc_semaphore("e_" + e)
        for o in self.ops:
            w = {}
            for d in o.deps:
                if d[0] == "dma":
                    sem, val = d[1], d[2]
                else:
                    dop = self.ops[d[1]]
                    sem, val = self.esem[dop.eng], dop.ticket
                if w.get(sem, 0) < val:
                    w[sem] = val
            o.waits = w
        self.stats = dict(cnt)
        with nc.Block() as block:
            for engname, deco in (("sp", block.sync), ("act", block.scalar), ("pe", block.tensor),
                                  ("dve", block.vector), ("pool", block.gpsimd)):
                ops = [o for o in self.ops if o.eng == engname]

                def body(e, ops=ops, engname=engname):
                    waited = {}
                    for o in ops:
                        for sem, val in o.waits.items():
                            if waited.get(sem, 0) < val:
                                e.wait_ge(sem, val)
                                waited[sem] = val
                        ins = o.fn(e)
                        if o.dsem is not None:
                            ins.then_inc(o.dsem, 16)
                        elif o.observed:
                            ins.then_inc(self.esem[o.eng], 1)
                    if engname == "sp":
                        for sem, tot in self.dma_tot.items():
                            if tot > 0:
                                e.wait_ge(sem, tot)

                deco(body)


class Arena:
    def __init__(self, nc):
        self.nc = nc
        self.base = (nc.sbuf_base + 63) // 64 * 64
        self.top = nc.sbuf_top
        self.cur = self.base
        self.n = 0
        self.peak = 0

    def alloc(self, name, shape, dtype):
        esz = 2 if dtype == BF16 else 4
        size = esz
        for s in shape[1:]:
            size *= s
        off = (self.cur + 63) // 64 * 64
        assert off + size <= self.top, f"SBUF overflow allocating {name}: need {off + size - self.top} more bytes"
        self.cur = off + size
        self.peak = max(self.peak, self.cur)
        self.n += 1
        self.last_off = off
        return self.nc.alloc_sbuf_tensor_at(f"{name}_{self.n}", list(shape), dtype, offset=off)

    def mark(self):
        return self.cur

    def reset(self, m):
        self.cur = m


def build_program(stop=None, debug=False):
    nc = bass.Bass("TRN2", target_bir_lowering=False)
    dt = lambda name, shape, kind, dtp=F32: nc.dram_tensor(name, list(shape), dtp, kind=kind).ap()
    x = dt("x", [SEQ, D], "ExternalInput")
    meta = dt("meta", [NMETA, D], "ExternalInput")
    w_in = dt("w_in", [64 * 128, 2048], "ExternalInput")
    w_f = dt("w_f", [128, 128], "ExternalInput")
    w_ap = dt("w_ap", [16 * 128, 2048], "ExternalInput")
    pool_w = dt("pool_w", [128, 2048], "ExternalInput")
    w_out = dt("w_out", [16 * 128, 2048], "ExternalInput")
    w_up = dt("w_up", [88 * 128, 2048], "ExternalInput")
    w_down = dt("w_down", [16 * 128, DFF], "ExternalInput")
    cvec_d = dt("cvec", [128, NCV], "ExternalInput")
    rowv_d = dt("rowv", [2, D], "ExternalInput")
    consts_d = dt("consts", [128, NKC], "ExternalInput")
    out = dt("out", [SEQ, D], "ExternalOutput")
    r1s = dt("r1s", [L, D], "Internal")
    wscB = dt("wscB", [64 * 128, 2048], "Internal", BF16)
    wscC = dt("wscC", [136 * 128, 2048], "Internal", BF16)
    dbg = {}
    if debug:
        dbg["attnT"] = dt("dbg_attnT", [128, 8, L], "ExternalOutput", BF16)
        dbg["poolT"] = dt("dbg_poolT", [128, 8, L], "ExternalOutput", BF16)
        dbg["hT"] = dt("dbg_hT", [128, 16, L], "ExternalOutput", BF16)
        dbg["c8"] = dt("dbg_c8", [8, L], "ExternalOutput")

    S = Sched(nc)
    A = Arena(nc)
    psA = nc.alloc_psum_tensor("psA", [128, 2048], F32)
    psB = nc.alloc_psum_tensor("psB", [128, 2048], F32)

    def bank(i):
        t = psA if i < 4 else psB
        return t[:, (i % 4) * 512:(i % 4 + 1) * 512]

    PS = lambda i: ("ps", i)

    cvec = A.alloc("cvec", [128, NCV], F32)
    consts = A.alloc("consts", [128, NKC], F32)
    ones_bf = A.alloc("ones_bf", [128, 128], BF16)
    stat = A.alloc("stat", [128, 64], F32)
    bar_scr = A.alloc("bar", [128, 8], F32)
    identf = consts[:, K_ID:K_ID + 128]
    maskf = consts[:, K_MASK:K_MASK + 128]
    onesf = consts[:, K_ONES:K_ONES + 128]
    eps_t = consts[:, K_EPS:K_EPS + 1]

    S.add("sp", lambda e: e.dma_start(out=cvec[:], in_=cvec_d[:, :]), writes=["cvec"], dsem=S.dsem("cvec"))
    S.add("sp", lambda e: e.dma_start(out=consts[:], in_=consts_d[:, :]), writes=["consts"], dsem=S.dsem("consts"))
    S.add("dve", lambda e: e.tensor_copy(out=ones_bf[:], in_=onesf), reads=["consts"], writes=["ones_bf"])
    bqs = A.alloc("bqs", [128, 8], F32)
    S.add("dve", lambda e: e.tensor_scalar(out=bqs[:], in0=cvec[:, C_BQ:C_BQ + 8], scalar1=QSCALE, scalar2=None,
                                           op0=ALU.mult), reads=["cvec"], writes=["bqs"])

    def slabsrc(w, j, nk, ncol, k0=0):
        return w[j * 128:(j + 1) * 128, k0 * ncol:(k0 + nk) * ncol]

    bgB = []
    for c in range(16):
        bgB += [slabsrc(w_in, 32 + c, 16, 128), slabsrc(w_in, 48 + c, 16, 128), slabsrc(w_ap, c, 16, 128)]
    for oc in range(16):
        bgB.append(slabsrc(w_out, oc, 16, 128))
    bgC = []
    for j in range(NJ):
        bgC += [slabsrc(w_up, j, 16, 128), slabsrc(w_up, NJ + j, 16, 128)]
    for oc in range(16):
        for (k0, nk) in ((0, 16), (16, 16), (32, 12)):
            bgC.append(slabsrc(w_down, oc, nk, 128, k0))
    class Ring:
        def __init__(self, nf, nb):
            self.f = []
            self.f_off = None
            for i in range(nf):
                self.f.append(A.alloc(f"wf{i}", [128, 2048], F32))
                if i == 0:
                    self.f_off = A.last_off
            self.b = [A.alloc(f"wb{i}", [128, 2048], BF16) for i in range(nb)]
            self.fi = 0
            self.bi = 0
            self.ci = 0
            self.cast_engs = ("dve", "act")

    ring = [None]

    def load_slab(parts, cache=None):
        R = ring[0]
        offs = []
        off = 0
        for (ap3, k, w) in parts:
            offs.append(off)
            off += k * w
        bi = R.bi % len(R.b)
        R.bi += 1
        wb = R.b[bi]
        if cache is not None and cache[2] == "load":
            sc, cid = cache[0], cache[1]
            S.add("sp", lambda e, o=off: e.dma_start(out=wb[:, 0:o], in_=sc[cid * 128:(cid + 1) * 128, 0:o]),
                  reads=[("wsc", cache[3], cid)], writes=[("wb", bi)], dsem=S.dsem(f"wbl{bi}"))
            return wb, ("wb", bi), offs
        fi = R.fi % len(R.f)
        R.fi += 1
        wf = R.f[fi]
        for (ap3, k, w), o0 in zip(parts, offs):
            dst = wf[:, o0:o0 + k * w]
            S.add("sp", lambda e, dst=dst, ap3=ap3: e.dma_start(out=dst, in_=ap3),
                  writes=[("wf", fi)], dsem=S.dsem(f"wf{fi}"))
        ceng = R.cast_engs[R.ci % len(R.cast_engs)]
        R.ci += 1
        if ceng == "dve":
            S.add("dve", lambda e, o=off: e.tensor_copy(out=wb[:, 0:o], in_=wf[:, 0:o]),
                  reads=[("wf", fi)], writes=[("wb", bi)])
        else:
            S.add("act", lambda e, o=off: e.activation(out=wb[:, 0:o], in_=wf[:, 0:o], func=AF.Copy),
                  reads=[("wf", fi)], writes=[("wb", bi)])
        if cache is not None and cache[2] == "store":
            sc, cid = cache[0], cache[1]
            S.add("pool", lambda e, o=off: e.dma_start(out=sc[cid * 128:(cid + 1) * 128, 0:o], in_=wb[:, 0:o]),
                  reads=[("wb", bi)], writes=[("wsc", cid)], dsem=S.dsem(f"wst{bi}"))
        return wb, ("wb", bi), offs

    def slabv(w, j, nk, ncol, k0=0):
        return (w[j * 128:(j + 1) * 128, k0 * ncol:(k0 + nk) * ncol], nk, ncol)

    class Stream:
        def __init__(self, reqs, pf):
            self.reqs, self.pf, self.nxt, self.loaded = reqs, pf, 0, {}

        def get(self, i):
            hi = min(i + self.pf, len(self.reqs) - 1)
            while self.nxt <= hi:
                r_ = self.reqs[self.nxt]
                self.loaded[self.nxt] = load_slab(*r_) if isinstance(r_, tuple) else load_slab(r_)
                self.nxt += 1
            return self.loaded.pop(i)

    def rms_rows(src_tile, n, col, key_in, junk, junk_key):
        ss = stat[0:n, col:col + 1]
        rt = stat[0:n, 20 + col:21 + col]
        rs = stat[0:n, 40 + col:41 + col]
        S.add("act", lambda e: e.activation(out=junk[0:n, :], in_=src_tile, func=AF.Square, accum_out=ss),
              reads=[key_in], writes=[junk_key, ("ss", col)])
        S.add("act", lambda e: e.activation(out=rt, in_=ss, func=AF.Sqrt, scale=1.0 / D, bias=eps_t[0:n, :]),
              reads=[("ss", col), "consts"], writes=[("rt", col)])
        S.add("dve", lambda e: e.reciprocal(out=rs, in_=rt), reads=[("rt", col)], writes=[("rs", col)])
        return rs, ("rs", col)

    def to_feature_major(tile, n, tile_key, dstT, dst_key, col0, gcol, bankbase, evac_engs=("dve", "dve")):
        for cg in range(4):
            bk = bankbase + (cg % 2)

            def tr(e, cg=cg, bk=bk):
                ins = None
                for i in range(4):
                    c = cg * 4 + i
                    ins = e.transpose(out=bank(bk)[:, i * 128:i * 128 + n], in_=tile[0:n, c * 128:(c + 1) * 128],
                                      identity=identf[0:n, 0:n])
                return ins

            S.add("pe", tr, reads=[tile_key, "consts"], writes=[PS(bk)])
            for i in range(4):
                c = cg * 4 + i
                eng = evac_engs[i % 2]
                if eng == "dve":
                    S.add("dve", lambda e, c=c, i=i, bk=bk: e.tensor_scalar(
                        out=dstT[:, c, col0:col0 + n], in0=bank(bk)[:, i * 128:i * 128 + n],
                        scalar1=cvec[:, gcol + c:gcol + c + 1], scalar2=None, op0=ALU.mult),
                        reads=[PS(bk), "cvec"], writes=[dst_key])
                else:
                    S.add("act", lambda e, c=c, i=i, bk=bk: e.activation(
                        out=dstT[:, c, col0:col0 + n], in_=bank(bk)[:, i * 128:i * 128 + n], func=AF.Copy,
                        scale=cvec[:, gcol + c:gcol + c + 1]),
                        reads=[PS(bk), "cvec"], writes=[dst_key])

    def proj(wb, wkey, woffs, nk_list, ins_list, in_keys, m, bk, n, col_lists, first=True, last=True):
        def fn(e):
            ins = None
            tot = sum(nk_list)
            cnt = 0
            for wi, nk in enumerate(nk_list):
                for k in range(nk):
                    o = woffs[wi] + k * m
                    ins = e.matmul(bank(bk)[0:m, 0:n], lhsT=wb[:, o:o + m], rhs=ins_list[wi](k),
                                   start=(first and cnt == 0), stop=(last and cnt == tot - 1))
                    cnt += 1
            return ins
        S.add("pe", fn, reads=[wkey] + list(in_keys), writes=[PS(bk)])

    mA = A.mark()
    attnT = A.alloc("attnT", [128, 8, L], BF16)
    poolT = A.alloc("poolT", [128, 8, L], BF16)
    mA1 = A.mark()
    hT = A.alloc("hT", [128, 16, L], BF16)
    ring[0] = Ring(2, 3)
    R = ring[0]
    R.cast_engs = ("act",)
    reqA = [[slabv(w_f, 0, 16, 8)]]
    for h in range(8):
        for c0 in (0, 1024, 2048):
            reqA.append([slabv(w_in, (c0 // 1024) * 8 + h, 16, 128)])
    for c in range(8):
        reqA.append([slabv(w_in, 24 + c, 16, 128)])
    stA = Stream(reqA, 2)
    mA2 = A.mark()
    c8 = A.alloc("c8", [8, L], F32)
    negc = A.alloc("negc", [128, 17, 8], F32)
    mA2b = A.mark()

    for b, (t0, n) in enumerate(BLK):
        xt = R.f[b % 2]
        xk = ("wf", b % 2)
        src = meta[0:16, :] if b == 0 else x[t0 - 16:t0 - 16 + n, :]
        S.add("sp", lambda e, xt=xt, src=src, n=n: e.dma_start(out=xt[0:n, :], in_=src),
              writes=[xk], dsem=S.dsem(f"wf{b % 2}"))
        rs, rsk = rms_rows(xt[0:n, :], n, b, xk, R.b[0], ("wb", 0))
        S.add("act", lambda e, xt=xt, n=n, rs=rs: e.activation(out=xt[0:n, :], in_=xt[0:n, :], func=AF.Copy, scale=rs),
              reads=[xk, rsk], writes=[xk])
        to_feature_major(xt, n, xk, hT, ("hT", b), t0, C_G1, 0)
    HT_ALL = [("hT", b) for b in range(17)]

    def hT_keys(t0, n):
        return [("hT", b) for b, (b0, bn) in enumerate(BLK) if b0 < t0 + n and b0 + bn > t0]

    if debug:
        S.add("sp", lambda e: e.dma_start(out=dbg["hT"][:, :, :], in_=hT[:]), reads=HT_ALL, dsem=S.dsem("dbg"))

    wb, wk, wo = stA.get(0)
    lt = [A.alloc(f"lt{i}", [8, L], F32) for i in range(4)]
    for ti, (t0, n) in enumerate(TILES):
        bk = ti % 2
        proj(wb, wk, wo, [16], [lambda k, t0=t0, n=n: hT[:, k, t0:t0 + n]], hT_keys(t0, n), 8, bk, n, None)
        S.add("dve", lambda e, bk=bk, t0=t0, n=n: e.tensor_scalar(
            out=lt[0][:, t0:t0 + n], in0=bank(bk)[0:8, 0:n], scalar1=cvec[0:8, C_BF:C_BF + 1], scalar2=None,
            op0=ALU.add), reads=[PS(bk), "cvec"], writes=["lt0"])
    tf, ta, tb_, tc = lt
    V = lambda eng, fn, r, w: S.add(eng, fn, reads=r, writes=w)
    V("act", lambda e: e.activation(out=ta[:], in_=tf[:], func=AF.Abs), ["lt0"], ["lt1"])
    V("act", lambda e: e.activation(out=ta[:], in_=ta[:], func=AF.Exp, scale=-1.0), ["lt1"], ["lt1"])
    V("dve", lambda e: e.tensor_scalar(out=tb_[:], in0=ta[:], scalar1=2.0, scalar2=None, op0=ALU.add), ["lt1"], ["lt2"])
    V("dve", lambda e: e.reciprocal(out=tb_[:], in_=tb_[:]), ["lt2"], ["lt2"])
    V("dve", lambda e: e.tensor_tensor(out=ta[:], in0=ta[:], in1=tb_[:], op=ALU.mult), ["lt1", "lt2"], ["lt1"])
    V("dve", lambda e: e.tensor_tensor(out=tb_[:], in0=ta[:], in1=ta[:], op=ALU.mult), ["lt1"], ["lt2"])
    V("dve", lambda e: e.tensor_scalar(out=tc[:], in0=tb_[:], scalar1=1.0 / 9.0, scalar2=None, op0=ALU.mult), ["lt2"], ["lt3"])
    for cst in (1.0 / 7.0, 1.0 / 5.0, 1.0 / 3.0):
        V("dve", lambda e, cst=cst: e.scalar_tensor_tensor(out=tc[:], in0=tc[:], scalar=cst, in1=tb_[:],
                                                           op0=ALU.add, op1=ALU.mult), ["lt3", "lt2"], ["lt3"])
    V("dve", lambda e: e.scalar_tensor_tensor(out=tc[:], in0=tc[:], scalar=1.0, in1=ta[:], op0=ALU.add, op1=ALU.mult),
      ["lt3", "lt1"], ["lt3"])
    V("dve", lambda e: e.tensor_scalar(out=ta[:], in0=tf[:], scalar1=0.0, scalar2=None, op0=ALU.min), ["lt0", "lt1"], ["lt1"])
    V("dve", lambda e: e.scalar_tensor_tensor(out=tb_[:], in0=tc[:], scalar=-2.0, in1=ta[:], op0=ALU.mult, op1=ALU.add),
      ["lt3", "lt1", "lt2"], ["lt2"])
    V("dve", lambda e: e.memset(tc[:], 1.0), ["lt3"], ["lt3"])
    V("dve", lambda e: e.tensor_tensor_scan(out=c8[:], data0=tc[:], data1=tb_[:], initial=0.0, op0=ALU.mult, op1=ALU.add),
      ["lt3", "lt2"], ["c8"])
    for b, (t0, n) in enumerate(BLK):
        bk = b % 2
        S.add("pe", lambda e, bk=bk, t0=t0, n=n: e.transpose(out=bank(bk)[0:n, 0:8], in_=c8[0:8, t0:t0 + n],
                                                             identity=identf[0:8, 0:8]),
              reads=["c8", "consts"], writes=[PS(bk)])
        S.add("dve", lambda e, bk=bk, b=b, n=n: e.tensor_scalar(out=negc[0:n, b, :], in0=bank(bk)[0:n, 0:8], scalar1=-1.0,
                                                                scalar2=None, op0=ALU.mult),
              reads=[PS(bk)], writes=["negc"])
    if stop not in ("A0", "A"):
        nbg = 0
        for nm, scr, lst in (("B", wscB, bgB), ("C", wscC, bgC)):
            for cid, src in enumerate(lst):
                ncol = src.shape[1]
                sem = S.dsem(f"bg{nbg // 8}")
                S.bg_sems.add(sem)
                nbg += 1
                S.add("pool", lambda e, scr=scr, cid=cid, src=src, ncol=ncol: e.dma_start(
                    out=scr[cid * 128:(cid + 1) * 128, 0:ncol], in_=src), reads=["c8"], writes=[("wsc", nm, cid)], dsem=sem)

    if debug:
        S.add("sp", lambda e: e.dma_start(out=dbg["c8"][:, :], in_=c8[:]), reads=["c8"], dsem=S.dsem("dbg"))
    S.barrier(bar_scr[0:1, 0:1])
    A.reset(mA2b)

    if stop != "A0":
        qT = A.alloc("qT", [128, L], BF16)
        kT = A.alloc("kT", [128, L], BF16)
        vTf = A.alloc("vTf", [128, L], F32)
        Vtm = A.alloc("Vtm", [128, 17, 128], BF16)
        cq = [A.alloc(f"cq{i}", [128, 512], F32) for i in range(2)]
        c8h = A.alloc("c8h", [8, 512], F32)
        PT = [A.alloc(f"PT_{i}", [128, 512], BF16) for i in range(5)]
        SBANK = (2, 3, 6, 5, 0)
        NSD = len(SBANK)
        rden = A.alloc("rden", [128, 512], F32)
        for h in range(8):
            for which, c0, dstname in (("q", 0, "qT"), ("k", 1024, "kT"), ("v", 2048, "vTf")):
                wb, wk, wo = stA.get(1 + h * 3 + (c0 // 1024))
                for ti, (t0, n) in enumerate(TILES):
                    bk = ti % 2
                    proj(wb, wk, wo, [16], [lambda k, t0=t0, n=n: hT[:, k, t0:t0 + n]], hT_keys(t0, n), 128, bk, n, None)
                    if which == "q":
                        S.add("act", lambda e, bk=bk, t0=t0, n=n, h=h: e.activation(
                            out=qT[:, t0:t0 + n], in_=bank(bk)[:, 0:n], func=AF.Identity, scale=QSCALE,
                            bias=bqs[:, h:h + 1]), reads=[PS(bk), "bqs"], writes=[("qT", ti)])
                    elif which == "k":
                        S.add("act", lambda e, bk=bk, t0=t0, n=n, h=h: e.activation(
                            out=kT[:, t0:t0 + n], in_=bank(bk)[:, 0:n], func=AF.Identity,
                            bias=cvec[:, C_BK + h:C_BK + h + 1]), reads=[PS(bk), "cvec"], writes=[("kT", ti)])
                    else:
                        S.add("act", lambda e, bk=bk, t0=t0, n=n, h=h: e.activation(
                            out=vTf[:, t0:t0 + n], in_=bank(bk)[:, 0:n], func=AF.Identity,
                            bias=cvec[:, C_BV + h:C_BV + h + 1]), reads=[PS(bk), "cvec"], writes=[("vTf", ti)])
            for b, (t0, n) in enumerate(BLK):
                bk = b % 2
                ti = 0 if b == 0 else 1 + (b - 1) // 4
                S.add("pe", lambda e, bk=bk, t0=t0, n=n: e.transpose(out=bank(bk)[0:n, 0:128], in_=vTf[:, t0:t0 + n],
                                                                     identity=identf),
                      reads=[("vTf", ti), "consts"], writes=[PS(bk)])
                S.add("act", lambda e, bk=bk, b=b, n=n: e.activation(out=Vtm[0:n, b, :], in_=bank(bk)[0:n, 0:128], func=AF.Copy),
                      reads=[PS(bk)], writes=[("Vtm", b)])
            blocks = []
            for ti, (t0, n) in enumerate(TILES):
                kbs = [(b, k0, kn) for b, (k0, kn) in enumerate(BLK) if k0 < t0 + n]
                for bi_, (b, k0, kn) in enumerate(kbs):
                    qlo = max(t0, k0)
                    blocks.append(dict(ti=ti, t0=t0, n=n, b=b, k0=k0, kn=kn, qlo=qlo, N=t0 + n - qlo, off=qlo - t0,
                                       diag=k0 >= t0, first=bi_ == 0, last=bi_ == len(kbs) - 1,
                                       kti=0 if b == 0 else 1 + (b - 1) // 4))

            def emit_cq(ti, h=h):
                t0, n = TILES[ti]
                cqt = cq[ti % 2]
                S.add("dve", lambda e, t0=t0, n=n, h=h: e.tensor_scalar(
                    out=c8h[:, 0:n], in0=c8[:, t0:t0 + n], scalar1=identf[0:8, h:h + 1], scalar2=None, op0=ALU.mult),
                    reads=["c8", "consts"], writes=["c8h"])
                S.add("pe", lambda e, n=n, ti=ti: e.matmul(bank(1)[:, 0:n], lhsT=onesf[0:8, :], rhs=c8h[:, 0:n],
                                                           start=True, stop=True),
                      reads=["c8h", "consts"], writes=[PS(1)])
                S.add("act", lambda e, n=n, ti=ti, cqt=cqt: e.activation(out=cqt[:, 0:n], in_=bank(1)[:, 0:n], func=AF.Copy),
                      reads=[PS(1)], writes=[("cq", ti % 2)])

            def emit_S(idx):
                B_ = blocks[idx]
                sb = SBANK[idx % NSD]
                S.add("pe", lambda e, sb=sb, k0=B_["k0"], kn=B_["kn"], qlo=B_["qlo"], N=B_["N"]: e.matmul(
                    bank(sb)[0:kn, 0:N], lhsT=kT[:, k0:k0 + kn], rhs=qT[:, qlo:qlo + N], start=True, stop=True),
                    reads=[("kT", B_["kti"]), ("qT", B_["ti"])], writes=[PS(sb)])

            emit_cq(0)
            for _i in range(min(NSD, len(blocks))):
                emit_S(_i)
            for idx, B_ in enumerate(blocks):
                ti, t0, n, b, kn, N, off = B_["ti"], B_["t0"], B_["n"], B_["b"], B_["kn"], B_["N"], B_["off"]
                ob, db = 4, 7
                sb = SBANK[idx % NSD]
                tb = idx % NSD
                cqt = cq[ti % 2]
                if B_["first"] and ti + 1 < len(TILES):
                    emit_cq(ti + 1)
                S.add("dve", lambda e, sb=sb, kn=kn, N=N, off=off, cqt=cqt: e.tensor_tensor(
                    out=bank(sb)[0:kn, 0:N], in0=bank(sb)[0:kn, 0:N], in1=cqt[0:kn, off:off + N], op=ALU.add),
                    reads=[PS(sb), ("cq", ti % 2)], writes=[PS(sb)])
                if B_["diag"]:
                    S.add("dve", lambda e, sb=sb, kn=kn: e.tensor_tensor(
                        out=bank(sb)[0:kn, 0:kn], in0=bank(sb)[0:kn, 0:kn], in1=maskf[0:kn, 0:kn], op=ALU.add),
                        reads=[PS(sb), "consts"], writes=[PS(sb)])
                S.add("act", lambda e, sb=sb, tb=tb, kn=kn, N=N, b=b, h=h: e.activation(
                    out=PT[tb][0:kn, 0:N], in_=bank(sb)[0:kn, 0:N], func=AF.Exp, bias=negc[0:kn, b, h:h + 1]),
                    reads=[PS(sb), "negc"], writes=[("PT", tb)])

                def pv(e, tb=tb, kn=kn, N=N, off=off, b=b, ob=ob, db=db, first=B_["first"], lastb=B_["last"]):
                    e.matmul(bank(ob)[:, off:off + N], lhsT=Vtm[0:kn, b, :], rhs=PT[tb][0:kn, 0:N],
                             start=first, stop=lastb)
                    return e.matmul(bank(db)[:, off:off + N], lhsT=ones_bf[0:kn, :], rhs=PT[tb][0:kn, 0:N],
                                    start=first, stop=lastb)
                S.add("pe", pv, reads=[("PT", tb), ("Vtm", b), "ones_bf"], writes=[PS(ob), PS(db)])
                if idx + NSD < len(blocks):
                    emit_S(idx + NSD)
                if B_["last"]:
                    S.add("dve", lambda e, db=db, n=n: e.reciprocal(out=rden[:, 0:n], in_=bank(db)[:, 0:n]),
                          reads=[PS(db)], writes=["rden"])
                    S.add("dve", lambda e, ob=ob, t0=t0, n=n, h=h: e.tensor_tensor(
                        out=attnT[:, h, t0:t0 + n], in0=bank(ob)[:, 0:n], in1=rden[:, 0:n], op=ALU.mult),
                        reads=[PS(ob), "rden"], writes=[("attnT", ti)])
        S.barrier(bar_scr[0:1, 0:1])
        A.reset(mA2)

        ub = [A.alloc(f"ub{i}", [128, 16 + L], F32) for i in range(2)]
        tA = A.alloc("tA", [128, 16 + L], F32)
        tB = A.alloc("tB", [128, 16 + L], F32)
        dT = A.alloc("dT", [128, 2, L], BF16)
        t16 = A.alloc("t16", [128, 16], F32)
        for i, tt in enumerate((ub[0], ub[1], tA, tB)):
            S.add("pool", lambda e, tt=tt: e.memset(tt[:, 0:16], 0.0), writes=[("pad", i)])
        pwb = A.alloc("pwb", [128, 2048], BF16)
        pwk, pwo = "pwb", [0]
        _fi = R.fi % len(R.f)
        R.fi += 1
        S.add("sp", lambda e: e.dma_start(out=R.f[_fi][:, :], in_=pool_w[:, :]),
              writes=[("wf", _fi)], dsem=S.dsem(f"wf{_fi}"))
        S.add("pool", lambda e: e.tensor_copy(out=pwb[:], in_=R.f[_fi][:]), reads=[("wf", _fi)], writes=["pwb"])
        for c in range(8):
            g = c // 2
            w = POOL_WINDOWS[g]
            u = ub[c % 2]
            uk = ("ub", c % 2)
            wb, wk, wo = stA.get(25 + c)
            for ti, (t0, n) in enumerate(TILES):
                bk = ti % 2
                proj(wb, wk, wo, [16], [lambda k, t0=t0, n=n: hT[:, k, t0:t0 + n]], hT_keys(t0, n), 128, bk, n, None)
                S.add("act", lambda e, bk=bk, t0=t0, n=n, c=c, u=u: e.activation(
                    out=u[:, 16 + t0:16 + t0 + n], in_=bank(bk)[:, 0:n], func=AF.Identity,
                    bias=cvec[:, C_BU + c:C_BU + c + 1]), reads=[PS(bk), "cvec", ("pad", c % 2)], writes=[uk])
            src, srck, srcpad = u, uk, ("pad", c % 2)
            sh = 1
            pp = [(tA, "tA", ("pad", 2)), (tB, "tB", ("pad", 3))]
            pi = 0
            while sh < w:
                dst, dk, dpad = pp[pi % 2]
                S.add("dve", lambda e, dst=dst, src=src, sh=sh: e.tensor_tensor(
                    out=dst[:, 16:16 + L], in0=src[:, 16:16 + L], in1=src[:, 16 - sh:16 - sh + L], op=ALU.add),
                    reads=[srck, srcpad], writes=[dk])
                src, srck, srcpad = dst, dk, dpad
                sh *= 2
                pi += 1
            wi = POOL_WINDOWS.index(w)
            S.add("dve", lambda e, src=src, u=u, w=w, c=c: e.scalar_tensor_tensor(
                out=dT[:, c % 2, 16:L], in0=src[:, 32:16 + L], scalar=1.0 / w, in1=u[:, 32:16 + L],
                op0=ALU.mult, op1=ALU.subtract), reads=[srck, uk], writes=[("dT", c % 2)])
            S.add("dve", lambda e, src=src, wi=wi: e.tensor_tensor(
                out=t16[:], in0=src[:, 16:32], in1=consts[:, K_RCNT + wi * 16:K_RCNT + wi * 16 + 16], op=ALU.mult),
                reads=[srck, "consts"], writes=["t16"])
            S.add("dve", lambda e, u=u, c=c: e.tensor_tensor(
                out=dT[:, c % 2, 0:16], in0=t16[:], in1=u[:, 16:32], op=ALU.subtract),
                reads=["t16", uk], writes=[("dT", c % 2)])
            if c % 2 == 1:
                for ocl in range(2):
                    oc = 2 * g + ocl
                    for ti, (t0, n) in enumerate(TILES):
                        bk = ti % 2

                        def fn(e, g=g, ocl=ocl, bk=bk, t0=t0, n=n):
                            ins = None
                            for kl in range(2):
                                o = pwo[0] + (g * 2 + kl) * 256 + ocl * 128
                                ins = e.matmul(bank(bk)[:, 0:n], lhsT=pwb[:, o:o + 128], rhs=dT[:, kl, t0:t0 + n],
                                               start=(kl == 0), stop=(kl == 1))
                            return ins
                        S.add("pe", fn, reads=[pwk, ("dT", 0), ("dT", 1)], writes=[PS(bk)])
                        S.add("act", lambda e, bk=bk, oc=oc, t0=t0, n=n: e.activation(
                            out=poolT[:, oc, t0:t0 + n], in_=bank(bk)[:, 0:n], func=AF.Copy,
                            scale=cvec[:, C_PSC + oc:C_PSC + oc + 1]), reads=[PS(bk), "cvec"], writes=[("poolT", ti)])
        if debug:
            S.add("sp", lambda e: e.dma_start(out=dbg["attnT"][:, :, :], in_=attnT[:]),
                  reads=[("attnT", i) for i in range(5)], dsem=S.dsem("dbg"))
            S.add("sp", lambda e: e.dma_start(out=dbg["poolT"][:, :, :], in_=poolT[:]),
                  reads=[("poolT", i) for i in range(5)], dsem=S.dsem("dbg"))
        S.barrier(bar_scr[0:1, 0:1])
    A.reset(mA1)

    if stop not in ("A0", "A"):
        gpost = A.alloc("gpost", [128, D], F32)
        S.add("sp", lambda e: e.dma_start(out=gpost[:], in_=rowv_d[0:1, :].partition_broadcast(128)),
              writes=["gpost"], dsem=S.dsem("gpost"))
        xtB = A.alloc("xtB", [128, D], F32)
        junk = A.alloc("junk", [128, D], BF16)
        r1t = [A.alloc(f"r1t{i}", [128, D], F32) for i in range(2)]
        mTg = A.alloc("mTg", [128, 16, 528], BF16)
        ring[0] = Ring(0, 5)
        reqB = []
        for _gi in range(4):
            for c in range(16):
                reqB.append(([slabv(w_in, 32 + c, 16, 128)], (wscB, c * 3, "load", "B")))
                reqB.append(([slabv(w_in, 48 + c, 16, 128)], (wscB, c * 3 + 1, "load", "B")))
                reqB.append(([slabv(w_ap, c, 16, 128)], (wscB, c * 3 + 2, "load", "B")))
            for oc in range(16):
                reqB.append(([slabv(w_out, oc, 16, 128)], (wscB, 48 + oc, "load", "B")))
        stB = Stream(reqB, 4)
        mixT = A.alloc("mixT", [128, 16, 528], F32)
        hTg = A.alloc("hTg", [128, 16, 528], BF16)
        gt = [[A.alloc(f"gt{i}_{j}", [128, 512], F32) for j in range(4)] for i in range(2)]
        rot = [(xtB, "xtB", "xtB"), (r1t[0], ("r1t", 0), "r1st0"), (r1t[1], ("r1t", 1), "r1st1")]

        def B0_block(gi, bi0, bankbase):
            G = GROUPS[gi]
            b = G["blocks"][bi0]
            t0, n = BLK[b]
            xt_, xk_, xs_ = rot[bi0 % 3]
            src = meta[0:16, :] if b == 0 else x[t0 - 16:t0 - 16 + n, :]
            S.add("sp", lambda e, src=src, n=n, xt_=xt_: e.dma_start(out=xt_[0:n, :], in_=src), writes=[xk_],
                  dsem=S.dsem(xs_))
            rs, rsk = rms_rows(xt_[0:n, :], n, b, xk_, junk, "junk")
            S.add("act", lambda e, n=n, rs=rs, xt_=xt_: e.activation(out=xt_[0:n, :], in_=xt_[0:n, :], func=AF.Copy, scale=rs),
                  reads=[xk_, rsk], writes=[xk_])
            to_feature_major(xt_, n, xk_, hTg, ("hTg", bi0), t0 - G["start"], C_G1, bankbase)

        def B1(gi, b3_prev=None):
            G = GROUPS[gi]
            gs = G["start"]
            hk = [("hTg", i) for i in range(len(G["blocks"]))]
            it = 0
            nb3 = len(GROUPS[b3_prev]["blocks"]) if b3_prev is not None else 0
            b3_at = {1 + 3 * k: k for k in range(nb3)}
            for c in range(16):
                if c - 1 in b3_at and len(G["tiles"]) == 1:
                    B3_block(b3_prev, b3_at[c - 1], 4 * (it % 2))
                    it += 1
                tl = []
                for (t0, n) in G["tiles"]:
                    tl.append((t0, n, t0 - gs, 4 * (it % 2), it % 2, TILES.index((t0, n))))
                    it += 1
                wga = stB.get(gi * 64 + c * 3)
                for (t0, n, lo, pb, par, ti) in tl:
                    proj(wga[0], wga[1], wga[2], [16], [lambda k, lo=lo, n=n: hTg[:, k, lo:lo + n]], hk, 128, pb + 0, n, None)
                wgp = stB.get(gi * 64 + c * 3 + 1)
                for (t0, n, lo, pb, par, ti) in tl:
                    proj(wgp[0], wgp[1], wgp[2], [16], [lambda k, lo=lo, n=n: hTg[:, k, lo:lo + n]], hk, 128, pb + 1, n, None)
                wap = stB.get(gi * 64 + c * 3 + 2)
                for (t0, n, lo, pb, par, ti) in tl:
                    proj(wap[0], wap[1], [wap[2][0]], [8], [lambda k, t0=t0, n=n: attnT[:, k, t0:t0 + n]],
                         [("attnT", ti)], 128, pb + 2, n, None)
                    proj(wap[0], wap[1], [wap[2][0] + 1024], [8], [lambda k, t0=t0, n=n: poolT[:, k, t0:t0 + n]],
                         [("poolT", ti)], 128, pb + 3, n, None)
                for (t0, n, lo, pb, par, ti) in tl:
                    g4 = gt[par]
                    S.add("act", lambda e, pb=pb, g4=g4, n=n, c=c: e.activation(
                        out=g4[0][:, 0:n], in_=bank(pb)[:, 0:n], func=AF.Sigmoid, bias=cvec[:, C_BGA + c:C_BGA + c + 1]),
                        reads=[PS(pb), "cvec"], writes=[("gt", par, 0)])
                    S.add("act", lambda e, pb=pb, g4=g4, n=n, c=c: e.activation(
                        out=g4[1][:, 0:n], in_=bank(pb + 1)[:, 0:n], func=AF.Sigmoid, bias=cvec[:, C_BGP + c:C_BGP + c + 1]),
                        reads=[PS(pb + 1), "cvec"], writes=[("gt", par, 1)])
                    S.add("dve", lambda e, pb=pb, g4=g4, n=n: e.tensor_tensor(
                        out=g4[2][:, 0:n], in0=g4[0][:, 0:n], in1=bank(pb + 2)[:, 0:n], op=ALU.mult),
                        reads=[PS(pb + 2), ("gt", par, 0)], writes=[("gt", par, 2)])
                    S.add("dve", lambda e, pb=pb, g4=g4, n=n: e.tensor_tensor(
                        out=g4[3][:, 0:n], in0=g4[1][:, 0:n], in1=bank(pb + 3)[:, 0:n], op=ALU.mult),
                        reads=[PS(pb + 3), ("gt", par, 1)], writes=[("gt", par, 3)])
                    S.add("pool", lambda e, g4=g4, n=n, c=c, lo=lo: e.tensor_tensor(
                        out=mTg[:, c, lo:lo + n], in0=g4[2][:, 0:n], in1=g4[3][:, 0:n], op=ALU.add),
                        reads=[("gt", par, 2), ("gt", par, 3)], writes=[("mTg", c)])

        def B2_iter(gi, oc, itc):
            G = GROUPS[gi]
            gs = G["start"]
            mx = mixT
            mk = [("mTg", c) for c in range(16)]
            wo_ = stB.get(gi * 64 + 48 + oc)
            for (t0, n) in G["tiles"]:
                lo = t0 - gs
                bk = itc[0] % 2
                itc[0] += 1
                proj(wo_[0], wo_[1], wo_[2], [16], [lambda k, lo=lo, n=n: mTg[:, k, lo:lo + n]], mk, 128, bk, n, None)
                S.add("act", lambda e, bk=bk, oc=oc, lo=lo, n=n, mx=mx: e.activation(
                    out=mx[:, oc, lo:lo + n], in_=bank(bk)[:, 0:n], func=AF.Copy),
                    reads=[PS(bk)], writes=[("mixT", oc)])

        def B3(gi):
            for bi_ in range(len(GROUPS[gi]["blocks"])):
                B3_block(gi, bi_, 4 * (bi_ % 2))

        def B3_block(gi, bi_, half):
            G = GROUPS[gi]
            gs = G["start"]
            mx = mixT
            xk_ = [("mixT", oc) for oc in range(16)]
            if True:
                b = G["blocks"][bi_]
                t0, n = BLK[b]
                lo = t0 - gs
                pst = psA if half == 0 else psB
                for cg in range(4):
                    def tr(e, cg=cg, lo=lo, n=n, half=half, mx=mx):
                        ins = None
                        for i in range(4):
                            oc = cg * 4 + i
                            ins = e.transpose(out=bank(half + cg)[0:n, i * 128:(i + 1) * 128], in_=mx[:, oc, lo:lo + n],
                                              identity=identf)
                        return ins
                    S.add("pe", tr, reads=xk_ + ["consts"], writes=[PS(half + cg)])
                pkeys = [PS(half + i) for i in range(4)]
                rt_ = r1t[bi_ % 2]
                rk = ("r1t", bi_ % 2)
                src = meta[0:16, :] if b == 0 else x[t0 - 16:t0 - 16 + n, :]
                S.add("sp", lambda e, src=src, n=n: e.dma_start(out=xtB[0:n, :], in_=src), writes=["xtB"], dsem=S.dsem("xtB"))
                ss = stat[0:n, b:b + 1]
                S.add("act", lambda e, pst=pst, n=n, ss=ss: e.activation(out=junk[0:n, :], in_=pst[0:n, :], func=AF.Square,
                                                                         accum_out=ss),
                      reads=pkeys, writes=["junk", ("ss", b)])
                rtt = stat[0:n, 20 + b:21 + b]
                rs = stat[0:n, 40 + b:41 + b]
                S.add("act", lambda e, rtt=rtt, ss=ss, n=n: e.activation(out=rtt, in_=ss, func=AF.Sqrt, scale=1.0 / D,
                                                                         bias=eps_t[0:n, :]),
                      reads=[("ss", b), "consts"], writes=[("rt", b)])
                S.add("dve", lambda e, rs=rs, rtt=rtt: e.reciprocal(out=rs, in_=rtt), reads=[("rt", b)], writes=[("rs", b)])
                S.add("dve", lambda e, pst=pst, n=n, rs=rs, rt_=rt_: e.scalar_tensor_tensor(
                    out=rt_[0:n, :], in0=pst[0:n, :], scalar=rs, in1=gpost[0:n, :], op0=ALU.mult, op1=ALU.mult),
                    reads=pkeys + [("rs", b), "gpost"], writes=[rk])
                S.add("pool", lambda e, n=n, rt_=rt_: e.tensor_tensor(out=rt_[0:n, :], in0=rt_[0:n, :], in1=xtB[0:n, :],
                                                                       op=ALU.add),
                      reads=[rk, "xtB"], writes=[rk])
                S.add("sp", lambda e, n=n, rt_=rt_, t0=t0: e.dma_start(out=r1s[t0:t0 + n, :], in_=rt_[0:n, :]),
                      reads=[rk], writes=[("r1s", b)], dsem=S.dsem(f"r1st{bi_ % 2}"))

        itc = [0]
        for bi0 in range(len(GROUPS[0]["blocks"])):
            B0_block(0, bi0, 0)
        for gi in range(0, 4):
            B1(gi, b3_prev=(gi - 1 if gi > 0 else None))
            b0_at = {2: 0, 5: 1, 8: 2, 11: 3} if gi < 3 else {}
            for oc in range(16):
                B2_iter(gi, oc, itc)
                if oc in b0_at:
                    B0_block(gi + 1, b0_at[oc], 2)
        B3(3)
        S.barrier(bar_scr[0:1, 0:1])
    A.reset(mA)

    if stop not in ("A0", "A", "B"):
        gpost2 = A.alloc("gpost2", [128, D], F32)
        S.add("sp", lambda e: e.dma_start(out=gpost2[:], in_=rowv_d[1:2, :].partition_broadcast(128)),
              writes=["gpost2"], dsem=S.dsem("gpost2"))
        junkC = A.alloc("junkC", [128, D], BF16)
        r1c = [A.alloc(f"r1c{i}", [128, D], F32) for i in range(2)]
        ot = [A.alloc(f"ot{i}", [128, D], F32) for i in range(2)]
        carry = A.alloc("carry", [128, 88, 2], F32)
        S.add("pool", lambda e: e.memset(carry[:], 0.0), writes=["carry"])
        h2T = A.alloc("h2T", [128, 16, 528], BF16)
        actT = A.alloc("actT", [128, NJ, 528], BF16)
        ffT = A.alloc("ffT", [128, 16, 512], F32)
        upb = [[A.alloc(f"up{i}_{j}", [128, 530], F32) for j in range(2)] for i in range(2)]
        cv = [[A.alloc(f"cv{i}_{j}", [128, 528], F32) for j in range(3)] for i in range(2)]
        ring[0] = Ring(0, 6)
        reqC = []
        for _gi in range(4):
            for j in range(NJ):
                reqC.append(([slabv(w_up, j, 16, 128)], (wscC, j * 2, "load", "C")))
                reqC.append(([slabv(w_up, NJ + j, 16, 128)], (wscC, j * 2 + 1, "load", "C")))
            for oc in range(16):
                for pi_, (k0, nk) in enumerate(((0, 16), (16, 16), (32, 12))):
                    reqC.append(([slabv(w_down, oc, nk, 128, k0)], (wscC, 88 + oc * 3 + pi_, "load", "C")))
        stC = Stream(reqC, 5)

        def C0_front(gi, bi_):
            G = GROUPS[gi]
            b = G["blocks"][bi_]
            t0, n = BLK[b]
            rc = r1c[bi_ % 2]
            rck = ("r1c", bi_ % 2)
            S.add("sp", lambda e, rc=rc, t0=t0, n=n: e.dma_start(out=rc[0:n, :], in_=r1s[t0:t0 + n, :]),
                  reads=[("r1s", b)], writes=[rck], dsem=S.dsem(f"r1c{bi_ % 2}"))
            rs, rsk = rms_rows(rc[0:n, :], n, b, rck, junkC, "junkC")
            S.add("act", lambda e, rc=rc, n=n, rs=rs: e.activation(out=rc[0:n, :], in_=rc[0:n, :], func=AF.Copy, scale=rs),
                  reads=[rck, rsk], writes=[rck])

        def C0_back(gi, bi_, bankbase):
            G = GROUPS[gi]
            b = G["blocks"][bi_]
            t0, n = BLK[b]
            to_feature_major(r1c[bi_ % 2], n, ("r1c", bi_ % 2), h2T, ("h2T", bi_), t0 - G["start"], C_G2, bankbase)

        def C0_block(gi, bi_, bankbase):
            C0_front(gi, bi_)
            C0_back(gi, bi_, bankbase)

        def C1_iter(gi, j):
            G = GROUPS[gi]
            gs, gn = G["start"], G["n"]
            hk = [("h2T", i) for i in range(len(G["blocks"]))]
            s2 = j % 2
            for half_, jj in ((0, j), (1, NJ + j)):
                wu = stC.get(gi * 136 + j * 2 + half_)
                ub_ = upb[s2][half_]
                ubk = ("upb", s2, half_)
                S.add("pool", lambda e, ub_=ub_, jj=jj: e.tensor_copy(out=ub_[:, 0:2], in_=carry[:, jj, :]),
                      reads=["carry"], writes=[ubk])
                for tix, (t0, n) in enumerate(G["tiles"]):
                    lo = t0 - gs
                    bk = 2 * half_ + (j + tix) % 2
                    proj(wu[0], wu[1], wu[2], [16], [lambda k, lo=lo, n=n: h2T[:, k, lo:lo + n]], hk, 128, bk, n, None)
                    S.add("act", lambda e, bk=bk, ub_=ub_, lo=lo, n=n: e.activation(
                        out=ub_[:, 2 + lo:2 + lo + n], in_=bank(bk)[:, 0:n], func=AF.Copy),
                        reads=[PS(bk)], writes=[ubk])
                S.add("pool", lambda e, ub_=ub_, jj=jj, gn=gn: e.tensor_copy(out=carry[:, jj, :], in_=ub_[:, gn:gn + 2]),
                      reads=[ubk], writes=["carry"])
                tg = cv[s2][half_]
                tgk = ("cv", s2, half_)
                S.add("dve", lambda e, tg=tg, ub_=ub_, jj=jj, gn=gn: e.tensor_scalar(
                    out=tg[:, 0:gn], in0=ub_[:, 0:gn], scalar1=cvec[:, C_CW0 + jj:C_CW0 + jj + 1],
                    scalar2=cvec[:, C_CB + jj:C_CB + jj + 1], op0=ALU.mult, op1=ALU.add),
                    reads=[ubk, "cvec"], writes=[tgk])
                S.add("dve", lambda e, tg=tg, ub_=ub_, jj=jj, gn=gn: e.scalar_tensor_tensor(
                    out=tg[:, 0:gn], in0=ub_[:, 1:gn + 1], scalar=cvec[:, C_CW1 + jj:C_CW1 + jj + 1], in1=tg[:, 0:gn],
                    op0=ALU.mult, op1=ALU.add), reads=[ubk, "cvec", tgk], writes=[tgk])
                S.add("dve", lambda e, tg=tg, ub_=ub_, jj=jj, gn=gn: e.scalar_tensor_tensor(
                    out=tg[:, 0:gn], in0=ub_[:, 2:gn + 2], scalar=cvec[:, C_CW2 + jj:C_CW2 + jj + 1], in1=tg[:, 0:gn],
                    op0=ALU.mult, op1=ALU.add), reads=[ubk, "cvec", tgk], writes=[tgk])
            gl = cv[s2][2]
            S.add("act", lambda e, gl=gl, s2=s2, gn=gn: e.activation(out=gl[:, 0:gn], in_=cv[s2][0][:, 0:gn],
                                                                      func=AF.Gelu_apprx_tanh),
                  reads=[("cv", s2, 0)], writes=[("cv", s2, 2)])
            S.add("pool", lambda e, gl=gl, s2=s2, gn=gn, j=j: e.tensor_tensor(
                out=actT[:, j, 0:gn], in0=gl[:, 0:gn], in1=cv[s2][1][:, 0:gn], op=ALU.mult),
                reads=[("cv", s2, 2), ("cv", s2, 1)], writes=[("actT", j)])

        def C2_iter(gi, oc):
            G = GROUPS[gi]
            t0r, nr = G["tiles"][-1]
            lo = t0r - G["start"]
            ak = [("actT", j) for j in range(NJ)]
            bk = oc % 2
            for pi, (k0, nk) in enumerate(((0, 16), (16, 16), (32, 12))):
                wd = stC.get(gi * 136 + 88 + oc * 3 + pi)
                proj(wd[0], wd[1], wd[2], [nk], [lambda k, k0=k0, lo=lo, nr=nr: actT[:, k0 + k, lo:lo + nr]],
                     ak, 128, bk, nr, None, first=(pi == 0), last=(pi == 2))
            S.add("act", lambda e, bk=bk, oc=oc, nr=nr: e.activation(out=ffT[:, oc, 0:nr], in_=bank(bk)[:, 0:nr], func=AF.Copy),
                  reads=[PS(bk)], writes=[("ffT", oc)])

        def C3_block(gi, bi_):
            G = GROUPS[gi]
            t0r, nr = G["tiles"][-1]
            rblocks = [b for b in G["blocks"] if b > 0]
            b = rblocks[bi_]
            t0, n = BLK[b]
            lo2 = t0 - t0r
            fk = [("ffT", oc) for oc in range(16)]
            half = 4
            pst = psB
            for cg in range(4):
                def tr(e, cg=cg, lo2=lo2, n=n, half=half):
                    ins = None
                    for i in range(4):
                        oc = cg * 4 + i
                        ins = e.transpose(out=bank(half + cg)[0:n, i * 128:(i + 1) * 128], in_=ffT[:, oc, lo2:lo2 + n],
                                          identity=identf)
                    return ins
                S.add("pe", tr, reads=fk + ["consts"], writes=[PS(half + cg)])
            pkeys = [PS(half + i) for i in range(4)]
            rc = r1c[bi_ % 2]
            rck = ("r1c", bi_ % 2)
            S.add("sp", lambda e, rc=rc, t0=t0, n=n: e.dma_start(out=rc[0:n, :], in_=r1s[t0:t0 + n, :]),
                  reads=[("r1s", b)], writes=[rck], dsem=S.dsem(f"r1c{bi_ % 2}"))
            ss = stat[0:n, b:b + 1]
            S.add("act", lambda e, pst=pst, n=n, ss=ss: e.activation(out=junkC[0:n, :], in_=pst[0:n, :], func=AF.Square,
                                                                     accum_out=ss),
                  reads=pkeys, writes=["junkC", ("ss", b)])
            rtt = stat[0:n, 20 + b:21 + b]
            rs = stat[0:n, 40 + b:41 + b]
            S.add("act", lambda e, rtt=rtt, ss=ss, n=n: e.activation(out=rtt, in_=ss, func=AF.Sqrt, scale=1.0 / D,
                                                                     bias=eps_t[0:n, :]),
                  reads=[("ss", b), "consts"], writes=[("rt", b)])
            S.add("dve", lambda e, rs=rs, rtt=rtt: e.reciprocal(out=rs, in_=rtt), reads=[("rt", b)], writes=[("rs", b)])
            o_ = ot[bi_ % 2]
            ok_ = ("ot", bi_ % 2)
            S.add("dve", lambda e, pst=pst, n=n, rs=rs, o_=o_: e.scalar_tensor_tensor(
                out=o_[0:n, :], in0=pst[0:n, :], scalar=rs, in1=gpost2[0:n, :], op0=ALU.mult, op1=ALU.mult),
                reads=pkeys + [("rs", b), "gpost2"], writes=[ok_])
            S.add("pool", lambda e, n=n, o_=o_, rc=rc: e.tensor_tensor(out=o_[0:n, :], in0=o_[0:n, :], in1=rc[0:n, :],
                                                                        op=ALU.add),
                  reads=[ok_, rck], writes=[ok_])
            S.add("sp", lambda e, n=n, o_=o_, t0=t0: e.dma_start(out=out[t0 - 16:t0 - 16 + n, :], in_=o_[0:n, :]),
                  reads=[ok_], dsem=S.dsem(f"ost{bi_ % 2}"))

        for bi_ in range(len(GROUPS[0]["blocks"])):
            C0_block(0, bi_, 0)
        for gi in range(4):
            c3_at = {4: 0, 12: 1, 20: 2, 28: 3} if gi > 0 else {}
            for j in range(NJ):
                C1_iter(gi, j)
                if j in c3_at:
                    C3_block(gi - 1, c3_at[j])
            c0f_at = {0: 0, 3: 1, 6: 2, 9: 3} if gi < 3 else {}
            c0b_at = {2: 0, 5: 1, 8: 2, 11: 3} if gi < 3 else {}
            for oc in range(16):
                C2_iter(gi, oc)
                if oc in c0f_at:
                    C0_front(gi + 1, c0f_at[oc])
                if oc in c0b_at:
                    C0_back(gi + 1, c0b_at[oc], 2)
        for bi_ in range(4):
            C3_block(3, bi_)
    S.emit()
    return nc, S, A


def host_layout(inputs):
    f32 = np.float32
    fm = lambda v: np.ascontiguousarray(np.asarray(v, f32).reshape(-1, 128).T)
    b_in = np.asarray(inputs["b_in"], f32)[0]
    cvec = np.zeros((128, NCV), f32)
    cvec[:, C_G1:C_G1 + 16] = fm(inputs["mix_pre_g"][0])
    cvec[:, C_BQ:C_BQ + 8] = fm(b_in[0:1024])
    cvec[:, C_BK:C_BK + 8] = fm(b_in[1024:2048])
    cvec[:, C_BV:C_BV + 8] = fm(b_in[2048:3072])
    cvec[:, C_BU:C_BU + 8] = fm(b_in[3080:4104])
    cvec[:, C_BGA:C_BGA + 16] = fm(b_in[4104:6152])
    cvec[:, C_BGP:C_BGP + 16] = fm(b_in[6152:8200])
    cvec[:, C_PSC:C_PSC + 8] = fm(inputs["pool_scale"][0])
    cvec[:, C_G2:C_G2 + 16] = fm(inputs["ffn_pre_g"][0])
    cvec[:, C_CB:C_CB + 88] = fm(inputs["ffn_conv_b"][0])
    cw = np.asarray(inputs["ffn_conv_w"], f32)[0]
    cvec[:, C_CW0:C_CW0 + 88] = fm(cw[0])
    cvec[:, C_CW1:C_CW1 + 88] = fm(cw[1])
    cvec[:, C_CW2:C_CW2 + 88] = fm(cw[2])
    cvec[0:8, C_BF] = b_in[3072:3080]
    rowv = np.stack([np.asarray(inputs["mix_post_g"], f32)[0], np.asarray(inputs["ffn_post_g"], f32)[0]])
    consts = np.zeros((128, NKC), f32)
    consts[:, K_ID:K_ID + 128] = np.eye(128, dtype=f32)
    p = np.arange(128)[:, None]
    j = np.arange(128)[None, :]
    consts[:, K_MASK:K_MASK + 128] = np.where(j >= p, 0.0, -30000.0)
    for wi, w in enumerate(POOL_WINDOWS):
        for t in range(16):
            consts[:, K_RCNT + wi * 16 + t] = 1.0 / min(t + 1, w)
    consts[:, K_ONES:K_ONES + 128] = 1.0
    consts[:, K_EPS] = EPS
    return cvec, np.ascontiguousarray(rowv), consts


_CACHE = {}


def kernel(**inputs):
    f32 = np.float32
    x = np.asarray(inputs["x"], f32)
    B = x.shape[0]
    cvec, rowv, consts = host_layout(inputs)
    if "nc" not in _CACHE:
        _CACHE["nc"] = build_program()[0]
    nc = _CACHE["nc"]
    def slabs(w):
        K_, N_ = w.shape
        t = w.reshape(K_ // 128, 128, N_ // 128, 128).transpose(2, 1, 0, 3)
        return np.ascontiguousarray(t).reshape(N_ // 128 * 128, K_)

    win = np.asarray(inputs["w_in"], f32)[0]
    win_main = np.concatenate([win[:, 0:3072], win[:, 3080:8200]], axis=1)
    wf = np.ascontiguousarray(win[:, 3072:3080].reshape(16, 128, 8).transpose(1, 0, 2)).reshape(128, 128)
    wao = slabs(np.asarray(inputs["w_attn_o"], f32)[0])
    wpo = slabs(np.asarray(inputs["w_pool_o"], f32)[0])
    pw = np.asarray(inputs["pool_w"], f32)[0].reshape(4, 2, 128, 256).transpose(2, 0, 1, 3)
    shared = {
        "meta": np.ascontiguousarray(np.asarray(inputs["meta_tokens"], f32)),
        "w_in": slabs(win_main),
        "w_f": wf,
        "w_ap": np.ascontiguousarray(np.concatenate([wao, wpo], axis=1)),
        "pool_w": np.ascontiguousarray(pw).reshape(128, 2048),
        "w_out": slabs(np.asarray(inputs["w_out"], f32)[0]),
        "w_up": slabs(np.asarray(inputs["w_ffn_up"], f32)[0]),
        "w_down": slabs(np.asarray(inputs["w_ffn_down"], f32)[0]),
        "cvec": cvec, "rowv": rowv, "consts": consts,
    }
    in_maps = []
    for b in range(B):
        m = dict(shared)
        m["x"] = np.ascontiguousarray(x[b])
        in_maps.append(m)
    res = run_bass_kernel_spmd(nc, in_maps, core_ids=list(range(B)))
    return np.stack([np.asarray(r["out"], f32) for r in res.results], axis=0)
```

```python
import numpy as np
import concourse.bass as bass
import concourse.mybir as mybir
from concourse.bass_utils import run_bass_kernel_spmd

F32 = mybir.dt.float32
BF16 = mybir.dt.bfloat16
AF = mybir.ActivationFunctionType
ALU = mybir.AluOpType

D = 2048
SEQ = 2048
NMETA = 16
L = SEQ + NMETA
DIN = 8200
DFF = 5632
NJ = DFF // 128
EPS = 1e-6
QSCALE = 128 ** -0.5
POOL_WINDOWS = (2, 4, 8, 16)

BLK = [(0, 16)] + [(16 + 128 * i, 128) for i in range(16)]
TILES = [(0, 16)] + [(16 + 512 * i, 512) for i in range(4)]
GROUPS = [dict(start=0, n=528, blocks=list(range(0, 5)), tiles=[(0, 16), (16, 512)])]
for _g in range(1, 4):
    GROUPS.append(dict(start=16 + 512 * _g, n=512, blocks=list(range(1 + 4 * _g, 5 + 4 * _g)),
                       tiles=[(16 + 512 * _g, 512)]))

C_G1, C_BQ, C_BK, C_BV, C_BU, C_BGA, C_BGP, C_PSC, C_G2 = 0, 16, 24, 32, 40, 48, 64, 80, 88
C_CB, C_CW0, C_CW1, C_CW2, C_BF, NCV = 104, 192, 280, 368, 456, 457
K_ID, K_MASK, K_RCNT, K_ONES, K_EPS, NKC = 0, 128, 256, 320, 448, 449


class _Op:
    __slots__ = ("eng", "fn", "deps", "dsem", "ticket", "observed", "idx", "waits")


class Sched:
    def __init__(self, nc):
        self.nc = nc
        self.ops = []
        self.lastw = {}
        self.readers = {}
        self.dma_tot = {}
        self.esem = {}
        self._dsems = {}
        self.bar_op = None
        self.last_on = {}
        self.bg_sems = set()

    def dsem(self, name):
        if name not in self._dsems:
            self._dsems[name] = self.nc.alloc_semaphore("d_" + name)
            self.dma_tot[self._dsems[name]] = 0
        return self._dsems[name]

    def add(self, eng, fn, reads=(), writes=(), dsem=None):
        op = _Op()
        op.eng, op.fn, op.dsem = eng, fn, dsem
        op.idx = len(self.ops)
        op.observed = False
        deps = {}
        for k in reads:
            w = self.lastw.get(k)
            if w is not None:
                deps[w] = deps.get(w, 0) | 1
        for k in writes:
            w = self.lastw.get(k)
            if w is not None:
                deps[w] = deps.get(w, 0) | 1
            for r in self.readers.get(k, ()):
                deps[r] = deps.get(r, 0) | 2
        op.deps = []
        for d, kind in deps.items():
            dop = self.ops[d]
            if dop.dsem is not None:
                op.deps.append(("dma", dop.dsem, self.dma_tot[dop.dsem]))
            else:
                if dop.eng == eng and eng == "pe":
                    continue
                op.deps.append(("eng", d))
        if self.bar_op is not None and eng != "dve":
            op.deps.append(("eng", self.bar_op))
        if dsem is not None:
            self.dma_tot[dsem] += 16
        else:
            self.last_on[eng] = op.idx
        for k in reads:
            self.readers.setdefault(k, []).append(op.idx)
        for k in writes:
            self.lastw[k] = op.idx
            self.readers[k] = []
        self.ops.append(op)
        return op

    def barrier(self, scratch):
        op = _Op()
        op.eng, op.dsem = "dve", None
        op.fn = lambda e: e.memset(scratch, 0.0)
        op.idx = len(self.ops)
        op.observed = False
        op.deps = [("eng", i) for e, i in self.last_on.items() if e != "dve"]
        op.deps += [("dma", s, t) for s, t in self.dma_tot.items() if t > 0 and s not in self.bg_sems]
        self.ops.append(op)
        self.bar_op = op.idx
        self.last_on["dve"] = op.idx
        self.lastw = {k: v for k, v in self.lastw.items() if isinstance(k, tuple) and k[0] == "wsc"}
        self.readers = {}

    def emit(self):
        nc = self.nc
        for o in self.ops:
            for d in o.deps:
                if d[0] == "eng":
                    self.ops[d[1]].observed = True
        cnt = {}
        for o in self.ops:
            if o.dsem is None and o.observed:
                cnt[o.eng] = cnt.get(o.eng, 0) + 1
                o.ticket = cnt[o.eng]
        for e in cnt:
            self.esem[e] = nc.alloc_semaphore("e_" + e)
        for o in self.ops:
            w = {}
            for d in o.deps:
                if d[0] == "dma":
                    sem, val = d[1], d[2]
                else:
                    dop = self.ops[d[1]]
                    sem, val = self.esem[dop.eng], dop.ticket
                if w.get(sem, 0) < val:
                    w[sem] = val
            o.waits = w
        self.stats = dict(cnt)
        with nc.Block() as block:
            for engname, deco in (("sp", block.sync), ("act", block.scalar), ("pe", block.tensor),
                                  ("dve", block.vector), ("pool", block.gpsimd)):
                ops = [o for o in self.ops if o.eng == engname]

                def body(e, ops=ops, engname=engname):
                    waited = {}
                    for o in ops:
                        for sem, val in o.waits.items():
                            if waited.get(sem, 0) < val:
                                e.wait_ge(sem, val)
                                waited[sem] = val
                        ins = o.fn(e)
                        if o.dsem is not None:
                            ins.then_inc(o.dsem, 16)
                        elif o.observed:
                            ins.then_inc(self.esem[o.eng], 1)
                    if engname == "sp":
                        for sem, tot in self.dma_tot.items():
                            if tot > 0:
                                e.wait_ge(sem, tot)

                deco(body)


class Arena:
    def __init__(self, nc):
        self.nc = nc
        self.base = (nc.sbuf_base + 63) // 64 * 64
        self.top = nc.sbuf_top
        self.cur = self.base
        self.n = 0
        self.peak = 0

    def alloc(self, name, shape, dtype):
        esz = 2 if dtype == BF16 else 4
        size = esz
        for s in shape[1:]:
            size *= s
        off = (self.cur + 63) // 64 * 64
        assert off + size <= self.top, f"SBUF overflow allocating {name}: need {off + size - self.top} more bytes"
        self.cur = off + size
        self.peak = max(self.peak, self.cur)
        self.n += 1
        self.last_off = off
        return self.nc.alloc_sbuf_tensor_at(f"{name}_{self.n}", list(shape), dtype, offset=off)

    def mark(self):
        return self.cur

    def reset(self, m):
        self.cur = m


def build_program(stop=None, debug=False):
    nc = bass.Bass("TRN2", target_bir_lowering=False)
    dt = lambda name, shape, kind, dtp=F32: nc.dram_tensor(name, list(shape), dtp, kind=kind).ap()
    x = dt("x", [SEQ, D], "ExternalInput")
    meta = dt("meta", [NMETA, D], "ExternalInput")
    w_in = dt("w_in", [64 * 128, 2048], "ExternalInput")
    w_f = dt("w_f", [128, 128], "ExternalInput")
    w_ap = dt("w_ap", [16 * 128, 2048], "ExternalInput")
    pool_w = dt("pool_w", [128, 2048], "ExternalInput")
    w_out = dt("w_out", [16 * 128, 2048], "ExternalInput")
    w_up = dt("w_up", [88 * 128, 2048], "ExternalInput")
    w_down = dt("w_down", [16 * 128, DFF], "ExternalInput")
    cvec_d = dt("cvec", [128, NCV], "ExternalInput")
    rowv_d = dt("rowv", [2, D], "ExternalInput")
    consts_d = dt("consts", [128, NKC], "ExternalInput")
    out = dt("out", [SEQ, D], "ExternalOutput")
    r1s = dt("r1s", [L, D], "Internal")
    wscB = dt("wscB", [64 * 128, 2048], "Internal", BF16)
    wscC = dt("wscC", [136 * 128, 2048], "Internal", BF16)
    dbg = {}
    if debug:
        dbg["attnT"] = dt("dbg_attnT", [128, 8, L], "ExternalOutput", BF16)
        dbg["poolT"] = dt("dbg_poolT", [128, 8, L], "ExternalOutput", BF16)
        dbg["hT"] = dt("dbg_hT", [128, 16, L], "ExternalOutput", BF16)
        dbg["c8"] = dt("dbg_c8", [8, L], "ExternalOutput")

    S = Sched(nc)
    A = Arena(nc)
    psA = nc.alloc_psum_tensor("psA", [128, 2048], F32)
    psB = nc.alloc_psum_tensor("psB", [128, 2048], F32)

    def bank(i):
        t = psA if i < 4 else psB
        return t[:, (i % 4) * 512:(i % 4 + 1) * 512]

    PS = lambda i: ("ps", i)

    cvec = A.alloc("cvec", [128, NCV], F32)
    consts = A.alloc("consts", [128, NKC], F32)
    ones_bf = A.alloc("ones_bf", [128, 128], BF16)
    stat = A.alloc("stat", [128, 64], F32)
    bar_scr = A.alloc("bar", [128, 8], F32)
    identf = consts[:, K_ID:K_ID + 128]
    maskf = consts[:, K_MASK:K_MASK + 128]
    onesf = consts[:, K_ONES:K_ONES + 128]
    eps_t = consts[:, K_EPS:K_EPS + 1]

    S.add("sp", lambda e: e.dma_start(out=cvec[:], in_=cvec_d[:, :]), writes=["cvec"], dsem=S.dsem("cvec"))
    S.add("sp", lambda e: e.dma_start(out=consts[:], in_=consts_d[:, :]), writes=["consts"], dsem=S.dsem("consts"))
    S.add("dve", lambda e: e.tensor_copy(out=ones_bf[:], in_=onesf), reads=["consts"], writes=["ones_bf"])
    bqs = A.alloc("bqs", [128, 8], F32)
    S.add("dve", lambda e: e.tensor_scalar(out=bqs[:], in0=cvec[:, C_BQ:C_BQ + 8], scalar1=QSCALE, scalar2=None,
                                           op0=ALU.mult), reads=["cvec"], writes=["bqs"])

    def slabsrc(w, j, nk, ncol, k0=0):
        return w[j * 128:(j + 1) * 128, k0 * ncol:(k0 + nk) * ncol]

    bgB = []
    for c in range(16):
        bgB += [slabsrc(w_in, 32 + c, 16, 128), slabsrc(w_in, 48 + c, 16, 128), slabsrc(w_ap, c, 16, 128)]
    for oc in range(16):
        bgB.append(slabsrc(w_out, oc, 16, 128))
    bgC = []
    for j in range(NJ):
        bgC += [slabsrc(w_up, j, 16, 128), slabsrc(w_up, NJ + j, 16, 128)]
    for oc in range(16):
        for (k0, nk) in ((0, 16), (16, 16), (32, 12)):
            bgC.append(slabsrc(w_down, oc, nk, 128, k0))
    class Ring:
        def __init__(self, nf, nb):
            self.f = []
            self.f_off = None
            for i in range(nf):
                self.f.append(A.alloc(f"wf{i}", [128, 2048], F32))
                if i == 0:
                    self.f_off = A.last_off
            self.b = [A.alloc(f"wb{i}", [128, 2048], BF16) for i in range(nb)]
            self.fi = 0
            self.bi = 0
            self.ci = 0
            self.cast_engs = ("dve", "act")

    ring = [None]

    def load_slab(parts, cache=None):
        R = ring[0]
        offs = []
        off = 0
        for (ap3, k, w) in parts:
            offs.append(off)
            off += k * w
        bi = R.bi % len(R.b)
        R.bi += 1
        wb = R.b[bi]
        if cache is not None and cache[2] == "load":
            sc, cid = cache[0], cache[1]
            S.add("sp", lambda e, o=off: e.dma_start(out=wb[:, 0:o], in_=sc[cid * 128:(cid + 1) * 128, 0:o]),
                  reads=[("wsc", cache[3], cid)], writes=[("wb", bi)], dsem=S.dsem(f"wbl{bi}"))
            return wb, ("wb", bi), offs
        fi = R.fi % len(R.f)
        R.fi += 1
        wf = R.f[fi]
        for (ap3, k, w), o0 in zip(parts, offs):
            dst = wf[:, o0:o0 + k * w]
            S.add("sp", lambda e, dst=dst, ap3=ap3: e.dma_start(out=dst, in_=ap3),
                  writes=[("wf", fi)], dsem=S.dsem(f"wf{fi}"))
        ceng = R.cast_engs[R.ci % len(R.cast_engs)]
        R.ci += 1
        if ceng == "dve":
            S.add("dve", lambda e, o=off: e.tensor_copy(out=wb[:, 0:o], in_=wf[:, 0:o]),
                  reads=[("wf", fi)], writes=[("wb", bi)])
        else:
            S.add("act", lambda e, o=off: e.activation(out=wb[:, 0:o], in_=wf[:, 0:o], func=AF.Copy),
                  reads=[("wf", fi)], writes=[("wb", bi)])
        if cache is not None and cache[2] == "store":
            sc, cid = cache[0], cache[1]
            S.add("pool", lambda e, o=off: e.dma_start(out=sc[cid * 128:(cid + 1) * 128, 0:o], in_=wb[:, 0:o]),
                  reads=[("wb", bi)], writes=[("wsc", cid)], dsem=S.dsem(f"wst{bi}"))
        return wb, ("wb", bi), offs

    def slabv(w, j, nk, ncol, k0=0):
        return (w[j * 128:(j + 1) * 128, k0 * ncol:(k0 + nk) * ncol], nk, ncol)

    class Stream:
        def __init__(self, reqs, pf):
            self.reqs, self.pf, self.nxt, self.loaded = reqs, pf, 0, {}

        def get(self, i):
            hi = min(i + self.pf, len(self.reqs) - 1)
            while self.nxt <= hi:
                r_ = self.reqs[self.nxt]
                self.loaded[self.nxt] = load_slab(*r_) if isinstance(r_, tuple) else load_slab(r_)
                self.nxt += 1
            return self.loaded.pop(i)

    def rms_rows(src_tile, n, col, key_in, junk, junk_key):
        ss = stat[0:n, col:col + 1]
        rt = stat[0:n, 20 + col:21 + col]
        rs = stat[0:n, 40 + col:41 + col]
        S.add("act", lambda e: e.activation(out=junk[0:n, :], in_=src_tile, func=AF.Square, accum_out=ss),
              reads=[key_in], writes=[junk_key, ("ss", col)])
        S.add("act", lambda e: e.activation(out=rt, in_=ss, func=AF.Sqrt, scale=1.0 / D, bias=eps_t[0:n, :]),
              reads=[("ss", col), "consts"], writes=[("rt", col)])
        S.add("dve", lambda e: e.reciprocal(out=rs, in_=rt), reads=[("rt", col)], writes=[("rs", col)])
        return rs, ("rs", col)

    def to_feature_major(tile, n, tile_key, dstT, dst_key, col0, gcol, bankbase, evac_engs=("dve", "dve")):
        for cg in range(4):
            bk = bankbase + (cg % 2)

            def tr(e, cg=cg, bk=bk):
                ins = None
                for i in range(4):
                    c = cg * 4 + i
                    ins = e.transpose(out=bank(bk)[:, i * 128:i * 128 + n], in_=tile[0:n, c * 128:(c + 1) * 128],
                                      identity=identf[0:n, 0:n])
                return ins

            S.add("pe", tr, reads=[tile_key, "consts"], writes=[PS(bk)])
            for i in range(4):
                c = cg * 4 + i
                eng = evac_engs[i % 2]
                if eng == "dve":
                    S.add("dve", lambda e, c=c, i=i, bk=bk: e.tensor_scalar(
                        out=dstT[:, c, col0:col0 + n], in0=bank(bk)[:, i * 128:i * 128 + n],
                        scalar1=cvec[:, gcol + c:gcol + c + 1], scalar2=None, op0=ALU.mult),
                        reads=[PS(bk), "cvec"], writes=[dst_key])
                else:
                    S.add("act", lambda e, c=c, i=i, bk=bk: e.activation(
                        out=dstT[:, c, col0:col0 + n], in_=bank(bk)[:, i * 128:i * 128 + n], func=AF.Copy,
                        scale=cvec[:, gcol + c:gcol + c + 1]),
                        reads=[PS(bk), "cvec"], writes=[dst_key])

    def proj(wb, wkey, woffs, nk_list, ins_list, in_keys, m, bk, n, col_lists, first=True, last=True):
        def fn(e):
            ins = None
            tot = sum(nk_list)
            cnt = 0
            for wi, nk in enumerate(nk_list):
                for k in range(nk):
                    o = woffs[wi] + k * m
                    ins = e.matmul(bank(bk)[0:m, 0:n], lhsT=wb[:, o:o + m], rhs=ins_list[wi](k),
                                   start=(first and cnt == 0), stop=(last and cnt == tot - 1))
                    cnt += 1
            return ins
        S.add("pe", fn, reads=[wkey] + list(in_keys), writes=[PS(bk)])

    mA = A.mark()
    attnT = A.alloc("attnT", [128, 8, L], BF16)
    attnT_off = A.last_off
    poolT = A.alloc("poolT", [128, 8, L], BF16)
    xtA = [nc.alloc_sbuf_tensor_at(f"xtA{i}", [128, D], F32, offset=attnT_off + i * 8192) for i in range(6)]
    mA1 = A.mark()
    hT = A.alloc("hT", [128, 16, L], BF16)
    ring[0] = Ring(2, 3)
    R = ring[0]
    R.cast_engs = ("act",)
    reqA = [[slabv(w_f, 0, 16, 8)]]
    for h in range(8):
        for c0 in (0, 1024, 2048):
            reqA.append([slabv(w_in, (c0 // 1024) * 8 + h, 16, 128)])
    for c in range(8):
        reqA.append([slabv(w_in, 24 + c, 16, 128)])
    stA = Stream(reqA, 2)
    mA2 = A.mark()
    c8 = A.alloc("c8", [8, L], F32)
    negc = A.alloc("negc", [128, 17, 8], F32)
    mA2b = A.mark()

    for b, (t0, n) in enumerate(BLK):
        xt = xtA[b % 6]
        xk = ("xtA", b % 6)
        src = meta[0:16, :] if b == 0 else x[t0 - 16:t0 - 16 + n, :]
        S.add("sp", lambda e, xt=xt, src=src, n=n: e.dma_start(out=xt[0:n, :], in_=src),
              writes=[xk], dsem=S.dsem(f"xtA{b % 6}"))
        rs, rsk = rms_rows(xt[0:n, :], n, b, xk, R.b[0], ("wb", 0))
        S.add("act", lambda e, xt=xt, n=n, rs=rs: e.activation(out=xt[0:n, :], in_=xt[0:n, :], func=AF.Copy, scale=rs),
              reads=[xk, rsk], writes=[xk])
        to_feature_major(xt, n, xk, hT, ("hT", b), t0, C_G1, 0)
    HT_ALL = [("hT", b) for b in range(17)]

    def hT_keys(t0, n):
        return [("hT", b) for b, (b0, bn) in enumerate(BLK) if b0 < t0 + n and b0 + bn > t0]

    if debug:
        S.add("sp", lambda e: e.dma_start(out=dbg["hT"][:, :, :], in_=hT[:]), reads=HT_ALL, dsem=S.dsem("dbg"))

    wb, wk, wo = stA.get(0)
    lt = [A.alloc(f"lt{i}", [8, L], F32) for i in range(4)]
    for ti, (t0, n) in enumerate(TILES):
        bk = ti % 2
        proj(wb, wk, wo, [16], [lambda k, t0=t0, n=n: hT[:, k, t0:t0 + n]], hT_keys(t0, n), 8, bk, n, None)
        S.add("dve", lambda e, bk=bk, t0=t0, n=n: e.tensor_scalar(
            out=lt[0][:, t0:t0 + n], in0=bank(bk)[0:8, 0:n], scalar1=cvec[0:8, C_BF:C_BF + 1], scalar2=None,
            op0=ALU.add), reads=[PS(bk), "cvec"], writes=["lt0"])
    tf, ta, tb_, tc = lt
    V = lambda eng, fn, r, w: S.add(eng, fn, reads=r, writes=w)
    V("act", lambda e: e.activation(out=ta[:], in_=tf[:], func=AF.Abs), ["lt0"], ["lt1"])
    V("act", lambda e: e.activation(out=ta[:], in_=ta[:], func=AF.Exp, scale=-1.0), ["lt1"], ["lt1"])
    V("dve", lambda e: e.tensor_scalar(out=tb_[:], in0=ta[:], scalar1=2.0, scalar2=None, op0=ALU.add), ["lt1"], ["lt2"])
    V("dve", lambda e: e.reciprocal(out=tb_[:], in_=tb_[:]), ["lt2"], ["lt2"])
    V("dve", lambda e: e.tensor_tensor(out=ta[:], in0=ta[:], in1=tb_[:], op=ALU.mult), ["lt1", "lt2"], ["lt1"])
    V("dve", lambda e: e.tensor_tensor(out=tb_[:], in0=ta[:], in1=ta[:], op=ALU.mult), ["lt1"], ["lt2"])
    V("dve", lambda e: e.tensor_scalar(out=tc[:], in0=tb_[:], scalar1=1.0 / 9.0, scalar2=None, op0=ALU.mult), ["lt2"], ["lt3"])
    for cst in (1.0 / 7.0, 1.0 / 5.0, 1.0 / 3.0):
        V("dve", lambda e, cst=cst: e.scalar_tensor_tensor(out=tc[:], in0=tc[:], scalar=cst, in1=tb_[:],
                                                           op0=ALU.add, op1=ALU.mult), ["lt3", "lt2"], ["lt3"])
    V("dve", lambda e: e.scalar_tensor_tensor(out=tc[:], in0=tc[:], scalar=1.0, in1=ta[:], op0=ALU.add, op1=ALU.mult),
      ["lt3", "lt1"], ["lt3"])
    V("dve", lambda e: e.tensor_scalar(out=ta[:], in0=tf[:], scalar1=0.0, scalar2=None, op0=ALU.min), ["lt0", "lt1"], ["lt1"])
    V("dve", lambda e: e.scalar_tensor_tensor(out=tb_[:], in0=tc[:], scalar=-2.0, in1=ta[:], op0=ALU.mult, op1=ALU.add),
      ["lt3", "lt1", "lt2"], ["lt2"])
    V("dve", lambda e: e.memset(tc[:], 1.0), ["lt3"], ["lt3"])
    V("dve", lambda e: e.tensor_tensor_scan(out=c8[:], data0=tc[:], data1=tb_[:], initial=0.0, op0=ALU.mult, op1=ALU.add),
      ["lt3", "lt2"], ["c8"])
    for b, (t0, n) in enumerate(BLK):
        bk = b % 2
        S.add("pe", lambda e, bk=bk, t0=t0, n=n: e.transpose(out=bank(bk)[0:n, 0:8], in_=c8[0:8, t0:t0 + n],
                                                             identity=identf[0:8, 0:8]),
              reads=["c8", "consts"], writes=[PS(bk)])
        S.add("dve", lambda e, bk=bk, b=b, n=n: e.tensor_scalar(out=negc[0:n, b, :], in0=bank(bk)[0:n, 0:8], scalar1=-1.0,
                                                                scalar2=None, op0=ALU.mult),
              reads=[PS(bk)], writes=["negc"])
    if stop not in ("A0", "A"):
        nbg = 0
        for nm, scr, lst in (("B", wscB, bgB), ("C", wscC, bgC)):
            for cid, src in enumerate(lst):
                ncol = src.shape[1]
                sem = S.dsem(f"bg{nbg // 8}")
                S.bg_sems.add(sem)
                nbg += 1
                S.add("pool", lambda e, scr=scr, cid=cid, src=src, ncol=ncol: e.dma_start(
                    out=scr[cid * 128:(cid + 1) * 128, 0:ncol], in_=src), reads=["c8"], writes=[("wsc", nm, cid)], dsem=sem)

    if debug:
        S.add("sp", lambda e: e.dma_start(out=dbg["c8"][:, :], in_=c8[:]), reads=["c8"], dsem=S.dsem("dbg"))
    S.barrier(bar_scr[0:1, 0:1])
    A.reset(mA2b)

    if stop != "A0":
        qT = A.alloc("qT", [128, L], BF16)
        kT = A.alloc("kT", [128, L], BF16)
        vTf = A.alloc("vTf", [128, L], F32)
        Vtm = A.alloc("Vtm", [128, 17, 128], BF16)
        cq = [A.alloc(f"cq{i}", [128, 512], F32) for i in range(2)]
        c8h = A.alloc("c8h", [8, 512], F32)
        PT = [A.alloc(f"PT_{i}", [128, 512], BF16) for i in range(5)]
        SBANK = (2, 3, 6, 5, 0)
        NSD = len(SBANK)
        rden = A.alloc("rden", [128, 512], F32)
        for h in range(8):
            for which, c0, dstname in (("q", 0, "qT"), ("k", 1024, "kT"), ("v", 2048, "vTf")):
                wb, wk, wo = stA.get(1 + h * 3 + (c0 // 1024))
                for ti, (t0, n) in enumerate(TILES):
                    bk = ti % 2
                    proj(wb, wk, wo, [16], [lambda k, t0=t0, n=n: hT[:, k, t0:t0 + n]], hT_keys(t0, n), 128, bk, n, None)
                    if which == "q":
                        S.add("act", lambda e, bk=bk, t0=t0, n=n, h=h: e.activation(
                            out=qT[:, t0:t0 + n], in_=bank(bk)[:, 0:n], func=AF.Identity, scale=QSCALE,
                            bias=bqs[:, h:h + 1]), reads=[PS(bk), "bqs"], writes=[("qT", ti)])
                    elif which == "k":
                        S.add("act", lambda e, bk=bk, t0=t0, n=n, h=h: e.activation(
                            out=kT[:, t0:t0 + n], in_=bank(bk)[:, 0:n], func=AF.Identity,
                            bias=cvec[:, C_BK + h:C_BK + h + 1]), reads=[PS(bk), "cvec"], writes=[("kT", ti)])
                    else:
                        S.add("act", lambda e, bk=bk, t0=t0, n=n, h=h: e.activation(
                            out=vTf[:, t0:t0 + n], in_=bank(bk)[:, 0:n], func=AF.Identity,
                            bias=cvec[:, C_BV + h:C_BV + h + 1]), reads=[PS(bk), "cvec"], writes=[("vTf", ti)])
            for b, (t0, n) in enumerate(BLK):
                bk = b % 2
                ti = 0 if b == 0 else 1 + (b - 1) // 4
                S.add("pe", lambda e, bk=bk, t0=t0, n=n: e.transpose(out=bank(bk)[0:n, 0:128], in_=vTf[:, t0:t0 + n],
                                                                     identity=identf),
                      reads=[("vTf", ti), "consts"], writes=[PS(bk)])
                S.add("act", lambda e, bk=bk, b=b, n=n: e.activation(out=Vtm[0:n, b, :], in_=bank(bk)[0:n, 0:128], func=AF.Copy),
                      reads=[PS(bk)], writes=[("Vtm", b)])
            blocks = []
            for ti, (t0, n) in enumerate(TILES):
                kbs = [(b, k0, kn) for b, (k0, kn) in enumerate(BLK) if k0 < t0 + n]
                for bi_, (b, k0, kn) in enumerate(kbs):
                    qlo = max(t0, k0)
                    blocks.append(dict(ti=ti, t0=t0, n=n, b=b, k0=k0, kn=kn, qlo=qlo, N=t0 + n - qlo, off=qlo - t0,
                                       diag=k0 >= t0, first=bi_ == 0, last=bi_ == len(kbs) - 1,
                                       kti=0 if b == 0 else 1 + (b - 1) // 4))

            def emit_cq(ti, h=h):
                t0, n = TILES[ti]
                cqt = cq[ti % 2]
                S.add("dve", lambda e, t0=t0, n=n, h=h: e.tensor_scalar(
                    out=c8h[:, 0:n], in0=c8[:, t0:t0 + n], scalar1=identf[0:8, h:h + 1], scalar2=None, op0=ALU.mult),
                    reads=["c8", "consts"], writes=["c8h"])
                S.add("pe", lambda e, n=n, ti=ti: e.matmul(bank(1)[:, 0:n], lhsT=onesf[0:8, :], rhs=c8h[:, 0:n],
                                                           start=True, stop=True),
                      reads=["c8h", "consts"], writes=[PS(1)])
                S.add("act", lambda e, n=n, ti=ti, cqt=cqt: e.activation(out=cqt[:, 0:n], in_=bank(1)[:, 0:n], func=AF.Copy),
                      reads=[PS(1)], writes=[("cq", ti % 2)])

            def emit_S(idx):
                B_ = blocks[idx]
                sb = SBANK[idx % NSD]
                S.add("pe", lambda e, sb=sb, k0=B_["k0"], kn=B_["kn"], qlo=B_["qlo"], N=B_["N"]: e.matmul(
                    bank(sb)[0:kn, 0:N], lhsT=kT[:, k0:k0 + kn], rhs=qT[:, qlo:qlo + N], start=True, stop=True),
                    reads=[("kT", B_["kti"]), ("qT", B_["ti"])], writes=[PS(sb)])

            emit_cq(0)
            for _i in range(min(NSD, len(blocks))):
                emit_S(_i)
            for idx, B_ in enumerate(blocks):
                ti, t0, n, b, kn, N, off = B_["ti"], B_["t0"], B_["n"], B_["b"], B_["kn"], B_["N"], B_["off"]
                ob, db = 4, 7
                sb = SBANK[idx % NSD]
                tb = idx % NSD
                cqt = cq[ti % 2]
                if B_["first"] and ti + 1 < len(TILES):
                    emit_cq(ti + 1)
                S.add("dve", lambda e, sb=sb, kn=kn, N=N, off=off, cqt=cqt: e.tensor_tensor(
                    out=bank(sb)[0:kn, 0:N], in0=bank(sb)[0:kn, 0:N], in1=cqt[0:kn, off:off + N], op=ALU.add),
                    reads=[PS(sb), ("cq", ti % 2)], writes=[PS(sb)])
                if B_["diag"]:
                    S.add("dve", lambda e, sb=sb, kn=kn: e.tensor_tensor(
                        out=bank(sb)[0:kn, 0:kn], in0=bank(sb)[0:kn, 0:kn], in1=maskf[0:kn, 0:kn], op=ALU.add),
                        reads=[PS(sb), "consts"], writes=[PS(sb)])
                S.add("act", lambda e, sb=sb, tb=tb, kn=kn, N=N, b=b, h=h: e.activation(
                    out=PT[tb][0:kn, 0:N], in_=bank(sb)[0:kn, 0:N], func=AF.Exp, bias=negc[0:kn, b, h:h + 1]),
                    reads=[PS(sb), "negc"], writes=[("PT", tb)])

                def pv(e, tb=tb, kn=kn, N=N, off=off, b=b, ob=ob, db=db, first=B_["first"], lastb=B_["last"]):
                    e.matmul(bank(ob)[:, off:off + N], lhsT=Vtm[0:kn, b, :], rhs=PT[tb][0:kn, 0:N],
                             start=first, stop=lastb)
                    return e.matmul(bank(db)[:, off:off + N], lhsT=ones_bf[0:kn, :], rhs=PT[tb][0:kn, 0:N],
                                    start=first, stop=lastb)
                S.add("pe", pv, reads=[("PT", tb), ("Vtm", b), "ones_bf"], writes=[PS(ob), PS(db)])
                if idx + NSD < len(blocks):
                    emit_S(idx + NSD)
                if B_["last"]:
                    S.add("dve", lambda e, db=db, n=n: e.reciprocal(out=rden[:, 0:n], in_=bank(db)[:, 0:n]),
                          reads=[PS(db)], writes=["rden"])
                    S.add("dve", lambda e, ob=ob, t0=t0, n=n, h=h: e.tensor_tensor(
                        out=attnT[:, h, t0:t0 + n], in0=bank(ob)[:, 0:n], in1=rden[:, 0:n], op=ALU.mult),
                        reads=[PS(ob), "rden"], writes=[("attnT", ti)])
        S.barrier(bar_scr[0:1, 0:1])
        A.reset(mA2)

        ub = [A.alloc(f"ub{i}", [128, 16 + L], F32) for i in range(2)]
        tA = A.alloc("tA", [128, 16 + L], F32)
        tB = A.alloc("tB", [128, 16 + L], F32)
        dT = A.alloc("dT", [128, 2, L], BF16)
        t16 = A.alloc("t16", [128, 16], F32)
        for i, tt in enumerate((ub[0], ub[1], tA, tB)):
            S.add("pool", lambda e, tt=tt: e.memset(tt[:, 0:16], 0.0), writes=[("pad", i)])
        pwb = A.alloc("pwb", [128, 2048], BF16)
        pwk, pwo = "pwb", [0]
        _fi = R.fi % len(R.f)
        R.fi += 1
        S.add("sp", lambda e: e.dma_start(out=R.f[_fi][:, :], in_=pool_w[:, :]),
              writes=[("wf", _fi)], dsem=S.dsem(f"wf{_fi}"))
        S.add("pool", lambda e: e.tensor_copy(out=pwb[:], in_=R.f[_fi][:]), reads=[("wf", _fi)], writes=["pwb"])
        for c in range(8):
            g = c // 2
            w = POOL_WINDOWS[g]
            u = ub[c % 2]
            uk = ("ub", c % 2)
            wb, wk, wo = stA.get(25 + c)
            for ti, (t0, n) in enumerate(TILES):
                bk = ti % 2
                proj(wb, wk, wo, [16], [lambda k, t0=t0, n=n: hT[:, k, t0:t0 + n]], hT_keys(t0, n), 128, bk, n, None)
                S.add("act", lambda e, bk=bk, t0=t0, n=n, c=c, u=u: e.activation(
                    out=u[:, 16 + t0:16 + t0 + n], in_=bank(bk)[:, 0:n], func=AF.Identity,
                    bias=cvec[:, C_BU + c:C_BU + c + 1]), reads=[PS(bk), "cvec", ("pad", c % 2)], writes=[uk])
            src, srck, srcpad = u, uk, ("pad", c % 2)
            sh = 1
            pp = [(tA, "tA", ("pad", 2)), (tB, "tB", ("pad", 3))]
            pi = 0
            while sh < w:
                dst, dk, dpad = pp[pi % 2]
                S.add("dve", lambda e, dst=dst, src=src, sh=sh: e.tensor_tensor(
                    out=dst[:, 16:16 + L], in0=src[:, 16:16 + L], in1=src[:, 16 - sh:16 - sh + L], op=ALU.add),
                    reads=[srck, srcpad], writes=[dk])
                src, srck, srcpad = dst, dk, dpad
                sh *= 2
                pi += 1
            wi = POOL_WINDOWS.index(w)
            S.add("dve", lambda e, src=src, u=u, w=w, c=c: e.scalar_tensor_tensor(
                out=dT[:, c % 2, 16:L], in0=src[:, 32:16 + L], scalar=1.0 / w, in1=u[:, 32:16 + L],
                op0=ALU.mult, op1=ALU.subtract), reads=[srck, uk], writes=[("dT", c % 2)])
            S.add("dve", lambda e, src=src, wi=wi: e.tensor_tensor(
                out=t16[:], in0=src[:, 16:32], in1=consts[:, K_RCNT + wi * 16:K_RCNT + wi * 16 + 16], op=ALU.mult),
                reads=[srck, "consts"], writes=["t16"])
            S.add("dve", lambda e, u=u, c=c: e.tensor_tensor(
                out=dT[:, c % 2, 0:16], in0=t16[:], in1=u[:, 16:32], op=ALU.subtract),
                reads=["t16", uk], writes=[("dT", c % 2)])
            if c % 2 == 1:
                for ocl in range(2):
                    oc = 2 * g + ocl
                    for ti, (t0, n) in enumerate(TILES):
                        bk = ti % 2

                        def fn(e, g=g, ocl=ocl, bk=bk, t0=t0, n=n):
                            ins = None
                            for kl in range(2):
                                o = pwo[0] + (g * 2 + kl) * 256 + ocl * 128
                                ins = e.matmul(bank(bk)[:, 0:n], lhsT=pwb[:, o:o + 128], rhs=dT[:, kl, t0:t0 + n],
                                               start=(kl == 0), stop=(kl == 1))
                            return ins
                        S.add("pe", fn, reads=[pwk, ("dT", 0), ("dT", 1)], writes=[PS(bk)])
                        S.add("act", lambda e, bk=bk, oc=oc, t0=t0, n=n: e.activation(
                            out=poolT[:, oc, t0:t0 + n], in_=bank(bk)[:, 0:n], func=AF.Copy,
                            scale=cvec[:, C_PSC + oc:C_PSC + oc + 1]), reads=[PS(bk), "cvec"], writes=[("poolT", ti)])
        if debug:
            S.add("sp", lambda e: e.dma_start(out=dbg["attnT"][:, :, :], in_=attnT[:]),
                  reads=[("attnT", i) for i in range(5)], dsem=S.dsem("dbg"))
            S.add("sp", lambda e: e.dma_start(out=dbg["poolT"][:, :, :], in_=poolT[:]),
                  reads=[("poolT", i) for i in range(5)], dsem=S.dsem("dbg"))
        S.barrier(bar_scr[0:1, 0:1])
    A.reset(mA1)

    if stop not in ("A0", "A"):
        gpost = A.alloc("gpost", [128, D], F32)
        S.add("sp", lambda e: e.dma_start(out=gpost[:], in_=rowv_d[0:1, :].partition_broadcast(128)),
              writes=["gpost"], dsem=S.dsem("gpost"))
        xtB = A.alloc("xtB", [128, D], F32)
        junk = A.alloc("junk", [128, D], BF16)
        r1t = [A.alloc(f"r1t{i}", [128, D], F32) for i in range(2)]
        mTg = A.alloc("mTg", [128, 16, 528], BF16)
        ring[0] = Ring(0, 5)
        reqB = []
        for _gi in range(4):
            for c in range(16):
                reqB.append(([slabv(w_in, 32 + c, 16, 128)], (wscB, c * 3, "load", "B")))
                reqB.append(([slabv(w_in, 48 + c, 16, 128)], (wscB, c * 3 + 1, "load", "B")))
                reqB.append(([slabv(w_ap, c, 16, 128)], (wscB, c * 3 + 2, "load", "B")))
            for oc in range(16):
                reqB.append(([slabv(w_out, oc, 16, 128)], (wscB, 48 + oc, "load", "B")))
        stB = Stream(reqB, 4)
        mixT = A.alloc("mixT", [128, 16, 528], F32)
        hTg = A.alloc("hTg", [128, 16, 528], BF16)
        gt = [[A.alloc(f"gt{i}_{j}", [128, 512], F32) for j in range(4)] for i in range(2)]
        rot = [(xtB, "xtB", "xtB"), (r1t[0], ("r1t", 0), "r1st0"), (r1t[1], ("r1t", 1), "r1st1")]

        def B0_block(gi, bi0, bankbase):
            G = GROUPS[gi]
            b = G["blocks"][bi0]
            t0, n = BLK[b]
            xt_, xk_, xs_ = rot[bi0 % 3]
            src = meta[0:16, :] if b == 0 else x[t0 - 16:t0 - 16 + n, :]
            S.add("sp", lambda e, src=src, n=n, xt_=xt_: e.dma_start(out=xt_[0:n, :], in_=src), writes=[xk_],
                  dsem=S.dsem(xs_))
            rs, rsk = rms_rows(xt_[0:n, :], n, b, xk_, junk, "junk")
            S.add("act", lambda e, n=n, rs=rs, xt_=xt_: e.activation(out=xt_[0:n, :], in_=xt_[0:n, :], func=AF.Copy, scale=rs),
                  reads=[xk_, rsk], writes=[xk_])
            to_feature_major(xt_, n, xk_, hTg, ("hTg", bi0), t0 - G["start"], C_G1, bankbase)

        def B1(gi, b3_prev=None):
            G = GROUPS[gi]
            gs = G["start"]
            hk = [("hTg", i) for i in range(len(G["blocks"]))]
            it = 0
            nb3 = len(GROUPS[b3_prev]["blocks"]) if b3_prev is not None else 0
            b3_at = {1 + 3 * k: k for k in range(nb3)}
            for c in range(16):
                if c - 1 in b3_at and len(G["tiles"]) == 1:
                    B3_block(b3_prev, b3_at[c - 1], 4 * (it % 2))
                    it += 1
                tl = []
                for (t0, n) in G["tiles"]:
                    tl.append((t0, n, t0 - gs, 4 * (it % 2), it % 2, TILES.index((t0, n))))
                    it += 1
                wga = stB.get(gi * 64 + c * 3)
                for (t0, n, lo, pb, par, ti) in tl:
                    proj(wga[0], wga[1], wga[2], [16], [lambda k, lo=lo, n=n: hTg[:, k, lo:lo + n]], hk, 128, pb + 0, n, None)
                wgp = stB.get(gi * 64 + c * 3 + 1)
                for (t0, n, lo, pb, par, ti) in tl:
                    proj(wgp[0], wgp[1], wgp[2], [16], [lambda k, lo=lo, n=n: hTg[:, k, lo:lo + n]], hk, 128, pb + 1, n, None)
                wap = stB.get(gi * 64 + c * 3 + 2)
                for (t0, n, lo, pb, par, ti) in tl:
                    proj(wap[0], wap[1], [wap[2][0]], [8], [lambda k, t0=t0, n=n: attnT[:, k, t0:t0 + n]],
                         [("attnT", ti)], 128, pb + 2, n, None)
                    proj(wap[0], wap[1], [wap[2][0] + 1024], [8], [lambda k, t0=t0, n=n: poolT[:, k, t0:t0 + n]],
                         [("poolT", ti)], 128, pb + 3, n, None)
                for (t0, n, lo, pb, par, ti) in tl:
                    g4 = gt[par]
                    S.add("act", lambda e, pb=pb, g4=g4, n=n, c=c: e.activation(
                        out=g4[0][:, 0:n], in_=bank(pb)[:, 0:n], func=AF.Sigmoid, bias=cvec[:, C_BGA + c:C_BGA + c + 1]),
                        reads=[PS(pb), "cvec"], writes=[("gt", par, 0)])
                    S.add("act", lambda e, pb=pb, g4=g4, n=n, c=c: e.activation(
                        out=g4[1][:, 0:n], in_=bank(pb + 1)[:, 0:n], func=AF.Sigmoid, bias=cvec[:, C_BGP + c:C_BGP + c + 1]),
                        reads=[PS(pb + 1), "cvec"], writes=[("gt", par, 1)])
                    S.add("dve", lambda e, pb=pb, g4=g4, n=n: e.tensor_tensor(
                        out=g4[2][:, 0:n], in0=g4[0][:, 0:n], in1=bank(pb + 2)[:, 0:n], op=ALU.mult),
                        reads=[PS(pb + 2), ("gt", par, 0)], writes=[("gt", par, 2)])
                    S.add("dve", lambda e, pb=pb, g4=g4, n=n: e.tensor_tensor(
                        out=g4[3][:, 0:n], in0=g4[1][:, 0:n], in1=bank(pb + 3)[:, 0:n], op=ALU.mult),
                        reads=[PS(pb + 3), ("gt", par, 1)], writes=[("gt", par, 3)])
                    S.add("pool", lambda e, g4=g4, n=n, c=c, lo=lo: e.tensor_tensor(
                        out=mTg[:, c, lo:lo + n], in0=g4[2][:, 0:n], in1=g4[3][:, 0:n], op=ALU.add),
                        reads=[("gt", par, 2), ("gt", par, 3)], writes=[("mTg", c)])

        def B2_iter(gi, oc, itc):
            G = GROUPS[gi]
            gs = G["start"]
            mx = mixT
            mk = [("mTg", c) for c in range(16)]
            wo_ = stB.get(gi * 64 + 48 + oc)
            for (t0, n) in G["tiles"]:
                lo = t0 - gs
                bk = itc[0] % 2
                itc[0] += 1
                proj(wo_[0], wo_[1], wo_[2], [16], [lambda k, lo=lo, n=n: mTg[:, k, lo:lo + n]], mk, 128, bk, n, None)
                S.add("act", lambda e, bk=bk, oc=oc, lo=lo, n=n, mx=mx: e.activation(
                    out=mx[:, oc, lo:lo + n], in_=bank(bk)[:, 0:n], func=AF.Copy),
                    reads=[PS(bk)], writes=[("mixT", oc)])

        def B3(gi):
            for bi_ in range(len(GROUPS[gi]["blocks"])):
                B3_block(gi, bi_, 4 * (bi_ % 2))

        def B3_block(gi, bi_, half):
            G = GROUPS[gi]
            gs = G["start"]
            mx = mixT
            xk_ = [("mixT", oc) for oc in range(16)]
            if True:
                b = G["blocks"][bi_]
                t0, n = BLK[b]
                lo = t0 - gs
                pst = psA if half == 0 else psB
                for cg in range(4):
                    def tr(e, cg=cg, lo=lo, n=n, half=half, mx=mx):
                        ins = None
                        for i in range(4):
                            oc = cg * 4 + i
                            ins = e.transpose(out=bank(half + cg)[0:n, i * 128:(i + 1) * 128], in_=mx[:, oc, lo:lo + n],
                                              identity=identf)
                        return ins
                    S.add("pe", tr, reads=xk_ + ["consts"], writes=[PS(half + cg)])
                pkeys = [PS(half + i) for i in range(4)]
                rt_ = r1t[bi_ % 2]
                rk = ("r1t", bi_ % 2)
                src = meta[0:16, :] if b == 0 else x[t0 - 16:t0 - 16 + n, :]
                S.add("sp", lambda e, src=src, n=n: e.dma_start(out=xtB[0:n, :], in_=src), writes=["xtB"], dsem=S.dsem("xtB"))
                ss = stat[0:n, b:b + 1]
                S.add("act", lambda e, pst=pst, n=n, ss=ss: e.activation(out=junk[0:n, :], in_=pst[0:n, :], func=AF.Square,
                                                                         accum_out=ss),
                      reads=pkeys, writes=["junk", ("ss", b)])
                rtt = stat[0:n, 20 + b:21 + b]
                rs = stat[0:n, 40 + b:41 + b]
                S.add("act", lambda e, rtt=rtt, ss=ss, n=n: e.activation(out=rtt, in_=ss, func=AF.Sqrt, scale=1.0 / D,
                                                                         bias=eps_t[0:n, :]),
                      reads=[("ss", b), "consts"], writes=[("rt", b)])
                S.add("dve", lambda e, rs=rs, rtt=rtt: e.reciprocal(out=rs, in_=rtt), reads=[("rt", b)], writes=[("rs", b)])
                S.add("dve", lambda e, pst=pst, n=n, rs=rs, rt_=rt_: e.scalar_tensor_tensor(
                    out=rt_[0:n, :], in0=pst[0:n, :], scalar=rs, in1=gpost[0:n, :], op0=ALU.mult, op1=ALU.mult),
                    reads=pkeys + [("rs", b), "gpost"], writes=[rk])
                S.add("pool", lambda e, n=n, rt_=rt_: e.tensor_tensor(out=rt_[0:n, :], in0=rt_[0:n, :], in1=xtB[0:n, :],
                                                                       op=ALU.add),
                      reads=[rk, "xtB"], writes=[rk])
                S.add("sp", lambda e, n=n, rt_=rt_, t0=t0: e.dma_start(out=r1s[t0:t0 + n, :], in_=rt_[0:n, :]),
                      reads=[rk], writes=[("r1s", b)], dsem=S.dsem(f"r1st{bi_ % 2}"))

        itc = [0]
        for bi0 in range(len(GROUPS[0]["blocks"])):
            B0_block(0, bi0, 0)
        for gi in range(0, 4):
            B1(gi, b3_prev=(gi - 1 if gi > 0 else None))
            b0_at = {2: 0, 5: 1, 8: 2, 11: 3} if gi < 3 else {}
            for oc in range(16):
                B2_iter(gi, oc, itc)
                if oc in b0_at:
                    B0_block(gi + 1, b0_at[oc], 2)
        B3(3)
        S.barrier(bar_scr[0:1, 0:1])
    A.reset(mA)

    if stop not in ("A0", "A", "B"):
        gpost2 = A.alloc("gpost2", [128, D], F32)
        S.add("sp", lambda e: e.dma_start(out=gpost2[:], in_=rowv_d[1:2, :].partition_broadcast(128)),
              writes=["gpost2"], dsem=S.dsem("gpost2"))
        junkC = A.alloc("junkC", [128, D], BF16)
        r1c = [A.alloc(f"r1c{i}", [128, D], F32) for i in range(2)]
        ot = [A.alloc(f"ot{i}", [128, D], F32) for i in range(2)]
        carry = A.alloc("carry", [128, 88, 2], F32)
        S.add("pool", lambda e: e.memset(carry[:], 0.0), writes=["carry"])
        h2T = A.alloc("h2T", [128, 16, 528], BF16)
        actT = A.alloc("actT", [128, NJ, 528], BF16)
        ffT = A.alloc("ffT", [128, 16, 512], F32)
        upb = [[A.alloc(f"up{i}_{j}", [128, 530], F32) for j in range(2)] for i in range(2)]
        cv = [[A.alloc(f"cv{i}_{j}", [128, 528], F32) for j in range(3)] for i in range(2)]
        ring[0] = Ring(0, 6)
        reqC = []
        for _gi in range(4):
            for j in range(NJ):
                reqC.append(([slabv(w_up, j, 16, 128)], (wscC, j * 2, "load", "C")))
                reqC.append(([slabv(w_up, NJ + j, 16, 128)], (wscC, j * 2 + 1, "load", "C")))
            for oc in range(16):
                for pi_, (k0, nk) in enumerate(((0, 16), (16, 16), (32, 12))):
                    reqC.append(([slabv(w_down, oc, nk, 128, k0)], (wscC, 88 + oc * 3 + pi_, "load", "C")))
        stC = Stream(reqC, 5)

        def C0_front(gi, bi_):
            G = GROUPS[gi]
            b = G["blocks"][bi_]
            t0, n = BLK[b]
            rc = r1c[bi_ % 2]
            rck = ("r1c", bi_ % 2)
            S.add("sp", lambda e, rc=rc, t0=t0, n=n: e.dma_start(out=rc[0:n, :], in_=r1s[t0:t0 + n, :]),
                  reads=[("r1s", b)], writes=[rck], dsem=S.dsem(f"r1c{bi_ % 2}"))
            rs, rsk = rms_rows(rc[0:n, :], n, b, rck, junkC, "junkC")
            S.add("act", lambda e, rc=rc, n=n, rs=rs: e.activation(out=rc[0:n, :], in_=rc[0:n, :], func=AF.Copy, scale=rs),
                  reads=[rck, rsk], writes=[rck])

        def C0_back(gi, bi_, bankbase):
            G = GROUPS[gi]
            b = G["blocks"][bi_]
            t0, n = BLK[b]
            to_feature_major(r1c[bi_ % 2], n, ("r1c", bi_ % 2), h2T, ("h2T", bi_), t0 - G["start"], C_G2, bankbase)

        def C0_block(gi, bi_, bankbase):
            C0_front(gi, bi_)
            C0_back(gi, bi_, bankbase)

        def C1_iter(gi, j):
            G = GROUPS[gi]
            gs, gn = G["start"], G["n"]
            hk = [("h2T", i) for i in range(len(G["blocks"]))]
            s2 = j % 2
            for half_, jj in ((0, j), (1, NJ + j)):
                wu = stC.get(gi * 136 + j * 2 + half_)
                ub_ = upb[s2][half_]
                ubk = ("upb", s2, half_)
                S.add("pool", lambda e, ub_=ub_, jj=jj: e.tensor_copy(out=ub_[:, 0:2], in_=carry[:, jj, :]),
                      reads=["carry"], writes=[ubk])
                for tix, (t0, n) in enumerate(G["tiles"]):
                    lo = t0 - gs
                    bk = 2 * half_ + (j + tix) % 2
                    proj(wu[0], wu[1], wu[2], [16], [lambda k, lo=lo, n=n: h2T[:, k, lo:lo + n]], hk, 128, bk, n, None)
                    S.add("act", lambda e, bk=bk, ub_=ub_, lo=lo, n=n: e.activation(
                        out=ub_[:, 2 + lo:2 + lo + n], in_=bank(bk)[:, 0:n], func=AF.Copy),
                        reads=[PS(bk)], writes=[ubk])
                S.add("pool", lambda e, ub_=ub_, jj=jj, gn=gn: e.tensor_copy(out=carry[:, jj, :], in_=ub_[:, gn:gn + 2]),
                      reads=[ubk], writes=["carry"])
                tg = cv[s2][half_]
                tgk = ("cv", s2, half_)
                S.add("dve", lambda e, tg=tg, ub_=ub_, jj=jj, gn=gn: e.tensor_scalar(
                    out=tg[:, 0:gn], in0=ub_[:, 0:gn], scalar1=cvec[:, C_CW0 + jj:C_CW0 + jj + 1],
                    scalar2=cvec[:, C_CB + jj:C_CB + jj + 1], op0=ALU.mult, op1=ALU.add),
                    reads=[ubk, "cvec"], writes=[tgk])
                S.add("dve", lambda e, tg=tg, ub_=ub_, jj=jj, gn=gn: e.scalar_tensor_tensor(
                    out=tg[:, 0:gn], in0=ub_[:, 1:gn + 1], scalar=cvec[:, C_CW1 + jj:C_CW1 + jj + 1], in1=tg[:, 0:gn],
                    op0=ALU.mult, op1=ALU.add), reads=[ubk, "cvec", tgk], writes=[tgk])
                S.add("dve", lambda e, tg=tg, ub_=ub_, jj=jj, gn=gn: e.scalar_tensor_tensor(
                    out=tg[:, 0:gn], in0=ub_[:, 2:gn + 2], scalar=cvec[:, C_CW2 + jj:C_CW2 + jj + 1], in1=tg[:, 0:gn],
                    op0=ALU.mult, op1=ALU.add), reads=[ubk, "cvec", tgk], writes=[tgk])
            gl = cv[s2][2]
            S.add("act", lambda e, gl=gl, s2=s2, gn=gn: e.activation(out=gl[:, 0:gn], in_=cv[s2][0][:, 0:gn],
                                                                      func=AF.Gelu_apprx_tanh),
                  reads=[("cv", s2, 0)], writes=[("cv", s2, 2)])
            S.add("pool", lambda e, gl=gl, s2=s2, gn=gn, j=j: e.tensor_tensor(
                out=actT[:, j, 0:gn], in0=gl[:, 0:gn], in1=cv[s2][1][:, 0:gn], op=ALU.mult),
                reads=[("cv", s2, 2), ("cv", s2, 1)], writes=[("actT", j)])

        def C2_iter(gi, oc):
            G = GROUPS[gi]
            t0r, nr = G["tiles"][-1]
            lo = t0r - G["start"]
            ak = [("actT", j) for j in range(NJ)]
            bk = oc % 2
            for pi, (k0, nk) in enumerate(((0, 16), (16, 16), (32, 12))):
                wd = stC.get(gi * 136 + 88 + oc * 3 + pi)
                proj(wd[0], wd[1], wd[2], [nk], [lambda k, k0=k0, lo=lo, nr=nr: actT[:, k0 + k, lo:lo + nr]],
                     ak, 128, bk, nr, None, first=(pi == 0), last=(pi == 2))
            S.add("act", lambda e, bk=bk, oc=oc, nr=nr: e.activation(out=ffT[:, oc, 0:nr], in_=bank(bk)[:, 0:nr], func=AF.Copy),
                  reads=[PS(bk)], writes=[("ffT", oc)])

        def C3_block(gi, bi_):
            G = GROUPS[gi]
            t0r, nr = G["tiles"][-1]
            rblocks = [b for b in G["blocks"] if b > 0]
            b = rblocks[bi_]
            t0, n = BLK[b]
            lo2 = t0 - t0r
            fk = [("ffT", oc) for oc in range(16)]
            half = 4
            pst = psB
            for cg in range(4):
                def tr(e, cg=cg, lo2=lo2, n=n, half=half):
                    ins = None
                    for i in range(4):
                        oc = cg * 4 + i
                        ins = e.transpose(out=bank(half + cg)[0:n, i * 128:(i + 1) * 128], in_=ffT[:, oc, lo2:lo2 + n],
                                          identity=identf)
                    return ins
                S.add("pe", tr, reads=fk + ["consts"], writes=[PS(half + cg)])
            pkeys = [PS(half + i) for i in range(4)]
            rc = r1c[bi_ % 2]
            rck = ("r1c", bi_ % 2)
            S.add("sp", lambda e, rc=rc, t0=t0, n=n: e.dma_start(out=rc[0:n, :], in_=r1s[t0:t0 + n, :]),
                  reads=[("r1s", b)], writes=[rck], dsem=S.dsem(f"r1c{bi_ % 2}"))
            ss = stat[0:n, b:b + 1]
            S.add("act", lambda e, pst=pst, n=n, ss=ss: e.activation(out=junkC[0:n, :], in_=pst[0:n, :], func=AF.Square,
                                                                     accum_out=ss),
                  reads=pkeys, writes=["junkC", ("ss", b)])
            rtt = stat[0:n, 20 + b:21 + b]
            rs = stat[0:n, 40 + b:41 + b]
            S.add("act", lambda e, rtt=rtt, ss=ss, n=n: e.activation(out=rtt, in_=ss, func=AF.Sqrt, scale=1.0 / D,
                                                                     bias=eps_t[0:n, :]),
                  reads=[("ss", b), "consts"], writes=[("rt", b)])
            S.add("dve", lambda e, rs=rs, rtt=rtt: e.reciprocal(out=rs, in_=rtt), reads=[("rt", b)], writes=[("rs", b)])
            o_ = ot[bi_ % 2]
            ok_ = ("ot", bi_ % 2)
            S.add("dve", lambda e, pst=pst, n=n, rs=rs, o_=o_: e.scalar_tensor_tensor(
                out=o_[0:n, :], in0=pst[0:n, :], scalar=rs, in1=gpost2[0:n, :], op0=ALU.mult, op1=ALU.mult),
                reads=pkeys + [("rs", b), "gpost2"], writes=[ok_])
            S.add("pool", lambda e, n=n, o_=o_, rc=rc: e.tensor_tensor(out=o_[0:n, :], in0=o_[0:n, :], in1=rc[0:n, :],
                                                                        op=ALU.add),
                  reads=[ok_, rck], writes=[ok_])
            S.add("sp", lambda e, n=n, o_=o_, t0=t0: e.dma_start(out=out[t0 - 16:t0 - 16 + n, :], in_=o_[0:n, :]),
                  reads=[ok_], dsem=S.dsem(f"ost{bi_ % 2}"))

        for bi_ in range(len(GROUPS[0]["blocks"])):
            C0_block(0, bi_, 0)
        for gi in range(4):
            c3_at = {4: 0, 12: 1, 20: 2, 28: 3} if gi > 0 else {}
            for j in range(NJ):
                C1_iter(gi, j)
                if j in c3_at:
                    C3_block(gi - 1, c3_at[j])
            c0f_at = {0: 0, 3: 1, 6: 2, 9: 3} if gi < 3 else {}
            c0b_at = {2: 0, 5: 1, 8: 2, 11: 3} if gi < 3 else {}
            for oc in range(16):
                C2_iter(gi, oc)
                if oc in c0f_at:
                    C0_front(gi + 1, c0f_at[oc])
                if oc in c0b_at:
                    C0_back(gi + 1, c0b_at[oc], 2)
        for bi_ in range(4):
            C3_block(3, bi_)
    S.emit()
    return nc, S, A


def host_layout(inputs):
    f32 = np.float32
    fm = lambda v: np.ascontiguousarray(np.asarray(v, f32).reshape(-1, 128).T)
    b_in = np.asarray(inputs["b_in"], f32)[0]
    cvec = np.zeros((128, NCV), f32)
    cvec[:, C_G1:C_G1 + 16] = fm(inputs["mix_pre_g"][0])
    cvec[:, C_BQ:C_BQ + 8] = fm(b_in[0:1024])
    cvec[:, C_BK:C_BK + 8] = fm(b_in[1024:2048])
    cvec[:, C_BV:C_BV + 8] = fm(b_in[2048:3072])
    cvec[:, C_BU:C_BU + 8] = fm(b_in[3080:4104])
    cvec[:, C_BGA:C_BGA + 16] = fm(b_in[4104:6152])
    cvec[:, C_BGP:C_BGP + 16] = fm(b_in[6152:8200])
    cvec[:, C_PSC:C_PSC + 8] = fm(inputs["pool_scale"][0])
    cvec[:, C_G2:C_G2 + 16] = fm(inputs["ffn_pre_g"][0])
    cvec[:, C_CB:C_CB + 88] = fm(inputs["ffn_conv_b"][0])
    cw = np.asarray(inputs["ffn_conv_w"], f32)[0]
    cvec[:, C_CW0:C_CW0 + 88] = fm(cw[0])
    cvec[:, C_CW1:C_CW1 + 88] = fm(cw[1])
    cvec[:, C_CW2:C_CW2 + 88] = fm(cw[2])
    cvec[0:8, C_BF] = b_in[3072:3080]
    rowv = np.stack([np.asarray(inputs["mix_post_g"], f32)[0], np.asarray(inputs["ffn_post_g"], f32)[0]])
    consts = np.zeros((128, NKC), f32)
    consts[:, K_ID:K_ID + 128] = np.eye(128, dtype=f32)
    p = np.arange(128)[:, None]
    j = np.arange(128)[None, :]
    consts[:, K_MASK:K_MASK + 128] = np.where(j >= p, 0.0, -30000.0)
    for wi, w in enumerate(POOL_WINDOWS):
        for t in range(16):
            consts[:, K_RCNT + wi * 16 + t] = 1.0 / min(t + 1, w)
    consts[:, K_ONES:K_ONES + 128] = 1.0
    consts[:, K_EPS] = EPS
    return cvec, np.ascontiguousarray(rowv), consts


_CACHE = {}


def kernel(**inputs):
    f32 = np.float32
    x = np.asarray(inputs["x"], f32)
    B = x.shape[0]
    cvec, rowv, consts = host_layout(inputs)
    if "nc" not in _CACHE:
        _CACHE["nc"] = build_program()[0]
    nc = _CACHE["nc"]
    def slabs(w):
        K_, N_ = w.shape
        t = w.reshape(K_ // 128, 128, N_ // 128, 128).transpose(2, 1, 0, 3)
        return np.ascontiguousarray(t).reshape(N_ // 128 * 128, K_)

    win = np.asarray(inputs["w_in"], f32)[0]
    win_main = np.concatenate([win[:, 0:3072], win[:, 3080:8200]], axis=1)
    wf = np.ascontiguousarray(win[:, 3072:3080].reshape(16, 128, 8).transpose(1, 0, 2)).reshape(128, 128)
    wao = slabs(np.asarray(inputs["w_attn_o"], f32)[0])
    wpo = slabs(np.asarray(inputs["w_pool_o"], f32)[0])
    pw = np.asarray(inputs["pool_w"], f32)[0].reshape(4, 2, 128, 256).transpose(2, 0, 1, 3)
    shared = {
        "meta": np.ascontiguousarray(np.asarray(inputs["meta_tokens"], f32)),
        "w_in": slabs(win_main),
        "w_f": wf,
        "w_ap": np.ascontiguousarray(np.concatenate([wao, wpo], axis=1)),
        "pool_w": np.ascontiguousarray(pw).reshape(128, 2048),
        "w_out": slabs(np.asarray(inputs["w_out"], f32)[0]),
        "w_up": slabs(np.asarray(inputs["w_ffn_up"], f32)[0]),
        "w_down": slabs(np.asarray(inputs["w_ffn_down"], f32)[0]),
        "cvec": cvec, "rowv": rowv, "consts": consts,
    }
    in_maps = []
    for b in range(B):
        m = dict(shared)
        m["x"] = np.ascontiguousarray(x[b])
        in_maps.append(m)
    res = run_bass_kernel_spmd(nc, in_maps, core_ids=list(range(B)))
    return np.stack([np.asarray(r["out"], f32) for r in res.results], axis=0)
```
